# Optimizing a Trainium2 kernel written in Bass

```python
import math
import jax
import jax.numpy as jnp
from jax import lax
import numpy as np

D_MODEL = 2048
BATCH = 8
SEQ = 2048
DEPTH = 2

PLE_DIM = 256
NORM_EPS = 1e-6
ROPE_THETA = 10000.0
NEG_INF = -1e30
FORCE_SCORE = 1e4
Q_BLOCK = 128

N_BRANCH = 4
MIX_WIDTH = D_MODEL // 2

SSD_D_INNER = MIX_WIDTH
SSD_HEAD_DIM = 64
SSD_N_HEADS = SSD_D_INNER // SSD_HEAD_DIM
SSD_N_GROUPS = 2
SSD_HPG = SSD_N_HEADS // SSD_N_GROUPS
SSD_D_STATE = 128
SSD_CONV = 4
SSD_CHUNK = 128
SSD_CONV_DIM = SSD_D_INNER + 2 * SSD_N_GROUPS * SSD_D_STATE

DIFF_N_HEADS = 8
DIFF_HEAD_DIM = 64
DIFF_V_DIM = 2 * DIFF_HEAD_DIM
DIFF_WIDTH = DIFF_N_HEADS * DIFF_V_DIM

NSA_N_HEADS = 16
NSA_N_KV = 4
NSA_HPG = NSA_N_HEADS // NSA_N_KV
NSA_HEAD_DIM = 64
NSA_WIDTH = NSA_N_HEADS * NSA_HEAD_DIM
NSA_KV_WIDTH = NSA_N_KV * NSA_HEAD_DIM
CMP_BLOCK = 32
CMP_STRIDE = 16
CMP_HIDDEN = 256
SEL_BLOCK = 64
SEL_TOPK = 8
WINDOW = 512

RNN_WIDTH = MIX_WIDTH
RNN_BLOCKS = 16
RNN_BLOCK_DIM = RNN_WIDTH // RNN_BLOCKS
RNN_CONV = 4
RG_C = 8.0

D_FF = 3 * D_MODEL
FFN_CONV = 3

IN_WIDTHS = (
    SSD_D_INNER, SSD_CONV_DIM, SSD_N_HEADS,
    DIFF_WIDTH, DIFF_WIDTH, DIFF_WIDTH,
    NSA_WIDTH, NSA_KV_WIDTH, NSA_KV_WIDTH, NSA_KV_WIDTH,
    NSA_KV_WIDTH, NSA_KV_WIDTH, NSA_KV_WIDTH, NSA_N_HEADS * 3,
    RNN_WIDTH, RNN_WIDTH,
)
D_IN = sum(IN_WIDTHS)
IN_OFFSETS = tuple(sum(IN_WIDTHS[:i + 1]) for i in range(len(IN_WIDTHS) - 1))

kernel_name = 'hybrid_ssd_diffattn_nsa_rglru_block'


def rmsnorm(x, g):
    xf = x.astype(jnp.float32)
    y = xf * lax.rsqrt(jnp.mean(xf * xf, axis=-1, keepdims=True) + NORM_EPS)
    return (y * g.astype(jnp.float32)).astype(x.dtype)


def causal_dwconv(x, w, b):
    k_width = w.shape[0]
    s = x.shape[1]
    xp = jnp.pad(x, ((0, 0), (k_width - 1, 0), (0, 0)))
    y = b
    for k in range(k_width):
        y = y + xp[:, k:k + s] * w[k]
    return y


def rope(x, pos):
    half = x.shape[-1] // 2
    inv_freq = ROPE_THETA ** (-jnp.arange(half, dtype=jnp.float32) / half)
    ang = pos.astype(jnp.float32)[:, None] * inv_freq[None, :]
    shape = (1, x.shape[1]) + (1,) * (x.ndim - 3) + (half,)
    cos = jnp.cos(ang).reshape(shape)
    sin = jnp.sin(ang).reshape(shape)
    xf = x.astype(jnp.float32)
    x1, x2 = xf[..., :half], xf[..., half:]
    return jnp.concatenate([x1 * cos - x2 * sin, x2 * cos + x1 * sin], axis=-1).astype(x.dtype)


def masked_softmax(s, mask):
    s = jnp.where(mask, s.astype(jnp.float32), NEG_INF)
    m = jnp.max(s, axis=-1, keepdims=True)
    e = jnp.where(mask, jnp.exp(s - m), 0.0)
    return e / jnp.maximum(jnp.sum(e, axis=-1, keepdims=True), 1e-30)


def segsum(a):
    t = a.shape[-1]
    ar = jnp.broadcast_to(a[..., :, None], a.shape + (t,))
    ar = jnp.where(jnp.tril(jnp.ones((t, t), dtype=bool), -1), ar, 0.0)
    cs = jnp.cumsum(ar, axis=-2)
    return jnp.where(jnp.tril(jnp.ones((t, t), dtype=bool)), cs, -jnp.inf)


def ssd_mixer(z, xbc, dt, conv_w, conv_b, dt_bias, a_log, d_skip, g_norm):
    bsz, s, _ = z.shape
    f32 = jnp.float32
    nc, cl = s // SSD_CHUNK, SSD_CHUNK
    g, j, hp, n = SSD_N_GROUPS, SSD_HPG, SSD_HEAD_DIM, SSD_D_STATE
    xbc = jax.nn.silu(causal_dwconv(xbc, conv_w, conv_b))
    xs, bm, cm = jnp.split(xbc, [SSD_D_INNER, SSD_D_INNER + g * n], axis=-1)
    xs = xs.astype(f32).reshape(bsz, nc, cl, g, j, hp)
    bm = bm.astype(f32).reshape(bsz, nc, cl, g, n)
    cm = cm.astype(f32).reshape(bsz, nc, cl, g, n)
    dt = jax.nn.softplus(dt.astype(f32) + dt_bias.astype(f32))
    a = -jnp.exp(a_log.astype(f32))
    dt_c = dt.reshape(bsz, nc, cl, g, j)
    xdt = xs * dt_c[..., None]
    a_dt = (dt_c * a.reshape(g, j)).transpose(0, 3, 4, 1, 2)
    acs = jnp.cumsum(a_dt, axis=-1)
    decay_in = jnp.exp(segsum(a_dt))
    cb = jnp.einsum('bclgn,bcsgn->bgcls', cm, bm)
    y_diag = jnp.einsum('bgjcls,bcsgjp->bclgjp', cb[:, :, None] * decay_in, xdt)

    def chunk_step(h, inp):
        c_c, b_c, xdt_c, acs_c = inp
        y_off = jnp.einsum('blgn,bgjpn,bgjl->blgjp', c_c, h, jnp.exp(acs_c))
        decay_st = jnp.exp(acs_c[..., -1:] - acs_c)
        s_c = jnp.einsum('blgn,bgjl,blgjp->bgjpn', b_c, decay_st, xdt_c)
        h = h * jnp.exp(acs_c[..., -1])[..., None, None] + s_c
        return h, y_off

    h0 = jnp.zeros((bsz, g, j, hp, n), f32)
    _, y_off = lax.scan(chunk_step, h0, (cm.transpose(1, 0, 2, 3, 4), bm.transpose(1, 0, 2, 3, 4),
                                         xdt.transpose(1, 0, 2, 3, 4, 5), acs.transpose(3, 0, 1, 2, 4)))
    y_off = y_off.transpose(1, 0, 2, 3, 4, 5)
    y = y_diag + y_off + xs * d_skip.astype(f32).reshape(g, j)[:, :, None]
    y = y.reshape(bsz, s, SSD_D_INNER)
    return rmsnorm(y * jax.nn.silu(z.astype(f32)), g_norm)


def diff_attention(q, k, v, lq1, lk1, lq2, lk2, g_norm, lambda_init):
    bsz, s, _ = q.shape
    h, dh = DIFF_N_HEADS, DIFF_HEAD_DIM
    pos = jnp.arange(s)
    q = rope(q.reshape(bsz, s, h, 2, dh), pos)
    k = rope(k.reshape(bsz, s, h, 2, dh), pos)
    v = v.reshape(bsz, s, h, DIFF_V_DIM)
    f32 = jnp.float32
    lam = (jnp.exp(jnp.sum(lq1.astype(f32) * lk1.astype(f32)))
           - jnp.exp(jnp.sum(lq2.astype(f32) * lk2.astype(f32))) + lambda_init)
    scale = dh ** -0.5
    nb = s // Q_BLOCK
    qb = q.reshape(bsz, nb, Q_BLOCK, h, 2, dh).transpose(1, 0, 2, 3, 4, 5)

    def block(args):
        q_blk, i = args
        sc = jnp.einsum('bqhmd,bkhmd->bhmqk', q_blk, k).astype(f32) * scale
        qpos = i * Q_BLOCK + jnp.arange(Q_BLOCK)
        pr = masked_softmax(sc, qpos[:, None] >= pos[None, :])
        w = pr[:, :, 0] - lam * pr[:, :, 1]
        return jnp.einsum('bhqk,bkhe->bqhe', w.astype(v.dtype), v)

    o = lax.map(block, (qb, jnp.arange(nb)))
    o = o.transpose(1, 0, 2, 3, 4).reshape(bsz, s, h, DIFF_V_DIM)
    o = rmsnorm(o, g_norm) * (1.0 - lambda_init)
    return o.reshape(bsz, s, DIFF_WIDTH)


def nsa_attention(q, k_cmp, v_cmp, k_slc, v_slc, k_win, v_win, gate_logits,
                  pos_cmp, ck_w1, ck_w2, cv_w1, cv_w2):
    bsz, s, _ = q.shape
    g, j, dh = NSA_N_KV, NSA_HPG, NSA_HEAD_DIM
    f32 = jnp.float32
    pos = jnp.arange(s)
    scale = dh ** -0.5
    q = rope(q.reshape(bsz, s, g, j, dh), pos)
    k_cmp = rope(k_cmp.reshape(bsz, s, g, dh), pos)
    k_slc = rope(k_slc.reshape(bsz, s, g, dh), pos)
    k_win = rope(k_win.reshape(bsz, s, g, dh), pos)
    v_cmp = v_cmp.reshape(bsz, s, g, dh)
    v_slc = v_slc.reshape(bsz, s, g, dh)
    v_win = v_win.reshape(bsz, s, g, dh)

    n_cmp = (s - CMP_BLOCK) // CMP_STRIDE + 1
    cidx = np.arange(n_cmp)[:, None] * CMP_STRIDE + np.arange(CMP_BLOCK)[None, :]

    def compress(t, w1, w2):
        blk = t[:, cidx] + pos_cmp[None, None, :, None, :]
        blk = blk.transpose(0, 1, 3, 2, 4).reshape(bsz, n_cmp, g, CMP_BLOCK * dh)
        return jax.nn.gelu(blk @ w1) @ w2

    kc = compress(k_cmp, ck_w1, ck_w2)
    vc = compress(v_cmp, cv_w1, cv_w2)
    s_c = jnp.einsum('bsgjd,bngd->bgjsn', q, kc).astype(f32) * scale
    mask_c = pos[:, None] >= jnp.asarray(cidx[:, -1])[None, :]
    p_c = masked_softmax(s_c, mask_c)
    o_cmp = jnp.einsum('bgjsn,bngd->bsgjd', p_c.astype(vc.dtype), vc)

    n_sel = s // SEL_BLOCK
    topk = min(SEL_TOPK, n_sel)
    cs = np.arange(n_cmp)[:, None] * CMP_STRIDE
    ss = np.arange(n_sel)[None, :] * SEL_BLOCK
    overlap = np.clip(np.minimum(cs + CMP_BLOCK, ss + SEL_BLOCK) - np.maximum(cs, ss), 0, None) / CMP_BLOCK
    imp = jnp.einsum('bgjsn,nm->bgsm', p_c, jnp.asarray(overlap, f32))
    cur = pos // SEL_BLOCK
    blk_ids = jnp.arange(n_sel)
    forced = (blk_ids[None, :] == cur[:, None]) | (blk_ids[None, :] == 0)
    future = blk_ids[None, :] > cur[:, None]
    imp = jnp.where(forced, FORCE_SCORE, jnp.where(future, NEG_INF, imp))
    _, sel_idx = lax.top_k(imp, topk)
    ks_blocks = k_slc.reshape(bsz, n_sel, SEL_BLOCK, g, dh).transpose(0, 3, 1, 2, 4)
    vs_blocks = v_slc.reshape(bsz, n_sel, SEL_BLOCK, g, dh).transpose(0, 3, 1, 2, 4)
    gather_blocks = jax.vmap(jax.vmap(lambda kb, ib: kb[ib]))
    nqb = s // SEL_BLOCK
    q_sb = q.reshape(bsz, nqb, SEL_BLOCK, g, j, dh).transpose(1, 0, 2, 3, 4, 5)
    idx_sb = sel_idx.reshape(bsz, g, nqb, SEL_BLOCK, topk).transpose(2, 0, 1, 3, 4)

    def sel_block(args):
        q_blk, idx_blk, i = args
        kg = gather_blocks(ks_blocks, idx_blk)
        vg = gather_blocks(vs_blocks, idx_blk)
        sc = jnp.einsum('bqgjd,bgqkld->bgjqkl', q_blk, kg).astype(f32) * scale
        qpos = i * SEL_BLOCK + jnp.arange(SEL_BLOCK)
        kpos = idx_blk[..., None] * SEL_BLOCK + jnp.arange(SEL_BLOCK)
        mask = (kpos <= qpos[None, None, :, None, None])[:, :, None]
        sc = sc.reshape(bsz, g, j, SEL_BLOCK, topk * SEL_BLOCK)
        mask = mask.reshape(bsz, g, 1, SEL_BLOCK, topk * SEL_BLOCK)
        pr = masked_softmax(sc, mask)
        vg = vg.reshape(bsz, g, SEL_BLOCK, topk * SEL_BLOCK, dh)
        return jnp.einsum('bgjqt,bgqtd->bqgjd', pr.astype(vg.dtype), vg)

    o_slc = lax.map(sel_block, (q_sb, idx_sb, jnp.arange(nqb)))
    o_slc = o_slc.transpose(1, 0, 2, 3, 4, 5).reshape(bsz, s, g, j, dh)

    nwb = s // Q_BLOCK
    span = WINDOW + Q_BLOCK
    kw_pad = jnp.pad(k_win, ((0, 0), (WINDOW, 0), (0, 0), (0, 0)))
    vw_pad = jnp.pad(v_win, ((0, 0), (WINDOW, 0), (0, 0), (0, 0)))
    q_wb = q.reshape(bsz, nwb, Q_BLOCK, g, j, dh).transpose(1, 0, 2, 3, 4, 5)

    def win_block(args):
        q_blk, i = args
        start = i * Q_BLOCK
        kb = lax.dynamic_slice_in_dim(kw_pad, start, span, axis=1)
        vb = lax.dynamic_slice_in_dim(vw_pad, start, span, axis=1)
        sc = jnp.einsum('bqgjd,bkgd->bgjqk', q_blk, kb).astype(f32) * scale
        qpos = start + jnp.arange(Q_BLOCK)
        kpos = start - WINDOW + jnp.arange(span)
        diff = qpos[:, None] - kpos[None, :]
        mask = (diff >= 0) & (diff < WINDOW) & (kpos[None, :] >= 0)
        pr = masked_softmax(sc, mask)
        return jnp.einsum('bgjqk,bkgd->bqgjd', pr.astype(vb.dtype), vb)

    o_win = lax.map(win_block, (q_wb, jnp.arange(nwb)))
    o_win = o_win.transpose(1, 0, 2, 3, 4, 5).reshape(bsz, s, g, j, dh)

    gates = jax.nn.sigmoid(gate_logits.astype(f32)).reshape(bsz, s, g, j, 3)
    o = (gates[..., 0:1] * o_cmp.astype(f32) + gates[..., 1:2] * o_slc.astype(f32)
         + gates[..., 2:3] * o_win.astype(f32))
    return o.reshape(bsz, s, NSA_WIDTH)


def _lru_combine(c1, c2):
    a1, b1 = c1
    a2, b2 = c2
    return a1 * a2, a2 * b1 + b2


def rglru_mixer(gate_in, x_in, conv_w, conv_b, w_r, b_r, w_i, b_i, lam):
    bsz, s, _ = x_in.shape
    f32 = jnp.float32
    xc = causal_dwconv(x_in, conv_w, conv_b)
    xb = xc.reshape(bsz, s, RNN_BLOCKS, RNN_BLOCK_DIM)
    r = jax.nn.sigmoid(jnp.einsum('bshi,hio->bsho', xb, w_r).reshape(bsz, s, RNN_WIDTH).astype(f32) + b_r)
    ig = jax.nn.sigmoid(jnp.einsum('bshi,hio->bsho', xb, w_i).reshape(bsz, s, RNN_WIDTH).astype(f32) + b_i)
    log_a = -RG_C * r * jax.nn.softplus(-lam.astype(f32))
    a = jnp.exp(log_a)
    u = jnp.sqrt(-jnp.expm1(2.0 * log_a)) * (ig * xc.astype(f32))
    _, hs = lax.associative_scan(_lru_combine, (a, u), axis=1)
    return hs * jax.nn.gelu(gate_in.astype(f32))


def setup_inputs(seed: int = 0) -> dict:
    key = jax.random.key(seed)
    ks = iter(jax.random.split(key, 64))
    f32 = jnp.float32
    L = DEPTH

    def nrm(shape, scale):
        return jax.random.normal(next(ks), shape, f32) * scale

    def gain(shape):
        return 1.0 + 0.02 * jax.random.normal(next(ks), shape, f32)

    dt0 = jnp.exp(jax.random.uniform(next(ks), (L, SSD_N_HEADS), f32, math.log(1e-3), math.log(1e-1)))
    a0 = jax.random.uniform(next(ks), (L, RNN_WIDTH), f32, 0.9, 0.999)
    return {
        'x': nrm((BATCH, SEQ, D_MODEL), 1.0),
        'p': nrm((L, BATCH, SEQ, PLE_DIM), 1.0),
        'norm_mix': gain((L, D_MODEL)),
        'norm_ffn': gain((L, D_MODEL)),
        'norm_ple': gain((L, D_MODEL)),
        'w_in': nrm((L, D_MODEL, D_IN), D_MODEL ** -0.5),
        'ssd_conv_w': nrm((L, SSD_CONV, SSD_CONV_DIM), SSD_CONV ** -0.5),
        'ssd_conv_b': nrm((L, SSD_CONV_DIM), 0.02),
        'ssd_dt_bias': dt0 + jnp.log(-jnp.expm1(-dt0)),
        'ssd_a_log': jnp.log(jax.random.uniform(next(ks), (L, SSD_N_HEADS), f32, 1.0, 16.0)),
        'ssd_d': gain((L, SSD_N_HEADS)),
        'ssd_norm': gain((L, SSD_D_INNER)),
        'diff_lq1': nrm((L, DIFF_HEAD_DIM), 0.1),
        'diff_lk1': nrm((L, DIFF_HEAD_DIM), 0.1),
        'diff_lq2': nrm((L, DIFF_HEAD_DIM), 0.1),
        'diff_lk2': nrm((L, DIFF_HEAD_DIM), 0.1),
        'diff_norm': gain((L, DIFF_V_DIM)),
        'nsa_pos_cmp': nrm((L, CMP_BLOCK, NSA_HEAD_DIM), 0.02),
        'nsa_ck_w1': nrm((L, CMP_BLOCK * NSA_HEAD_DIM, CMP_HIDDEN), (CMP_BLOCK * NSA_HEAD_DIM) ** -0.5),
        'nsa_ck_w2': nrm((L, CMP_HIDDEN, NSA_HEAD_DIM), CMP_HIDDEN ** -0.5),
        'nsa_cv_w1': nrm((L, CMP_BLOCK * NSA_HEAD_DIM, CMP_HIDDEN), (CMP_BLOCK * NSA_HEAD_DIM) ** -0.5),
        'nsa_cv_w2': nrm((L, CMP_HIDDEN, NSA_HEAD_DIM), CMP_HIDDEN ** -0.5),
        'rnn_conv_w': nrm((L, RNN_CONV, RNN_WIDTH), RNN_CONV ** -0.5),
        'rnn_conv_b': nrm((L, RNN_WIDTH), 0.02),
        'rnn_w_r': nrm((L, RNN_BLOCKS, RNN_BLOCK_DIM, RNN_BLOCK_DIM), RNN_BLOCK_DIM ** -0.5),
        'rnn_b_r': nrm((L, RNN_WIDTH), 0.02),
        'rnn_w_i': nrm((L, RNN_BLOCKS, RNN_BLOCK_DIM, RNN_BLOCK_DIM), RNN_BLOCK_DIM ** -0.5),
        'rnn_b_i': nrm((L, RNN_WIDTH), 0.02),
        'rnn_lambda': jnp.log(a0) - jnp.log1p(-a0),
        'w_merge_gate': nrm((L, N_BRANCH, D_MODEL, D_MODEL), D_MODEL ** -0.5),
        'w_branch': nrm((L, N_BRANCH, MIX_WIDTH, D_MODEL), MIX_WIDTH ** -0.5),
        'w_out': nrm((L, D_MODEL, D_MODEL), D_MODEL ** -0.5),
        'ffn_w_up': nrm((L, D_MODEL, 2 * D_FF), D_MODEL ** -0.5),
        'ffn_conv_w': nrm((L, FFN_CONV, 2 * D_FF), FFN_CONV ** -0.5),
        'ffn_conv_b': nrm((L, 2 * D_FF), 0.02),
        'ffn_w_down': nrm((L, D_FF, D_MODEL), D_FF ** -0.5),
        'ple_w_proj': nrm((L, PLE_DIM, D_MODEL), PLE_DIM ** -0.5),
        'ple_w_gate': nrm((L, D_MODEL, D_MODEL), D_MODEL ** -0.5),
        'norm_final': gain((D_MODEL,)),
    }


def reference(x, p, norm_mix, norm_ffn, norm_ple, w_in,
              ssd_conv_w, ssd_conv_b, ssd_dt_bias, ssd_a_log, ssd_d, ssd_norm,
              diff_lq1, diff_lk1, diff_lq2, diff_lk2, diff_norm,
              nsa_pos_cmp, nsa_ck_w1, nsa_ck_w2, nsa_cv_w1, nsa_cv_w2,
              rnn_conv_w, rnn_conv_b, rnn_w_r, rnn_b_r, rnn_w_i, rnn_b_i, rnn_lambda,
              w_merge_gate, w_branch, w_out,
              ffn_w_up, ffn_conv_w, ffn_conv_b, ffn_w_down,
              ple_w_proj, ple_w_gate, norm_final):
    f32 = jnp.float32
    for l in range(DEPTH):
        h = rmsnorm(x, norm_mix[l])
        proj = h @ w_in[l]
        (a_z, a_xbc, a_dt, b_q, b_k, b_v, c_q, c_kc, c_vc, c_ks, c_vs, c_kw, c_vw, c_g,
         d_gate, d_x) = jnp.split(proj, list(IN_OFFSETS), axis=-1)
        o_a = ssd_mixer(a_z, a_xbc, a_dt, ssd_conv_w[l], ssd_conv_b[l], ssd_dt_bias[l],
                        ssd_a_log[l], ssd_d[l], ssd_norm[l])
        o_b = diff_attention(b_q, b_k, b_v, diff_lq1[l], diff_lk1[l], diff_lq2[l], diff_lk2[l],
                             diff_norm[l], 0.8 - 0.6 * math.exp(-0.3 * l))
        o_c = nsa_attention(c_q, c_kc, c_vc, c_ks, c_vs, c_kw, c_vw, c_g, nsa_pos_cmp[l],
                            nsa_ck_w1[l], nsa_ck_w2[l], nsa_cv_w1[l], nsa_cv_w2[l])
        o_d = rglru_mixer(d_gate, d_x, rnn_conv_w[l], rnn_conv_b[l], rnn_w_r[l], rnn_b_r[l],
                          rnn_w_i[l], rnn_b_i[l], rnn_lambda[l])
        merged = jnp.zeros(x.shape, f32)
        for n, o in enumerate((o_a, o_b, o_c, o_d)):
            gate = jax.nn.sigmoid((h @ w_merge_gate[l, n]).astype(f32))
            merged = merged + gate * (o.astype(x.dtype) @ w_branch[l, n]).astype(f32)
        x = x + (merged.astype(x.dtype) @ w_out[l]).astype(x.dtype)
        h = rmsnorm(x, norm_ffn[l])
        u = causal_dwconv(h @ ffn_w_up[l], ffn_conv_w[l], ffn_conv_b[l])
        u_gate, u_val = jnp.split(u, 2, axis=-1)
        x = x + ((jax.nn.gelu(u_gate) * u_val) @ ffn_w_down[l]).astype(x.dtype)
        g_ple = jax.nn.sigmoid((rmsnorm(x, norm_ple[l]) @ ple_w_gate[l]).astype(f32))
        x = x + (g_ple * (p[l] @ ple_w_proj[l]).astype(f32)).astype(x.dtype)
    return rmsnorm(x, norm_final)
```

```python
import math
import numpy as np
import ml_dtypes
import concourse.bass as bass
import concourse.mybir as mybir
from concourse.bass_utils import run_bass_kernel_spmd

F32 = mybir.dt.float32
BF16 = mybir.dt.bfloat16
AF = mybir.ActivationFunctionType
ALU = mybir.AluOpType
AX = mybir.AxisListType

ENGS = ("pe", "act", "dve", "pool", "sp")
SEM_EPOCH = 30000

D = 2048
S = 2048
DEPTH = 2
D_IN = 10304
D_FF = 6144
PLE = 256
EPS = 1e-6
NEG = -30000.0
PLAN_ONLY = None
FFN_WSLOTS = 4
ROPE_ADD_ENG = "dve"


class Prog:
    def __init__(self, nc):
        self.nc = nc
        self.ops = []
        self.last_w = {}
        self.readers = {}
        self.dma_last = {}
        self.dma_since = []
        self.last_on = {}
        self.epoch_op = None

    def add(self, eng, fn, reads=(), writes=(), dma=None):
        deps = set()
        if self.epoch_op is not None:
            deps.add(self.epoch_op)
        for r in reads:
            if r in self.last_w:
                deps.add(self.last_w[r])
        for w in writes:
            if w in self.last_w:
                deps.add(self.last_w[w])
            for x in self.readers.get(w, ()):
                deps.add(x)
        idx = len(self.ops)
        if dma is not None:
            if dma in self.dma_last:
                deps.add(self.dma_last[dma])
            self.dma_last[dma] = idx
            self.dma_since.append(idx)
        else:
            self.last_on[eng] = idx
        self.ops.append(dict(eng=eng, fn=fn, deps=deps, dma=dma))
        for r in reads:
            self.readers.setdefault(r, []).append(idx)
        for w in writes:
            self.last_w[w] = idx
            self.readers[w] = []
        return idx

    def barrier(self):
        deps = set(self.last_on.values()) | set(self.dma_since)
        if self.epoch_op is not None:
            deps.add(self.epoch_op)
        idx = len(self.ops)
        self.ops.append(dict(eng="sp", fn=lambda e: e.nop(), deps=deps, dma=None, barrier=True))
        self.last_on["sp"] = idx
        self.dma_since = []
        self.epoch_op = idx
        return idx

    def emit(self, final_waits=()):
        nc = self.nc
        ops = self.ops
        n = len(ops)
        needed = [False] * n
        for i, o in enumerate(ops):
            pruned = set()
            for d in o["deps"]:
                od = ops[d]
                if od["dma"] is None and od["eng"] == "pe" and o["eng"] == "pe" and o["dma"] is None:
                    continue
                pruned.add(d)
            o["deps"] = pruned
            for d in pruned:
                needed[d] = True
        for i in final_waits:
            needed[i] = True
        sem_objs = {}
        ctxs = []

        def get_sem(key):
            if key not in sem_objs:
                c = nc.semaphore("s_%d" % len(sem_objs))
                s = c.__enter__()
                ctxs.append(c)
                sem_objs[key] = s
            return sem_objs[key]

        last_use = {}
        for i, o in enumerate(ops):
            if o["dma"] is not None:
                last_use[o["dma"]] = i
        free_slots = []
        key2slot = {}
        nslots = [0]
        cnt = {}
        for i, o in enumerate(ops):
            if o.get("barrier"):
                for k in [k for k in key2slot if last_use[k] < i]:
                    free_slots.append(key2slot.pop(k))
            if not needed[i]:
                o["sig"] = None
                continue
            if o["dma"] is not None:
                k = o["dma"]
                if k not in key2slot:
                    fs = [x for x in free_slots if x[2] == o["eng"]]
                    if fs:
                        free_slots.remove(fs[-1])
                        key2slot[k] = fs[-1]
                    else:
                        key2slot[k] = ("dmaslot", nslots[0], o["eng"])
                        nslots[0] += 1
                key = key2slot[k]
                c = cnt.get(key, 0) + 16
                cnt[key] = c
                o["sig"] = (key, get_sem(key), 16, c)
            else:
                base = ("eng", o["eng"])
                ep = cnt.get((base, "ep"), 0)
                c = cnt.get((base, ep), 0) + 1
                if c > SEM_EPOCH:
                    ep += 1
                    cnt[(base, "ep")] = ep
                    c = 1
                cnt[(base, ep)] = c
                o["sig"] = ((base, ep), get_sem((base, ep)), 1, c)
        self.n_sems = len(sem_objs)
        by_eng = {e: [] for e in ENGS}
        for i, o in enumerate(ops):
            by_eng[o["eng"]].append(i)
        with nc.Block() as block:
            def make(engname):
                def body(eng):
                    waited = {}
                    for i in by_eng[engname]:
                        o = ops[i]
                        need = {}
                        for d in o["deps"]:
                            k, s, _, v = ops[d]["sig"]
                            if v > need.get(k, (None, 0))[1]:
                                need[k] = (s, v)
                        for k, (s, v) in need.items():
                            if waited.get(k, 0) >= v:
                                continue
                            waited[k] = v
                            eng.wait_ge(s, v)
                        ins = o["fn"](eng)
                        if o["sig"] is not None:
                            _, s, inc, v = o["sig"]
                            ins.then_inc(s, inc)
                    if engname == "sp":
                        for i in final_waits:
                            _, s, _, v = ops[i]["sig"]
                            eng.wait_ge(s, v)
                return body
            block.tensor(make("pe"))
            block.scalar(make("act"))
            block.vector(make("dve"))
            block.gpsimd(make("pool"))
            block.sync(make("sp"))
        for c in reversed(ctxs):
            c.__exit__(None, None, None)


class KB:
    def __init__(self, debug=()):
        self.nc = bass.Bass("TRN2", target_bir_lowering=False)
        self.P = Prog(self.nc)
        self.debug = set(debug)
        self.ins = {}
        self.outs = {}
        self.n = 0
        self.stack = []

    def din(self, name, shape, dt=F32):
        ap = self.nc.dram_tensor(name, list(shape), dt, kind="ExternalInput").ap()
        self.ins[name] = ap
        return ap

    def dout(self, name, shape, dt=F32):
        ap = self.nc.dram_tensor(name, list(shape), dt, kind="ExternalOutput").ap()
        self.outs[name] = ap
        return ap

    def dtmp(self, name, shape, dt=F32):
        if name in getattr(self, "inject", ()):
            return self.din(name, shape, dt)
        if name in self.debug:
            return self.dout(name, shape, dt)
        return self.nc.dram_tensor(name, list(shape), dt, kind="Internal").ap()

    def push(self):
        self.stack.append([])

    def pop(self):
        for c in reversed(self.stack.pop()):
            c.__exit__(None, None, None)
        self.P.barrier()

    def sb(self, name, shape, dt=F32):
        self.n += 1
        c = self.nc.sbuf_tensor("%s_%d" % (name, self.n), list(shape), dt)
        t = c.__enter__()
        self.stack[-1].append(c)
        return t.ap()

    def psum(self, name, shape, dt=F32):
        self.n += 1
        c = self.nc.psum_tensor("%s_%d" % (name, self.n), list(shape), dt)
        t = c.__enter__()
        self.stack[-1].append(c)
        return t.ap()

    def dma(self, q, out, in_, r=(), w=(), key=None):
        assert key is not None
        return self.P.add(q, lambda e: e.dma_start(out=out, in_=in_), reads=r, writes=w, dma=key)

    def mm(self, out, lhsT, rhs, start, stop, r=(), w=()):
        return self.P.add("pe", lambda e: e.matmul(out, lhsT=lhsT, rhs=rhs, start=start, stop=stop), reads=r, writes=w)

    def tr(self, out, in_, ident, r=(), w=()):
        return self.P.add("pe", lambda e: e.transpose(out, in_, ident), reads=r, writes=w)

    def act(self, out, in_, func, r=(), w=(), bias=None, scale=1.0, accum_out=None):
        def fn(e):
            kw = {}
            if bias is not None:
                kw["bias"] = bias
            if accum_out is not None:
                kw["accum_out"] = accum_out
            return e.activation(out=out, in_=in_, func=func, scale=scale, **kw)
        return self.P.add("act", fn, reads=r, writes=w)

    def tt(self, out, in0, in1, op, r=(), w=(), eng="dve"):
        return self.P.add(eng, lambda e: e.tensor_tensor(out=out, in0=in0, in1=in1, op=op), reads=r, writes=w)

    def ts(self, out, in0, s1, op0, s2=None, op1=None, r=(), w=(), eng="dve", accum_out=None):
        def fn(e):
            kw = {}
            if op1 is not None:
                kw["op1"] = op1
            if accum_out is not None:
                kw["accum_out"] = accum_out
            return e.tensor_scalar(out=out, in0=in0, scalar1=s1, scalar2=s2, op0=op0, **kw)
        return self.P.add(eng, fn, reads=r, writes=w)

    def stt(self, out, in0, scalar, in1, op0, op1, r=(), w=()):
        return self.P.add("dve", lambda e: e.scalar_tensor_tensor(out=out, in0=in0, scalar=scalar, in1=in1, op0=op0, op1=op1), reads=r, writes=w)

    def copy(self, out, in_, r=(), w=(), eng="dve"):
        if eng == "act":
            return self.act(out, in_, AF.Copy, r=r, w=w)
        return self.P.add(eng, lambda e: e.tensor_copy(out=out, in_=in_), reads=r, writes=w)

    def memset(self, ap, val, w=(), eng="dve"):
        return self.P.add(eng, lambda e: e.memset(ap, val), writes=w)

    def recip(self, out, in_, r=(), w=()):
        return self.P.add("dve", lambda e: e.reciprocal(out=out, in_=in_), reads=r, writes=w)


class Rot:
    def __init__(self, kb, name, n, shape, dt=F32, psum=False):
        self.t = []
        for i in range(n):
            ap = kb.psum(name, shape, dt) if psum else kb.sb(name, shape, dt)
            self.t.append((ap, "%s#%d_%d" % (name, i, kb.n)))
        self.i = 0

    def next(self):
        x = self.t[self.i % len(self.t)]
        self.i += 1
        return x


class Prefetch:
    def __init__(self, kb, rot, srcs, q="sp", ahead=2):
        self.kb, self.rot, self.srcs, self.q, self.ahead = kb, rot, srcs, q, ahead
        self.issued = []
        self.i = 0

    def _issue(self):
        k = len(self.issued)
        if k < len(self.srcs):
            t, res = self.rot.next()
            self.kb.dma(self.q, t, self.srcs[k], w=[res], key=res)
            self.issued.append((t, res))

    def get(self):
        while len(self.issued) < min(len(self.srcs), self.i + 1 + self.ahead):
            self._issue()
        x = self.issued[self.i]
        self.i += 1
        return x


def load_w(kb, wt, wres, src, KC, ncol, nsplit=4):
    nsplit = max(1, min(nsplit, KC))
    step = (KC + nsplit - 1) // nsplit
    for i, k0 in enumerate(range(0, KC, step)):
        k1 = min(KC, k0 + step)
        kb.dma("pool", wt[:, k0:k1, :ncol], src[:, k0:k1, :], w=["%s/k%d" % (wres, i)], key="%s/k%d" % (wres, i))
    return lambda kc: "%s/k%d" % (wres, kc // step)


def gemm(kb, W, K, F, actT, act_res, orient, epi, wrot, psrot, T=S, gcols=512):
    KC = K // 128
    Wv = W.rearrange("(c p) n -> p c n", p=128)
    for g0 in range(0, F, gcols):
        gw = min(gcols, F - g0)
        wt, wres = wrot.next()
        wr = load_w(kb, wt, wres, Wv[:, :, g0:g0 + gw], KC, gw)
        if orient == "feat":
            assert gw % 128 == 0
            for fc in range(gw // 128):
                for tg in range(T // 512):
                    ps, pres = psrot.next()
                    for kc in range(KC):
                        kb.mm(ps, wt[:, kc, fc * 128:(fc + 1) * 128], actT[:, kc, tg * 512:(tg + 1) * 512],
                              kc == 0, kc == KC - 1, r=[wr(kc), act_res(kc, tg)], w=[pres])
                    epi(g0 + fc * 128, tg, ps, pres)
        else:
            for tt in range(T // 128):
                ps, pres = psrot.next()
                for kc in range(KC):
                    kb.mm(ps[:, :gw], actT[:, kc, tt * 128:(tt + 1) * 128], wt[:, kc, :gw],
                          kc == 0, kc == KC - 1, r=[wr(kc), act_res(kc, tt // 4)], w=[pres])
                epi(g0, gw, tt, ps[:, :gw], pres)


def norm_phase(kb, C, xT, gcol, hT, h_res, psrot, out_f32=None):
    xrot = Rot(kb, "nx", 2, [128, 16, 512], F32)
    sqrot = Rot(kb, "nsq", 2, [128, 512], F32)
    rs_rot = Rot(kb, "nrs", 2, [128, 512], F32)
    orot = Rot(kb, "nout", 2, [128, 512], F32) if out_f32 is not None else None
    xv = xT.rearrange("(c p) t -> p c t", p=128)
    for tg in range(4):
        xt, xres = xrot.next()
        kb.dma("sp", xt, xv[:, :, tg * 512:(tg + 1) * 512], r=["xT#%d_%d" % (c, tg) for c in range(16)], w=[xres], key=xres)
        ps, pres = psrot.next()
        for c in range(16):
            sq, sres = sqrot.next()
            kb.act(sq, xt[:, c, :], AF.Square, r=[xres], w=[sres])
            kb.mm(ps, C["ones_f"], sq, c == 0, c == 15, r=[sres, "consts"], w=[pres])
        rs, rres = rs_rot.next()
        kb.ts(rs, ps, 1.0 / D, ALU.mult, EPS, ALU.add, r=[pres], w=[rres])
        kb.act(rs, rs, AF.Sqrt, r=[rres], w=[rres])
        kb.recip(rs, rs, r=[rres], w=[rres])
        for c in range(16):
            if out_f32 is None:
                kb.stt(hT[:, c, tg * 512:(tg + 1) * 512], xt[:, c, :], C["vec"][:, gcol + c:gcol + c + 1], rs,
                       ALU.mult, ALU.mult, r=[xres, rres, "consts"], w=[h_res(c, tg)])
            else:
                o, ores = orot.next()
                kb.stt(o, xt[:, c, :], C["vec"][:, gcol + c:gcol + c + 1], rs,
                       ALU.mult, ALU.mult, r=[xres, rres, "consts"], w=[ores])
                kb.fin.append(kb.dma("sp", out_f32[c * 128:(c + 1) * 128, tg * 512:(tg + 1) * 512], o,
                                     r=[ores], w=["outT"], key=ores))


IN_W = (1024, 1536, 16, 1024, 1024, 1024, 1024, 256, 256, 256, 256, 256, 256, 48, 1024, 1024)
IN_NAMES = ("a_z", "a_xbc", "a_dt", "b_q", "b_k", "b_v", "c_q", "c_kc", "c_vc", "c_ks", "c_vs", "c_kw", "c_vw", "c_g", "d_gate", "d_x")
IN_OFF = {}
_o = 0
for _n, _w in zip(IN_NAMES, IN_W):
    IN_OFF[_n] = (_o, _w)
    _o += _w

VEC_SPEC = [("norm_mix", 2048), ("norm_ffn", 2048), ("norm_ple", 2048),
            ("ssd_conv_b", 1536), ("rnn_conv_b", 1024), ("rnn_b_r", 1024), ("rnn_b_i", 1024),
            ("rnn_lambda", 1024), ("ffn_conv_b", 12288)]
VEC_MULTI = [("ssd_conv_w", 4, 1536), ("rnn_conv_w", 4, 1024), ("ffn_conv_w", 3, 12288)]
BC_SPEC = [("ssd_dt_bias_rep", 256), ("ssd_a_log_rep", 256), ("ssd_d_rep", 1024), ("ssd_norm", 1024), ("diff_norm", 128),
           ("diff_lq1", 64), ("diff_lk1", 64), ("diff_lq2", 64), ("diff_lk2", 64)]


def vec_layout():
    off = {}
    o = 0
    for l in range(DEPTH):
        for n, f in VEC_SPEC:
            off[(n, l)] = o
            o += f // 128
        for n, k, f in VEC_MULTI:
            for kk in range(k):
                off[(n, l, kk)] = o
                o += f // 128
    off[("norm_final", 0)] = o
    o += 16
    return off, o


def bc_layout():
    off = {}
    o = 0
    for n, f in BC_SPEC:
        for l in range(DEPTH):
            off[(n, l)] = o
        o += f
    return off, o


def host_consts():
    c = {}
    c["ones_f"] = np.ones((128, 128), np.float32)
    c["ident_f"] = np.eye(128, dtype=np.float32)
    c["triu_f"] = np.triu(np.ones((128, 128), np.float32))
    r = np.arange(128)
    sw = np.where((r % 64) < 32, r + 32, r - 32)
    ps = np.zeros((128, 128), np.float32)
    ps[sw, r] = 1.0
    c["pswap"] = ps
    half = 32
    inv = 10000.0 ** (-np.arange(half, dtype=np.float32) / half)
    ang = np.arange(S, dtype=np.float32)[None, :] * inv[:, None]
    cos = np.cos(ang).astype(np.float32)
    sin = np.sin(ang).astype(np.float32)
    cos2 = np.concatenate([cos, cos, cos, cos], 0)
    sin2 = np.concatenate([-sin, sin, -sin, sin], 0)
    kk = np.arange(128)[:, None]
    qq = np.arange(128)[None, :]
    c["tri"] = np.where(qq >= kk, 0.0, NEG).astype(np.float32)
    c["wlo"] = np.where(qq < kk, 0.0, NEG).astype(np.float32)
    nn = np.arange(128)[:, None]
    tq = np.arange(S)[None, :]
    c["cmp_pen"] = np.where((tq >= 16 * nn + 31) & (nn < 127), 0.0, NEG).astype(np.float32)
    cs = np.arange(127)[:, None] * 16
    ss_ = np.arange(32)[None, :] * 64
    ov = np.clip(np.minimum(cs + 32, ss_ + 64) - np.maximum(cs, ss_), 0, None) / 32.0
    c["ovl"] = np.concatenate([ov, np.zeros((1, 32))], 0).astype(np.float32)
    pos = np.arange(S).reshape(16, 128).T
    cur = pos // 64
    bid = np.arange(32)[None, None, :]
    forced_m = (bid == cur[:, :, None]) | (bid == 0)
    future_m = bid > cur[:, :, None]
    c["forced"] = np.where(forced_m, 1e4, -3e38).astype(np.float32)
    c["future"] = np.where(future_m, -1e30, 3e38).astype(np.float32)
    es = np.zeros((32, 16, 128), np.float32)
    for j in range(16):
        for k in range(128):
            es[2 * j + (k >= 64), j, k] = 1.0
    c["esel"] = es
    c["rope"] = np.stack([cos2 * 0.125, sin2 * 0.125, cos2, sin2], 1).astype(np.float32)
    return c


def pack_vec(inputs):
    off, nv = vec_layout()
    v = np.zeros((128, nv), np.float32)
    for l in range(DEPTH):
        for n, f in VEC_SPEC:
            v[:, off[(n, l)]:off[(n, l)] + f // 128] = np.asarray(inputs[n][l], np.float32).reshape(f // 128, 128).T
        for n, k, f in VEC_MULTI:
            for kk in range(k):
                v[:, off[(n, l, kk)]:off[(n, l, kk)] + f // 128] = np.asarray(inputs[n][l][kk], np.float32).reshape(f // 128, 128).T
    o = off[("norm_final", 0)]
    v[:, o:o + 16] = np.asarray(inputs["norm_final"], np.float32).reshape(16, 128).T
    return v


def pack_bc(inputs):
    off, nb = bc_layout()
    v = np.zeros((DEPTH, 128, nb), np.float32)
    for l in range(DEPTH):
        for n, f in BC_SPEC:
            if n == "ssd_dt_bias_rep":
                a = np.tile(np.asarray(inputs["ssd_dt_bias"][l], np.float32), 16)
            elif n == "ssd_a_log_rep":
                a = np.tile(np.asarray(inputs["ssd_a_log"][l], np.float32), 16)
            elif n == "ssd_d_rep":
                a = np.repeat(np.asarray(inputs["ssd_d"][l], np.float32), 64)
            else:
                a = np.asarray(inputs[n][l], np.float32)
            v[l, :, off[(n, l)]:off[(n, l)] + f] = a.reshape(1, f)
    return v


def load_consts(kb):
    C = {}
    hc = host_consts()
    voff, nv = vec_layout()
    boff, nb = bc_layout()
    C["voff"], C["boff"] = voff, boff
    specs = [("ones_f", [128, 128], F32), ("ident_f", [128, 128], F32), ("pswap", [128, 128], F32), ("triu_f", [128, 128], F32),
             ("vec", [128, nv], F32)]
    for name, shape, dt in specs:
        d = kb.din("c_" + name, shape, dt)
        t = kb.sb("c_" + name, shape, dt)
        kb.dma("sp", t, d, w=["consts"], key="c_" + name)
        C[name] = t
    for name in ("ident", "pswap"):
        src = C["ident_f"] if name == "ident" else C["pswap"]
        t = kb.sb("c_" + name + "_b", [128, 128], BF16)
        kb.copy(t, src, r=["consts"], w=["consts2"])
        C[name + "_b"] = t
        C[name] = t
    C["rope_d"] = kb.din("c_rope", [128, 4, S], F32)
    C["bc_d"] = kb.din("c_bc", [DEPTH, 128, nb], F32)
    C["nb"] = nb
    for name, shape in (("forced", [128, 16, 32]), ("future", [128, 16, 32]), ("cmp_pen", [128, S]), ("esel", [32, 16, 128])):
        C[name + "_d"] = kb.din("c_" + name, shape, F32)
    C["ovl_d"] = kb.din("c_ovl", [128, 32], F32)
    C["posT_d"] = kb.din("c_posT", [DEPTH, 64, 32], F32)
    for name, shape in (("tri", [128, 128]), ("wlo", [128, 128])):
        d = kb.din("c_" + name, shape, F32)
        t = kb.sb("c_" + name + "_b", shape, BF16)
        kb.dma("pool", t, d, w=["consts2"], key="c_" + name)
        C[name + "_b"] = t
        C[name] = t
    return C


def proj_phase(kb, C, l, w_in, hT, h_res, SC):
    kb.push()
    wrot = Rot(kb, "pw", 2, [128, 16, 512], BF16)
    psrot = Rot(kb, "pps", 4, [128, 512], F32, psum=True)
    ps2rot = Rot(kb, "pps2", 2, [128, 512], F32, psum=True)
    rope = kb.sb("rope", [128, 4, S], F32)
    kb.dma("sp", rope, C["rope_d"], w=["rope"], key="rope")
    st32 = Rot(kb, "pst32", 3, [128, 512], F32)
    st16 = Rot(kb, "pst16", 3, [128, 512], BF16)
    xb16 = Rot(kb, "pxb", 3, [128, 512], BF16)
    t1r = Rot(kb, "pt1", 2, [128, 512], F32)
    t2r = Rot(kb, "pt2", 2, [128, 512], F32)
    cnt = [0]
    pending = []

    def flush_pending():
        while pending:
            pending.pop(0)()

    def evac_eng():
        cnt[0] += 1
        return "act" if cnt[0] % 2 else "dve"

    def feat_store(dst, dt, func=None):
        def epi(f0, tg, ps, pres, base):
            st, sres = (st32 if dt == F32 else st16).next()
            if func is not None:
                kb.act(st, ps, func, r=[pres], w=[sres])
            else:
                kb.copy(st, ps, r=[pres], w=[sres], eng=evac_eng())
            fo = f0 - base
            kb.dma("sp", dst[fo:fo + 128, tg * 512:(tg + 1) * 512], st, r=[sres], key=sres)
        return epi

    def feat_rope(dst, qk):
        ci, si = (0, 1) if qk == "q" else (2, 3)

        def epi(f0, tg, ps, pres, base):
            xb, xres = xb16.next()
            kb.copy(xb, ps, r=[pres], w=[xres], eng="act")
            flush_pending()

            def rest(xb=xb, xres=xres, f0=f0, tg=tg, base=base):
                p2, p2res = ps2rot.next()
                kb.mm(p2, C["pswap_b"], xb, True, True, r=[xres, "consts2"], w=[p2res])
                t1, t1res = t1r.next()
                t2, t2res = t2r.next()
                st, sres = st16.next()
                kb.tt(t1, xb, rope[:, ci, tg * 512:(tg + 1) * 512], ALU.mult, r=[xres, "rope"], w=[t1res])
                kb.tt(t2, p2, rope[:, si, tg * 512:(tg + 1) * 512], ALU.mult, r=[p2res, "rope"], w=[t2res])
                kb.tt(st, t1, t2, ALU.add, r=[t1res, t2res], w=[sres], eng=ROPE_ADD_ENG)
                fo = f0 - base
                kb.dma("sp", dst[fo:fo + 128, tg * 512:(tg + 1) * 512], st, r=[sres], key=sres)
            pending.append(rest)
        return epi

    def tok_store(dst, dt, func=None):
        def epi(c0, cw, tt, ps, pres, base):
            st, sres = (st32 if dt == F32 else st16).next()
            if func is not None:
                kb.act(st[:, :cw], ps, func, r=[pres], w=[sres])
            else:
                kb.copy(st[:, :cw], ps, r=[pres], w=[sres], eng=evac_eng())
            co = c0 - base
            kb.dma("sp", dst[tt * 128:(tt + 1) * 128, co:co + cw], st[:, :cw], r=[sres], key=sres)
        return epi

    plan = [
        ("a_z", "tok", tok_store(SC["zs"], F32, AF.Silu)),
        ("a_xbc", "feat", feat_store(SC["xbcT"], F32)),
        ("a_dt", "tok", tok_store(SC["dt"], F32)),
        ("b_q", "feat", feat_rope(SC["bqT"], "q")),
        ("b_k", "feat", feat_rope(SC["bkT"], "k")),
        ("b_v", "tok", tok_store(SC["bv"], BF16)),
        ("c_q", "feat", feat_rope(SC["cqT"], "q")),
        ("c_kc", "feat", feat_rope(SC["ckcT"], "k")),
        ("c_vc", "feat", feat_store(SC["cvcT"], BF16)),
        ("c_ks", "feat", feat_rope(SC["cksT"], "k")),
        ("c_vs", "tok", tok_store(SC["cvs"], BF16)),
        ("c_kw", "feat", feat_rope(SC["ckwT"], "k")),
        ("c_vw", "tok", tok_store(SC["cvw"], BF16)),
        ("c_g", "tok", tok_store(SC["cg"], F32, AF.Sigmoid)),
        ("d_gate", "feat", feat_store(SC["dgT"], F32, AF.Gelu_apprx_tanh)),
        ("d_x", "feat", feat_store(SC["dxT"], F32)),
    ]
    for name, orient, epi in plan:
        if PLAN_ONLY is not None and name not in PLAN_ONLY:
            continue
        c0, cw = IN_OFF[name]
        if orient == "feat":
            gemm(kb, w_in[l][:, c0:c0 + cw], D, cw, hT, h_res, "feat",
                 lambda f0, tg, ps, pres, epi=epi: epi(f0, tg, ps, pres, 0), wrot, psrot)
        else:
            gemm(kb, w_in[l][:, c0:c0 + cw], D, cw, hT, h_res, "tok",
                 lambda g0, gw, tt, ps, pres, epi=epi: epi(g0, gw, tt, ps, pres, 0), wrot, psrot)
        flush_pending()
    kb.pop()


def alloc_scratch(kb):
    SC = {}
    SC["xT"] = kb.dtmp("xT", [D, S], F32)
    SC["zs"] = kb.dtmp("zs", [S, 1024], F32)
    SC["xbcT"] = kb.dtmp("xbcT", [1536, S], F32)
    SC["dt"] = kb.dtmp("dt", [S, 16], F32)
    SC["bqT"] = kb.dtmp("bqT", [1024, S], BF16)
    SC["bkT"] = kb.dtmp("bkT", [1024, S], BF16)
    SC["bv"] = kb.dtmp("bv", [S, 1024], BF16)
    SC["cqT"] = kb.dtmp("cqT", [1024, S], BF16)
    SC["ckcT"] = kb.dtmp("ckcT", [256, S], BF16)
    SC["cvcT"] = kb.dtmp("cvcT", [256, S], BF16)
    SC["cksT"] = kb.dtmp("cksT", [256, S], BF16)
    SC["cvs"] = kb.dtmp("cvs", [S, 256], BF16)
    SC["ckwT"] = kb.dtmp("ckwT", [256, S], BF16)
    SC["cvw"] = kb.dtmp("cvw", [S, 256], BF16)
    SC["cg"] = kb.dtmp("cg", [S, 48], F32)
    SC["dgT"] = kb.dtmp("dgT", [1024, S], F32)
    SC["dxT"] = kb.dtmp("dxT", [1024, S], F32)
    SC["oT"] = kb.dtmp("oT", [4, 1024, S], BF16)
    SC["xs"] = kb.dtmp("xs", [S, 1024], F32)
    SC["Btok"] = kb.dtmp("Btok", [S, 256], BF16)
    SC["mT"] = kb.dtmp("mT", [D, S], BF16)
    SC["gT"] = kb.dtmp("gT", [4, D, S], BF16)
    SC["aT"] = kb.dtmp("aT", [D_FF, S], BF16)
    return SC


def rglru_phase(kb, C, l, Wd, SC):
    kb.push()
    voff = C["voff"]
    vec = C["vec"]
    psr = Rot(kb, "dps", 4, [128, 512], F32, psum=True)
    wbd = kb.sb("dwbd", [128, 2, 8, 128], BF16)
    kb.memset(wbd, 0.0, w=["wbd"])
    for wi, wn in enumerate(("rnn_w_r", "rnn_w_i")):
        src = Wd[wn][l].rearrange("(c two) i o -> two i c o", two=2)
        for hh in range(2):
            kb.dma("pool", wbd[hh * 64:(hh + 1) * 64, wi, :, hh * 64:(hh + 1) * 64], src[hh], r=[], w=["wbd"], key="wbd%d%d" % (wi, hh))
    cl = kb.sb("dcl", [128, 8], F32)
    lam = vec[:, voff[("rnn_lambda", l)]:voff[("rnn_lambda", l)] + 8]
    kb.act(cl, lam, AF.Exp, r=["consts"], w=["cl"], scale=-1.0)
    kb.act(cl, cl, AF.Ln, r=["cl"], w=["cl"], bias=1.0)
    kb.ts(cl, cl, -8.0, ALU.mult, r=["cl"], w=["cl"])
    xrot = Rot(kb, "dx", 2, [128, S + 3], F32)
    grot = Rot(kb, "dg", 2, [128, S], F32)
    for ap, res in xrot.t:
        kb.memset(ap[:, 0:3], 0.0, w=[res])
    wk = [dict(xc=kb.sb("dxc", [128, S], F32), xcb=kb.sb("dxcb", [128, S], BF16), rr=kb.sb("dr", [128, S], F32),
               ig=kb.sb("dig", [128, S], F32), aa=kb.sb("da", [128, S], F32), tmp=kb.sb("dtmp", [128, S], F32),
               hh=kb.sb("dh", [128, S], F32)) for _ in range(2)]
    orot = Rot(kb, "do", 2, [128, S], BF16)

    class _XP(Prefetch):
        def _issue(self):
            k = len(self.issued)
            if k < len(self.srcs):
                t, res = self.rot.next()
                self.kb.dma(self.q, t[:, 3:], self.srcs[k], w=[res], key=res)
                self.issued.append((t, res))
    x_pf = _XP(kb, xrot, [SC["dxT"][c * 128:(c + 1) * 128, :] for c in range(8)], ahead=1)
    g_pf = Prefetch(kb, grot, [SC["dgT"][c * 128:(c + 1) * 128, :] for c in range(8)], ahead=1)
    for c in range(8):
        W_ = wk[c % 2]
        xc, xcb, rr, ig, aa, tmp, hh_ = W_["xc"], W_["xcb"], W_["rr"], W_["ig"], W_["aa"], W_["tmp"], W_["hh"]
        sfx = "%d" % (c % 2)
        x, xres = x_pf.get()
        g, gres = g_pf.get()
        wcol = lambda k: vec[:, voff[("rnn_conv_w", l, k)] + c:voff[("rnn_conv_w", l, k)] + c + 1]
        bcol = vec[:, voff[("rnn_conv_b", l)] + c:voff[("rnn_conv_b", l)] + c + 1]
        kb.act(xc, x[:, 3:3 + S], AF.Identity, r=[xres, "consts"], w=["xc" + sfx], scale=wcol(3), bias=bcol)
        for k in range(3):
            kb.stt(xc, x[:, k:k + S], wcol(k), xc, ALU.mult, ALU.add, r=[xres, "xc" + sfx, "consts"], w=["xc" + sfx])
        kb.copy(xcb, xc, r=["xc" + sfx], w=["xcb" + sfx], eng="act")
        for wi, (dst, dres, bn) in enumerate(((rr, "rr" + sfx, "rnn_b_r"), (ig, "ig" + sfx, "rnn_b_i"))):
            bias = vec[:, voff[(bn, l)] + c:voff[(bn, l)] + c + 1]
            for tg in range(4):
                ps, pres = psr.next()
                kb.mm(ps, wbd[:, wi, c, :], xcb[:, tg * 512:(tg + 1) * 512], True, True, r=["wbd", "xcb" + sfx], w=[pres])
                kb.act(dst[:, tg * 512:(tg + 1) * 512], ps, AF.Sigmoid, r=[pres, "consts"], w=[dres], bias=bias)
        kb.act(aa, rr, AF.Exp, r=["rr" + sfx, "cl"], w=["aa" + sfx], scale=cl[:, c:c + 1])
        kb.act(tmp, aa, AF.Square, r=["aa" + sfx], w=["tmp" + sfx])
        kb.act(tmp, tmp, AF.Sqrt, r=["tmp" + sfx], w=["tmp" + sfx], scale=-1.0, bias=1.0)
        kb.tt(ig, ig, xc, ALU.mult, r=["ig" + sfx, "xc" + sfx], w=["ig" + sfx], eng="pool")
        kb.tt(tmp, tmp, ig, ALU.mult, r=["tmp" + sfx, "ig" + sfx], w=["tmp" + sfx])
        kb.P.add("dve", lambda e, hh_=hh_, aa=aa, tmp=tmp: e.tensor_tensor_scan(out=hh_, data0=aa, data1=tmp, initial=0.0, op0=ALU.mult, op1=ALU.add),
                 reads=["aa" + sfx, "tmp" + sfx], writes=["hh" + sfx])
        o, ores = orot.next()
        kb.tt(o, hh_, g, ALU.mult, r=["hh" + sfx, gres], w=[ores], eng="pool")
        kb.dma("sp", SC["oT"][3, c * 128:(c + 1) * 128, :], o, r=[ores], key=ores)
    kb.pop()


def rms_rstd(kb, out, ss, n, r, w):
    kb.ts(out, ss, 1.0 / n, ALU.mult, EPS, ALU.add, r=r, w=w)
    kb.act(out, out, AF.Sqrt, r=w, w=w)
    kb.recip(out, out, r=w, w=w)


def diff_phase(kb, C, l, SC):
    kb.push()
    boff, bc = C["boff"], C["bc"]
    lam_init = 0.8 - 0.6 * math.exp(-0.3 * l)
    lt = kb.sb("blt", [128, 64], F32)
    ls = kb.sb("bls", [128, 4], F32)
    for i, (a, b) in enumerate((("diff_lq1", "diff_lk1"), ("diff_lq2", "diff_lk2"))):
        oa, ob_ = boff[(a, l)], boff[(b, l)]
        kb.tt(lt, bc[:, oa:oa + 64], bc[:, ob_:ob_ + 64], ALU.mult, r=["consts"], w=["lt"])
        kb.P.add("dve", lambda e, i=i: e.tensor_reduce(out=ls[:, i:i + 1], in_=lt, axis=AX.X, op=ALU.add), reads=["lt"], writes=["ls"])
    kb.act(ls[:, 0:2], ls[:, 0:2], AF.Exp, r=["ls"], w=["ls"])
    kb.tt(ls[:, 2:3], ls[:, 1:2], ls[:, 0:1], ALU.subtract, r=["ls"], w=["ls"])
    kb.ts(ls[:, 2:3], ls[:, 2:3], -lam_init, ALU.add, r=["ls"], w=["ls"])
    neglam = ls[:, 2:3]
    gn = kb.sb("bgn", [128, 128], F32)
    og = boff[("diff_norm", l)]
    kb.ts(gn, bc[:, og:og + 128], 1.0 - lam_init, ALU.mult, r=["consts"], w=["gn"])

    qrot = Rot(kb, "bq", 2, [128, S], BF16)
    krot = Rot(kb, "bk", 2, [128, 2, S], BF16)
    for ap, res in krot.t:
        kb.memset(ap, 0.0, w=[res])
    vrot = Rot(kb, "bvv", 2, [128, 16, 129], BF16)
    for ap, res in vrot.t:
        kb.memset(ap[:, :, 128:129], 1.0, w=[res])
    pss = Rot(kb, "bps", 3, [128, 512], F32, psum=True)
    pso_all = kb.psum("bpo", [128, 4, 512], F32)
    pst = kb.psum("bpt", [128, 512], BF16)
    Pbuf = [kb.sb("bP%d" % i, [128, 16, 512], BF16) for i in range(2)]
    Osr = Rot(kb, "bO", 2, [128, 2, 4, 129], F32)
    sm = kb.sb("bsm", [128, 2, 4], F32)
    ssq = kb.sb("bssq", [128, 4], F32)
    rstd = kb.sb("brstd", [128, 4], F32)
    o1 = kb.sb("bo1", [128, 4, 128], F32)
    t2 = kb.sb("bt2", [128, 4, 128], F32)
    obr2 = Rot(kb, "bob", 2, [128, 4, 128], BF16)
    otr = Rot(kb, "bot", 2, [128, 512], BF16)
    ev = [0]
    cur = {}
    deferred = []

    def load_head(h):
        qT, qres = qrot.next()
        kT, kres = krot.next()
        v, vres = vrot.next()
        kb.dma("sp", qT, SC["bqT"][h * 128:(h + 1) * 128, :], w=[qres], key=qres)
        for m_ in range(2):
            kb.dma("sp", kT[m_ * 64:(m_ + 1) * 64, m_, :], SC["bkT"][h * 128 + m_ * 64:h * 128 + (m_ + 1) * 64, :], w=[kres], key=kres + "m%d" % m_)
        kb.dma("sp", v[:, :, 0:128], SC["bv"][:, h * 128:(h + 1) * 128].rearrange("(j p) e -> p j e", p=128), w=[vres], key=vres)
        return dict(qT=qT, qres=qres, kT=kT, kres=kres, v=v, vres=vres)

    def combine(h, qg, Osb, ores):
        rO = [ores + "m0", ores + "m1"]
        kb.recip(sm, Osb[:, :, :, 128], r=rO, w=["sm"])
        kb.ts(sm[:, 1, :], sm[:, 1, :], neglam, ALU.mult, r=["sm", "ls"], w=["sm"])
        kb.tt(o1, Osb[:, 0, :, 0:128], sm[:, 0, :].unsqueeze(2).broadcast_to([128, 4, 128]), ALU.mult, r=rO + ["sm"], w=["o1"])
        kb.tt(t2, Osb[:, 1, :, 0:128], sm[:, 1, :].unsqueeze(2).broadcast_to([128, 4, 128]), ALU.mult, r=rO + ["sm"], w=["t2"])
        kb.tt(o1, o1, t2, ALU.add, r=["o1", "t2"], w=["o1"], eng="pool")
        kb.tt(t2, o1, o1, ALU.mult, r=["o1"], w=["t2"], eng="pool")
        kb.P.add("dve", lambda e: e.tensor_reduce(out=ssq, in_=t2, axis=AX.X, op=ALU.add), reads=["t2"], writes=["ssq"])
        kb.act(rstd, ssq, AF.Ln, r=["ssq"], w=["rstd"], scale=1.0 / 128, bias=EPS)
        kb.act(rstd, rstd, AF.Exp, r=["rstd"], w=["rstd"], scale=-0.5)
        kb.tt(o1, o1, rstd.unsqueeze(2).broadcast_to([128, 4, 128]), ALU.mult, r=["o1", "rstd"], w=["o1"])
        ob, obres = obr2.next()
        kb.tt(ob, o1, gn.unsqueeze(1).broadcast_to([128, 4, 128]), ALU.mult, r=["o1", "gn"], w=[obres], eng="pool")

        def pe_part(ob=ob, obres=obres, h=h, qg=qg):
            for qt in range(4):
                kb.tr(pst[:, qt * 128:(qt + 1) * 128], ob[:, qt, :], C["ident_b"], r=[obres, "consts2"], w=["pst"])
            ot, otres = otr.next()
            kb.copy(ot, pst, r=["pst"], w=[otres], eng="dve")
            kb.dma("sp", SC["oT"][1, h * 128:(h + 1) * 128, qg * 512:(qg + 1) * 512], ot, r=[otres], key=otres)
        deferred.append(pe_part)

    groups = [(h, qg, m) for h in range(8) for qg in range(4) for m in range(2)]

    def score_steps(gi):
        h, qg, m = groups[gi]
        P = Pbuf[gi % 2]
        steps = []
        for j in range(4 * qg + 4):
            def step(j=j):
                if (h, qg, m, j) == (0, 0, 0, 0):
                    cur[0] = load_head(0)
                if (qg, m, j) == (1, 0, 0) and h + 1 < 8:
                    cur[h + 1] = load_head(h + 1)
                H = cur[h]
                r = j - 4 * qg
                c0 = 128 * r if r > 0 else 0
                ps, pres = pss.next()
                kb.mm(ps[:, c0:], H["kT"][:, m, j * 128:(j + 1) * 128], H["qT"][:, qg * 512 + c0:(qg + 1) * 512],
                      True, r < 0, r=[H["kres"], H["qres"]], w=[pres])
                if r >= 0:
                    kb.mm(ps[:, c0:c0 + 128], C["ident_b"], C["tri_b"], False, True, r=["consts2"], w=[pres])
                kb.act(P[:, j, c0:], ps[:, c0:], AF.Exp, r=[pres], w=["bP%d_%d" % (gi % 2, j)])
            steps.append(step)
        return steps

    def pv_steps(gi):
        h, qg, m = groups[gi]
        P = Pbuf[gi % 2]
        st = (gi % 2) * 2
        steps = []
        for qt in range(4):
            T = 4 * qg + qt
            bank, col = (st, qt * 129) if qt < 3 else (st + 1, 0)
            for j in range(T + 1):
                def step(qt=qt, j=j, T=T, bank=bank, col=col):
                    H = cur[h]
                    kb.mm(pso_all[:, bank, col:col + 129], P[:, j, qt * 128:(qt + 1) * 128], H["v"][:, j, :], j == 0, j == T,
                          r=["bP%d_%d" % (gi % 2, j), H["vres"]], w=["bpo%d" % bank])
                steps.append(step)

        def fin():
            while deferred:
                deferred.pop(0)()
            if m == 0:
                cur[(h, qg)] = Osr.next()
            Osb, ores = cur[(h, qg)]
            eng = "dve"
            kb.copy(Osb[:, m, 0:3, :], pso_all[:, st, 0:387].rearrange("p (q c) -> p q c", c=129), r=["bpo%d" % st], w=[ores + "m%d" % m], eng=eng)
            kb.copy(Osb[:, m, 3, :], pso_all[:, st + 1, 0:129], r=["bpo%d" % (st + 1)], w=[ores + "m%d" % m], eng=eng)
            if m == 1:
                combine(h, qg, Osb, ores)
        steps.append(fin)
        return steps

    for gi in range(len(groups) + 1):
        A = score_steps(gi) if gi < len(groups) else []
        B = pv_steps(gi - 1) if gi > 0 else []
        merge_steps(A, B)
    while deferred:
        deferred.pop(0)()
    kb.pop()


def merge_steps(A, B):
    na, nb = len(A), len(B)
    ia = ib = 0
    while ia < na or ib < nb:
        if ia < na and (ib >= nb or ia * max(nb, 1) <= ib * max(na, 1)):
            A[ia]()
            ia += 1
        else:
            B[ib]()
            ib += 1


def run_pipe(items, stage1, stage2, depth=1):
    q = []
    for it in items:
        q.append((it, stage1(it)))
        if len(q) > depth:
            stage2(*q.pop(0))
    while q:
        stage2(*q.pop(0))


def nsa_phase(kb, C, l, Wd, SC, posT_d):
    kb.push()
    NC_ = 127
    pss = Rot(kb, "cps", 3, [128, 512], F32, psum=True)
    pst = kb.psum("cpt", [128, 1024], BF16)
    kcT2 = kb.sb("ckcT2", [128, 2, 4, 128], BF16)
    vcx = kb.sb("cvcx", [128, 4, 97], BF16)
    kb.memset(kcT2, 0.0, w=["kcT2"])
    kb.memset(vcx, 0.0, w=["vcx"])
    kb.memset(vcx[:, :, 64:65], 1.0, w=["vcx"])
    for g in range(4):
        kb.dma("pool", vcx[:, g, 65:97], C["ovl_d"], w=["vcx"], key="ovl%d" % g)
    posT = kb.sb("cposT", [64, 32], F32)
    kb.dma("sp", posT, posT_d[l], w=["posT"], key="posT")
    srcT = kb.sb("csrc", [64, 4, S], BF16)
    w1 = kb.sb("cw1", [64, 32, 256], BF16)
    w2k = kb.sb("cw2k", [128, 2, 128], BF16)
    w2v = kb.sb("cw2v", [128, 2, 64], BF16)
    ktmp = kb.sb("cktmp", [64, 32, 128], BF16)
    hidT = kb.sb("chid", [128, 2, 128], BF16)
    for kv in range(2):
        src_d = SC["ckcT"] if kv == 0 else SC["cvcT"]
        kb.dma("sp", srcT, src_d.rearrange("(g d) t -> d g t", d=64), w=["srcT"], key="srcT")
        wn1, wn2 = (("nsa_ck_w1", "nsa_ck_w2") if kv == 0 else ("nsa_cv_w1", "nsa_cv_w2"))
        kb.dma("pool", w1, Wd[wn1][l].rearrange("(l d) h -> d l h", d=64), w=["w1"], key="w1")
        w2v_src = Wd[wn2][l].rearrange("(c p) d -> p c d", p=128)
        if kv == 0:
            kb.dma("pool", w2k[:, :, 0:64], w2v_src, w=["w2k"], key="w2ka")
            kb.dma("pool", w2k[:, :, 64:128], w2v_src, w=["w2k"], key="w2kb")
        else:
            kb.dma("pool", w2v, w2v_src, w=["w2v"], key="w2v")
        posb = kb.sb("cposb%d" % kv, [64, 32], BF16)
        kb.copy(posb, posT, r=["posT"], w=["posb%d" % kv])
        hpos = kb.sb("chpos%d" % kv, [128, 2], F32)
        for hc in range(2):
            ps, pres = pss.next()
            for ll in range(32):
                kb.mm(ps[:, 0:1], w1[:, ll, hc * 128:(hc + 1) * 128], posb[:, ll:ll + 1], ll == 0, ll == 31, r=["w1", "posb%d" % kv], w=[pres])
            kb.copy(hpos[:, hc:hc + 1], ps[:, 0:1], r=[pres], w=["hpos%d" % kv])
        for g in range(4):
            for hc in range(2):
                ps, pres = pss.next()
                for ll in range(32):
                    kb.mm(ps[:, 0:NC_], w1[:, ll, hc * 128:(hc + 1) * 128], srcT[:, g, ll:ll + 16 * (NC_ - 1) + 1:16], ll == 0, ll == 31,
                          r=["w1", "srcT"], w=[pres])
                kb.act(hidT[:, hc, 0:NC_], ps[:, 0:NC_], AF.Gelu_apprx_tanh, r=[pres, "hpos%d" % kv], w=["hidT"], bias=hpos[:, hc:hc + 1])
            ps, pres = pss.next()
            if kv == 0:
                for hc in range(2):
                    kb.mm(ps[:, 0:NC_], w2k[:, hc, :], hidT[:, hc, 0:NC_], hc == 0, hc == 1, r=["w2k", "hidT"], w=[pres])
                kb.copy(kcT2[0:64, 0, g, 0:NC_], ps[0:64, 0:NC_], r=[pres], w=["kcT2"])
                kb.copy(kcT2[64:128, 1, g, 0:NC_], ps[64:128, 0:NC_], r=[pres], w=["kcT2"])
            else:
                for hc in range(2):
                    kb.mm(ps[0:NC_, 0:64], hidT[:, hc, 0:NC_], w2v[:, hc, :], hc == 0, hc == 1, r=["w2v", "hidT"], w=[pres])
                kb.copy(vcx[0:NC_, g, 0:64], ps[0:NC_, 0:64], r=[pres], w=["vcx"])
    gates = kb.sb("cgate", [128, 16, 48], F32)
    kb.dma("sp", gates, SC["cg"].rearrange("(t p) c -> p t c", p=128), w=["gates"], key="gates")
    qrot = Rot(kb, "cq", 2, [128, 2, S], BF16)
    ksr = Rot(kb, "cks", 2, [128, 2, S], BF16)
    kwr = Rot(kb, "ckw", 2, [128, 2, S], BF16)
    for rot in (ksr, kwr):
        for ap, res in rot.t:
            kb.memset(ap, 0.0, w=[res])
    vsr = Rot(kb, "cvs", 2, [128, 16, 65], BF16)
    vwr = Rot(kb, "cvw", 2, [128, 16, 65], BF16)
    for rot in (vsr, vwr):
        for ap, res in rot.t:
            kb.memset(ap[:, :, 64:65], 1.0, w=[res])
    pso_all = kb.psum("cpo", [128, 4, 512], F32)
    pso_res = ["cpo%d" % i for i in range(4)]
    pT = Rot(kb, "cpT", 4, [128, 512], BF16)
    oaccr = Rot(kb, "coacc", 2, [128, 4, 4, 64], F32)
    imp = kb.sb("cimp", [128, 4, 32], F32)
    dn = kb.sb("cdn", [128, 4], F32)
    sc_ = kb.sb("csc", [128, 4], F32)
    tmpo = kb.sb("ctmpo", [128, 4, 64], F32)
    tmpi = kb.sb("ctmpi", [128, 4, 32], F32)
    top8 = kb.sb("ctop8", [128, 4, 8], F32)
    penq = kb.sb("cpenq", [128, 4, 32], BF16)
    penT = kb.sb("cpenT", [128, 512], BF16)
    kb.memset(penT, 0.0, w=["penT"])
    otr = Rot(kb, "cot", 2, [128, 2, 512], BF16)
    forced = kb.sb("cforced", [128, 16, 32], F32)
    future = kb.sb("cfuture", [128, 16, 32], F32)
    cmp_pen = kb.sb("ccmp_pen", [128, S], BF16)
    esel = kb.sb("cesel", [128, 16, 128], BF16)
    kb.memset(esel, 0.0, w=["consts2"])
    kb.dma("sp", forced, C["forced_d"], w=["consts"], key="cforced")
    kb.dma("sp", future, C["future_d"], w=["consts"], key="cfuture")
    kb.dma("pool", cmp_pen, C["cmp_pen_d"], w=["consts2"], key="ccmp_pen")
    kb.dma("pool", esel[0:32], C["esel_d"], w=["consts2"], key="cesel")
    cur = {}

    def load_group(g):
        qT, qres = qrot.next()
        kb.dma("sp", qT, SC["cqT"][g * 256:(g + 1) * 256, :].rearrange("(c p) t -> p c t", p=128), w=[qres], key=qres)
        ks, ksres = ksr.next()
        kw, kwres = kwr.next()
        for half in range(2):
            kb.dma("sp", ks[half * 64:(half + 1) * 64, half, :], SC["cksT"][g * 64:(g + 1) * 64, :], w=[ksres], key=ksres + "h%d" % half)
            kb.dma("sp", kw[half * 64:(half + 1) * 64, half, :], SC["ckwT"][g * 64:(g + 1) * 64, :], w=[kwres], key=kwres + "h%d" % half)
        vs, vsres = vsr.next()
        vw, vwres = vwr.next()
        kb.dma("sp", vs[:, :, 0:64], SC["cvs"][:, g * 64:(g + 1) * 64].rearrange("(j p) e -> p j e", p=128), w=[vsres], key=vsres)
        kb.dma("sp", vw[:, :, 0:64], SC["cvw"][:, g * 64:(g + 1) * 64].rearrange("(j p) e -> p j e", p=128), w=[vwres], key=vwres)
        return dict(qT=qT, qres=qres, ks=ks, ksres=ksres, kw=kw, kwres=kwres, vs=vs, vsres=vsres, vw=vw, vwres=vwres)

    def evac(O, den, rres, g, qg, jh, branch, first):
        oacc, oares = cur[("oacc", g, qg)]
        hh = g * 4 + jh
        kb.ts(dn, den, 1e-30, ALU.max, r=rres, w=["dn"])
        kb.recip(dn, dn, r=["dn"], w=["dn"])
        kb.tt(sc_, dn, gates[:, 4 * qg:4 * qg + 4, hh * 3 + branch], ALU.mult, r=["dn", "gates"], w=["sc"])
        dst = oacc[:, :, jh, :]
        dres = oares + "j%d" % jh
        sb_ = sc_.unsqueeze(2).broadcast_to([128, 4, 64])
        if first:
            kb.tt(dst, O, sb_, ALU.mult, r=rres + ["sc"], w=[dres])
        else:
            kb.tt(tmpo, O, sb_, ALU.mult, r=rres + ["sc"], w=["tmpo"])
            kb.tt(dst, dst, tmpo, ALU.add, r=[dres, "tmpo"], w=[dres], eng="pool")

    def topk_dve(g, qg):
        kb.tt(imp, imp, forced[:, 4 * qg:4 * qg + 4, :], ALU.max, r=["imp", "consts"], w=["imp"])
        kb.tt(imp, imp, future[:, 4 * qg:4 * qg + 4, :], ALU.min, r=["imp", "consts"], w=["imp"])
        for qt in range(4):
            kb.P.add("dve", lambda e, qt=qt: e.max(out=top8[:, qt, :], in_=imp[:, qt, :]), reads=["imp"], writes=["top8"])
        kb.tt(penq, imp, top8[:, :, 7:8].broadcast_to([128, 4, 32]), ALU.is_ge, r=["imp", "top8"], w=["penq"])
        kb.ts(penq, penq, -NEG, ALU.mult, NEG, ALU.add, r=["penq"], w=["penq"])

    def topk_pe(g, qg):
        for qt in range(4):
            kb.tr(pst[0:32, qt * 128:(qt + 1) * 128], penq[:, qt, :], C["ident_b"], r=["penq", "consts2"], w=["pst"])
        kb.copy(penT[0:32, :], pst[0:32, 0:512], r=["pst"], w=["penT"])

    ocbr = Rot(kb, "cocb", 2, [128, 4, 256], BF16)

    def writeout(g, qg):
        oacc, oares = cur[("oacc", g, qg)]
        ocb, ocres = ocbr.next()
        cur[("ocb", g, qg)] = (ocb, ocres)
        kb.copy(ocb, oacc.rearrange("p q j d -> p q (j d)"), r=[oares + "j%d" % jh for jh in range(4)], w=[ocres], eng="dve")

    def writeout_pe(g, qg):
        ocb, ocres = cur[("ocb", g, qg)]
        for qt in range(4):
            for c in range(2):
                kb.tr(pst[:, c * 512 + qt * 128:c * 512 + (qt + 1) * 128], ocb[:, qt, c * 128:(c + 1) * 128], C["ident_b"], r=[ocres, "consts2"], w=["pst"])
        ot, otres = otr.next()
        kb.copy(ot, pst.rearrange("p (c q) -> p c q", c=2), r=["pst"], w=[otres])
        for c in range(2):
            kb.dma("sp", SC["oT"][2, g * 256 + c * 128:g * 256 + (c + 1) * 128, qg * 512:(qg + 1) * 512], ot[:, c, :], r=[otres], key=otres + "c%d" % c)

    def stage1(it):
        kind, g, qg, jh = it[0], it[1], it[2], it[3]
        if kind == "sync":
            return None
        if kind == "cmp" and g == 0 and qg == 0 and jh == 0:
            cur[0] = load_group(0)
        if kind == "cmp" and qg == 1 and jh == 0 and g + 1 < 4:
            cur[g + 1] = load_group(g + 1)
        if kind == "cmp" and jh == 0:
            cur[("oacc", g, qg)] = oaccr.next()
        G = cur[g]
        c, hb = jh // 2, (jh % 2) * 64
        qT, qres = G["qT"], G["qres"]
        ps, pres = pss.next()
        p, ptres = pT.next()
        if kind == "cmp":
            kb.mm(ps, kcT2[:, jh % 2, g, :], qT[:, c, qg * 512:(qg + 1) * 512], True, False, r=["kcT2", qres], w=[pres])
            kb.mm(ps, C["ident_b"], cmp_pen[:, qg * 512:(qg + 1) * 512], False, True, r=["consts2"], w=[pres])
            kb.act(p, ps, AF.Exp, r=[pres], w=[ptres])
        elif kind == "slc":
            j = it[4]
            if jh == 0 and j == 0:
                topk_pe(g, qg)
            r = j - 4 * qg
            c0 = 128 * r if r > 0 else 0
            kb.mm(ps[:, c0:], G["ks"][:, jh % 2, j * 128:(j + 1) * 128], qT[:, c, qg * 512 + c0:(qg + 1) * 512], True, False,
                  r=[G["ksres"], qres], w=[pres])
            kb.mm(ps[:, c0:], esel[:, j, :], penT[:, c0:], False, r < 0, r=["consts2", "penT"], w=[pres])
            if r >= 0:
                kb.mm(ps[:, c0:c0 + 128], C["ident_b"], C["tri_b"], False, True, r=["consts2"], w=[pres])
            kb.act(p[:, c0:], ps[:, c0:], AF.Exp, r=[pres], w=[ptres])
        else:
            j, wk, r = it[4]
            if wk == "lo":
                ca, cb, pen = 0, 128 * (r + 1), C["wlo_b"]
            else:
                ca, cb, pen = 128 * r, 512, C["tri_b"]
            kb.mm(ps[:, ca:cb], G["kw"][:, jh % 2, j * 128:(j + 1) * 128], qT[:, c, qg * 512 + ca:qg * 512 + cb], True, False,
                  r=[G["kwres"], qres], w=[pres])
            kb.mm(ps[:, 128 * r:128 * r + 128], C["ident_b"], pen, False, True, r=["consts2"], w=[pres])
            kb.act(p[:, ca:cb], ps[:, ca:cb], AF.Exp, r=[pres], w=[ptres])
        return p, ptres

    def stage2(it, st):
        kind, g, qg, jh = it[0], it[1], it[2], it[3]
        if kind == "sync":
            it[4](g, qg)
            return
        p, ptres = st
        G = cur[g]
        if kind == "cmp":
            bank = pso_all[:, jh, :]
            for qt in range(4):
                kb.mm(bank[:, qt * 97:(qt + 1) * 97], p[:, qt * 128:(qt + 1) * 128], vcx[:, g, :], True, True, r=[ptres, "vcx"], w=[pso_res[jh]])
            a = bank[:, 0:388].rearrange("p (q c) -> p q c", c=97)
            evac(a[:, :, 0:64], a[:, :, 64], [pso_res[jh]], g, qg, jh, 0, True)
            dnb = dn.unsqueeze(2).broadcast_to([128, 4, 32])
            if jh == 0:
                kb.tt(imp, a[:, :, 65:97], dnb, ALU.mult, r=[pso_res[jh], "dn"], w=["imp"])
            else:
                kb.tt(tmpi, a[:, :, 65:97], dnb, ALU.mult, r=[pso_res[jh], "dn"], w=["tmpi"])
                kb.tt(imp, imp, tmpi, ALU.add, r=["imp", "tmpi"], w=["imp"], eng="pool")
            return
        if kind == "slc":
            j = it[4]
            v, vres = G["vs"], G["vsres"]
            for qt in range(4):
                T = 4 * qg + qt
                if j <= T:
                    kb.mm(pso_all[:, qt, :65], p[:, qt * 128:(qt + 1) * 128], v[:, j, :], j == 0, j == T, r=[ptres, vres], w=[pso_res[qt]])
            done = (j == 4 * qg + 3)
            branch = 1
        else:
            j, wk, r = it[4]
            v, vres = G["vw"], G["vwres"]
            uses = [qt for qt in range(4) if (qt <= r if wk == "lo" else qt >= r)]
            for qt in uses:
                first = (wk == "lo" and r == qt) or (wk == "hi" and r == 0 and qg == 0)
                last = (wk == "hi" and r == qt)
                kb.mm(pso_all[:, qt, :65], p[:, qt * 128:(qt + 1) * 128], v[:, j, :], first, last, r=[ptres, vres], w=[pso_res[qt]])
            done = (wk == "hi" and r == 3)
            branch = 2
        if done:
            evac(pso_all[:, :, 0:64], pso_all[:, :, 64], pso_res, g, qg, jh, branch, False)

    items = []
    prev_gq = None
    for g in range(4):
        for qg in range(4):
            for jh in range(4):
                items.append(("cmp", g, qg, jh))
            items.append(("sync", g, qg, 0, topk_dve))
            if prev_gq is not None:
                items.append(("sync", prev_gq[0], prev_gq[1], 0, writeout_pe))
            prev_gq = (g, qg)
            for jh in range(4):
                tiles = [(4 * qg - 4 + rp, "lo", rp) for rp in range(4) if qg > 0] + [(4 * qg + r, "hi", r) for r in range(4)]
                for t in tiles:
                    items.append(("win", g, qg, jh, t))
            for jh in range(4):
                for j in range(4 * qg + 4):
                    items.append(("slc", g, qg, jh, j))
            items.append(("sync", g, qg, 0, writeout))
    items.append(("sync", prev_gq[0], prev_gq[1], 0, writeout_pe))
    run_pipe(items, stage1, stage2, depth=2)
    kb.pop()


def ssd_phase(kb, C, l, SC):
    voff, vec, boff, bc = C["voff"], C["vec"], C["boff"], C["bc"]
    kb.push()
    BT = kb.sb("aBT", [128, 2, S], BF16)
    CT = kb.sb("aCT", [128, 2, S], BF16)
    kb.push()
    xrot = Rot(kb, "ax", 2, [128, S + 3], F32)
    for ap, res in xrot.t:
        kb.memset(ap[:, 0:3], 0.0, w=[res])
    xcr = Rot(kb, "axc", 2, [128, S], F32)
    ptr = Rot(kb, "aptr", 4, [128, 4, 128], F32, psum=True)
    st32 = Rot(kb, "ast32", 4, [128, 4, 128], F32)
    st16 = Rot(kb, "ast16", 4, [128, 4, 128], BF16)
    ev = [0]
    class _XP(Prefetch):
        def _issue(self):
            k = len(self.issued)
            if k < len(self.srcs):
                t, res = self.rot.next()
                self.kb.dma(self.q, t[:, 3:], self.srcs[k], w=[res], key=res)
                self.issued.append((t, res))
    x_pf = _XP(kb, xrot, [SC["xbcT"][c * 128:(c + 1) * 128, :] for c in range(12)], ahead=1)
    for c in range(12):
        x, xres = x_pf.get()
        xc, xcres = xcr.next()
        wcol = lambda k: vec[:, voff[("ssd_conv_w", l, k)] + c:voff[("ssd_conv_w", l, k)] + c + 1]
        bcol = vec[:, voff[("ssd_conv_b", l)] + c:voff[("ssd_conv_b", l)] + c + 1]
        kb.act(xc, x[:, 3:3 + S], AF.Identity, r=[xres, "consts"], w=[xcres], scale=wcol(3), bias=bcol)
        for k in range(3):
            kb.stt(xc, x[:, k:k + S], wcol(k), xc, ALU.mult, ALU.add, r=[xres, xcres, "consts"], w=[xcres])
        if c >= 8:
            dstT = BT if c < 10 else CT
            kb.act(dstT[:, c % 2, :], xc, AF.Silu, r=[xcres], w=["BCT%d" % c])
        kb.act(xc, xc, AF.Silu, r=[xcres], w=[xcres])
        if c < 10:
            for t4 in range(4):
                pt, ptres = ptr.next()
                for i in range(4):
                    tt_ = t4 * 4 + i
                    kb.tr(pt[:, i, :], xc[:, tt_ * 128:(tt_ + 1) * 128], C["ident_f"], r=[xcres, "consts"], w=[ptres])
                eng = "act"
                if c < 8:
                    st, sres = st32.next()
                    kb.copy(st, pt, r=[ptres], w=[sres], eng=eng)
                    kb.dma("sp", SC["xs"][t4 * 512:(t4 + 1) * 512, c * 128:(c + 1) * 128].rearrange("(i p) ch -> p i ch", p=128), st, r=[sres], key=sres)
                else:
                    st, sres = st16.next()
                    kb.copy(st, pt, r=[ptres], w=[sres], eng=eng)
                    kb.dma("sp", SC["Btok"][t4 * 512:(t4 + 1) * 512, (c - 8) * 128:(c - 7) * 128].rearrange("(i p) ch -> p i ch", p=128), st, r=[sres], key=sres)
    kb.pop()
    dts = kb.sb("adts", [128, 256], F32)
    adt = kb.sb("aadt", [128, 256], F32)
    acs = kb.sb("aacs", [128, 256], F32)
    eacs = kb.sb("aeacs", [128, 256], F32)
    dst_ = kb.sb("adst", [128, 256], F32)
    etot = kb.sb("aetot", [128, 256], F32)
    aexp = kb.sb("aaexp", [128, 256], F32)
    kb.push()
    ps1 = kb.psum("aps1", [128, 512], F32)
    ps2 = kb.psum("aps2", [128, 512], F32)
    kb.dma("sp", dts.rearrange("p (t h) -> p t h", h=16), SC["dt"].rearrange("(t p) h -> p t h", p=128), w=["dts"], key="dts")
    ob = boff[("ssd_dt_bias_rep", l)]
    kb.tt(dts, dts, bc[:, ob:ob + 256], ALU.add, r=["dts", "consts"], w=["dts"])
    kb.act(dts, dts, AF.Exp, r=["dts"], w=["dts"])
    kb.act(dts, dts, AF.Ln, r=["dts"], w=["dts"], bias=1.0)
    oa = boff[("ssd_a_log_rep", l)]
    kb.act(aexp, bc[:, oa:oa + 256], AF.Exp, r=["consts"], w=["aexp"])
    kb.stt(adt, dts, -1.0, aexp, ALU.mult, ALU.mult, r=["dts", "aexp"], w=["adt"])
    for t in range(16):
        kb.mm(ps1[:, t * 16:(t + 1) * 16], C["triu_f"], adt[:, t * 16:(t + 1) * 16], True, True, r=["adt", "consts"], w=["ps1"])
        kb.mm(ps2[:, t * 16:(t + 1) * 16], C["ones_f"], adt[:, t * 16:(t + 1) * 16], True, True, r=["adt", "consts"], w=["ps2"])
    kb.copy(acs, ps1[:, 0:256], r=["ps1"], w=["acs"])
    kb.act(eacs, acs, AF.Exp, r=["acs"], w=["eacs"])
    kb.tt(dst_, ps2[:, 0:256], acs, ALU.subtract, r=["ps2", "acs"], w=["dst"])
    kb.act(dst_, dst_, AF.Exp, r=["dst"], w=["dst"])
    kb.copy(etot, ps2[:, 0:256], r=["ps2"], w=["etot"])
    kb.act(etot, etot, AF.Exp, r=["etot"], w=["etot"])
    kb.pop()
    MTall = kb.sb("aMTall", [128, 16, 16, 128], BF16)
    kb.push()
    psg = Rot(kb, "apsg", 2, [128, 1024], F32, psum=True)
    pcb = Rot(kb, "apcb", 2, [128, 512], F32, psum=True)
    Xr = Rot(kb, "aX", 2, [128, 8, 128], F32)
    dr = Rot(kb, "ad", 2, [128, 8, 128], F32)
    cbr = Rot(kb, "acbm", 2, [128, 128], F32)
    triu_b8 = C["triu_f"].unsqueeze(1).broadcast_to([128, 8, 128])
    for t in range(16):
        tsl = slice(t * 128, (t + 1) * 128)
        for g in range(2):
            cols = slice(t * 16 + g * 8, t * 16 + g * 8 + 8)
            pc, pcres = pcb.next()
            kb.mm(pc[:, 0:128], BT[:, g, tsl], CT[:, g, tsl], True, True, r=["BCT%d" % (8 + g), "BCT%d" % (10 + g)], w=[pcres])
            cbm, cbres = cbr.next()
            kb.tt(cbm, pc[:, 0:128], C["triu_f"], ALU.mult, r=[pcres, "consts"], w=[cbres])
            xx, xxres = Xr.next()
            kb.tt(xx, triu_b8, adt[:, cols].unsqueeze(2).broadcast_to([128, 8, 128]), ALU.mult, r=["adt", "consts"], w=[xxres])
            pg, pgres = psg.next()
            for hf in range(2):
                kb.mm(pg[:, hf * 512:(hf + 1) * 512], C["ones_f"], xx[:, hf * 4:(hf + 1) * 4, :], True, True, r=[xxres, "consts"], w=[pgres])
            dd, ddres = dr.next()
            kb.tt(dd, pg.rearrange("p (h l) -> p h l", l=128), acs[:, cols].unsqueeze(2).broadcast_to([128, 8, 128]), ALU.subtract,
                  r=[pgres, "acs"], w=[ddres])
            kb.ts(dd, dd, 0.0, ALU.min, r=[ddres], w=[ddres])
            kb.act(dd, dd, AF.Exp, r=[ddres], w=[ddres])
            kb.tt(MTall[:, t, g * 8:(g + 1) * 8, :], dd, cbm.unsqueeze(1).broadcast_to([128, 8, 128]), ALU.mult,
                  r=[ddres, cbres], w=["MT%d" % t], eng="pool")
    kb.pop()
    yd = kb.psum("ayd", [128, 1024], F32)
    yo = kb.psum("ayo", [128, 1024], F32)
    psS = kb.psum("apsS", [128, 512], F32)
    pst = kb.psum("apst", [128, 1024], BF16)
    HT = kb.sb("aHT", [128, 1024], F32)
    HTb = kb.sb("aHTb", [128, 1024], BF16)
    xsr = Rot(kb, "axs", 3, [128, 1024], F32)
    zsr = Rot(kb, "azs", 3, [128, 1024], F32)
    btr = Rot(kb, "abt", 3, [128, 256], BF16)
    xdtr = Rot(kb, "axdt", 2, [128, 16, 64], BF16)
    xdtpr = Rot(kb, "axdtp", 2, [128, 16, 64], BF16)
    yoff = kb.sb("ayoff", [128, 1024], F32)
    yr = Rot(kb, "ay", 2, [128, 1024], F32)
    tmpr = Rot(kb, "atmp", 2, [128, 1024], F32)
    junk = kb.sb("ajunk", [128, 1024], F32)
    obr = Rot(kb, "aob", 2, [128, 1024], BF16)
    smr = Rot(kb, "asm", 2, [128, 4], F32)
    stg = Rot(kb, "astg", 2, [128, 8, 512], BF16)
    od = boff[("ssd_d_rep", l)]
    on = boff[("ssd_norm", l)]
    stg_cur = None
    ssd_deferred = []
    xs_pf = Prefetch(kb, xsr, [SC["xs"][t * 128:(t + 1) * 128, :] for t in range(16)], ahead=1)
    zs_pf = Prefetch(kb, zsr, [SC["zs"][t * 128:(t + 1) * 128, :] for t in range(16)], ahead=1)
    bt_pf = Prefetch(kb, btr, [SC["Btok"][t * 128:(t + 1) * 128, :] for t in range(16)], ahead=1)
    for t in range(16):
        tsl = slice(t * 128, (t + 1) * 128)
        xs, xsres = xs_pf.get()
        zs, zsres = zs_pf.get()
        bt, btres = bt_pf.get()
        xs3 = xs.rearrange("p (h d) -> p h d", d=64)
        bc16 = lambda tab: tab[:, t * 16:(t + 1) * 16].unsqueeze(2).broadcast_to([128, 16, 64])
        xdt, xdres = xdtr.next()
        xdtp, xpres = xdtpr.next()
        kb.tt(xdt, xs3, bc16(dts), ALU.mult, r=[xsres, "dts"], w=[xdres])
        kb.tt(xdtp, xdt, bc16(dst_), ALU.mult, r=[xdres, "dst"], w=[xpres], eng="pool")
        tmp, tmpres = tmpr.next()
        kb.tt(tmp, xs, bc[:, od:od + 1024], ALU.mult, r=[xsres, "consts"], w=[tmpres], eng="pool")
        for h in range(16):
            kb.mm(yd[:, h * 64:(h + 1) * 64], MTall[:, t, h, :], xdt[:, h, :], True, True, r=["MT%d" % t, xdres], w=["yd%d" % (h // 8)])
        y, yres = yr.next()
        if t > 0:
            for g in range(2):
                kb.mm(yo[:, g * 512:(g + 1) * 512], CT[:, g, tsl], HTb[:, g * 512:(g + 1) * 512], True, True,
                      r=["BCT%d" % (10 + g), "HTb%d" % g], w=["yo%d" % g])
            kb.tt(yoff.rearrange("p (h d) -> p h d", d=64), yo.rearrange("p (h d) -> p h d", d=64), bc16(eacs), ALU.mult,
                  r=["yo0", "yo1", "eacs"], w=["yoff"])
            kb.tt(y, yd, yoff, ALU.add, r=["yd0", "yd1", "yoff"], w=[yres])
        else:
            kb.copy(y, yd, r=["yd0", "yd1"], w=[yres])
        if t < 15:
            for g in range(2):
                kb.mm(psS, bt[:, g * 128:(g + 1) * 128], xdtp[:, g * 8:(g + 1) * 8, :], True, True, r=[btres, xpres], w=["psS"])
                hsl = slice(g * 512, (g + 1) * 512)
                if t == 0:
                    kb.copy(HT[:, hsl], psS, r=["psS"], w=["HT%d" % g])
                else:
                    et = etot[:, t * 16 + g * 8:t * 16 + g * 8 + 8].unsqueeze(2).broadcast_to([128, 8, 64])
                    kb.tt(HT[:, hsl].rearrange("p (h d) -> p h d", d=64), HT[:, hsl].rearrange("p (h d) -> p h d", d=64), et, ALU.mult,
                          r=["HT%d" % g, "etot"], w=["HT%d" % g])
                    kb.tt(HT[:, hsl], HT[:, hsl], psS, ALU.add, r=["HT%d" % g, "psS"], w=["HT%d" % g])
                kb.copy(HTb[:, hsl], HT[:, hsl], r=["HT%d" % g], w=["HTb%d" % g], eng="act")
        while ssd_deferred:
            ssd_deferred.pop(0)()
        sm, smres = smr.next()
        kb.tt(y, y, tmp, ALU.add, r=[yres, tmpres], w=[yres])
        kb.tt(y, y, zs, ALU.mult, r=[yres, zsres], w=[yres])
        kb.act(junk, y, AF.Square, r=[yres], w=["junk", smres + "ss"], accum_out=sm[:, 0:1])
        rms_rstd(kb, sm[:, 1:2], sm[:, 0:1], 1024, [smres + "ss"], [smres + "rstd"])
        ob16, obres = obr.next()
        kb.stt(ob16, y, sm[:, 1:2], bc[:, on:on + 1024], ALU.mult, ALU.mult, r=[yres, smres + "rstd", "consts"], w=[obres])
        if t % 4 == 0:
            stg_cur = stg.next()

        def pe_tail(t=t, ob16=ob16, obres=obres, stg_cur=stg_cur):
            for c in range(8):
                kb.tr(pst[:, c * 128:(c + 1) * 128], ob16[:, c * 128:(c + 1) * 128], C["ident_b"], r=[obres, "consts2"], w=["pst"])
            sg_, sgres_ = stg_cur
            kb.copy(sg_[:, :, (t % 4) * 128:(t % 4 + 1) * 128], pst.rearrange("p (c q) -> p c q", q=128), r=["pst"], w=[sgres_], eng="act")
            if t % 4 == 3:
                t4 = t // 4
                kb.dma("sp", SC["oT"][0].rearrange("(c p) t -> p c t", p=128)[:, :, t4 * 512:(t4 + 1) * 512], sg_, r=[sgres_], key=sgres_)
        ssd_deferred.append(pe_tail)
    while ssd_deferred:
        ssd_deferred.pop(0)()
    kb.pop()


def load_actT(kb, dst, dram, KC, res_fn, q="sp", T=S):
    v = dram.rearrange("(c p) t -> p c t", p=128)
    for c in range(KC):
        kb.dma(q, dst[:, c, :T], v[:, c, :], w=[res_fn(c, tg) for tg in range(4)], key="ld_%s_%d" % (res_fn(c, 0), c))


def gates_phase(kb, C, l, Wd, SC, hT, h_res):
    kb.push()
    wrot = Rot(kb, "gw", 2, [128, 16, 512], BF16)
    psrot = Rot(kb, "gps", 4, [128, 512], F32, psum=True)
    st16 = Rot(kb, "gst", 3, [128, 512], BF16)
    for n in range(4):
        def epi(f0, tg, ps, pres, n=n):
            st, sres = st16.next()
            kb.act(st, ps, AF.Sigmoid, r=[pres], w=[sres])
            kb.dma("sp", SC["gT"][n, f0:f0 + 128, tg * 512:(tg + 1) * 512], st, r=[sres], key=sres)
        gemm(kb, Wd["w_merge_gate"][l, n], D, D, hT, h_res, "feat", epi, wrot, psrot)
    kb.pop()


def merge_phase(kb, C, l, Wd, SC):
    kb.push()
    GC = 256
    wbrot = Rot(kb, "mwb", 3, [128, 8, GC], BF16)
    psb = Rot(kb, "mpb", 6, [128, 512], F32, psum=True)
    oT = kb.sb("moT", [128, 4, 8, S], BF16)
    for n in range(4):
        kb.dma("sp", oT[:, n], SC["oT"][n].rearrange("(c p) t -> p c t", p=128), w=["moT%d" % n], key="moT%d" % n)
    gtr = Rot(kb, "mgt", 4, [128, S], BF16)
    gpf = Prefetch(kb, gtr, [SC["gT"][n, g * GC + fc * 128:g * GC + fc * 128 + 128, :]
                             for g in range(D // GC) for n in range(4) for fc in range(GC // 128)], ahead=2)
    tmr = Rot(kb, "mtm", 6, [128, 512], F32)
    acc = kb.sb("macc", [128, GC // 128, S], F32)
    str_ = Rot(kb, "mst", 3, [128, 512], BF16)
    wpf = Prefetch(kb, wbrot, [Wd["w_branch"][l, n][:, g * GC:(g + 1) * GC].rearrange("(c p) n -> p c n", p=128)
                               for g in range(D // GC) for n in range(4)], q="pool", ahead=1)
    for g in range(D // GC):
        for n in range(4):
            wb, wbres = wpf.get()
            for fc in range(GC // 128):
                f0 = g * GC + fc * 128
                gtf, gtres = gpf.get()
                for tg in range(4):
                    gt = gtf[:, tg * 512:(tg + 1) * 512]
                    pb, pbres = psb.next()
                    for kc in range(8):
                        kb.mm(pb, wb[:, kc, fc * 128:(fc + 1) * 128], oT[:, n, kc, tg * 512:(tg + 1) * 512], kc == 0, kc == 7,
                              r=[wbres, "moT%d" % n], w=[pbres])
                    ares = "macc#%d_%d" % (fc, tg)
                    asl = acc[:, fc, tg * 512:(tg + 1) * 512]
                    if n == 0:
                        kb.tt(asl, gt, pb, ALU.mult, r=[gtres, pbres], w=[ares])
                    else:
                        tm, tmres = tmr.next()
                        kb.tt(tm, gt, pb, ALU.mult, r=[gtres, pbres], w=[tmres])
                        aeng = "pool" if (fc * 4 + tg) % 2 else "dve"
                        if n < 3:
                            kb.tt(asl, asl, tm, ALU.add, r=[ares, tmres], w=[ares], eng=aeng)
                        else:
                            st, sres = str_.next()
                            kb.tt(st, asl, tm, ALU.add, r=[ares, tmres], w=[sres], eng=aeng)
                            kb.dma("sp", SC["mT"][f0:f0 + 128, tg * 512:(tg + 1) * 512], st, r=[sres], key=sres)
    kb.pop()


def resid_gemm_phase(kb, C, W, K, actT, act_res, x_src, x_dst, tag):
    kb.push()
    wrot = Rot(kb, tag + "w", 2, [128, K // 128, 512], BF16)
    psrot = Rot(kb, tag + "ps", 4, [128, 512], F32, psum=True)
    xr = Rot(kb, tag + "x", 5, [128, 512], F32)
    pf = Prefetch(kb, xr, [x_src[f0:f0 + 128, tg * 512:(tg + 1) * 512] for f0 in range(0, D, 128) for tg in range(4)])

    def epi(f0, tg, ps, pres):
        xt, xres = pf.get()
        kb.tt(xt, xt, ps, ALU.add, r=[xres, pres], w=[xres])
        kb.dma("sp", x_dst[f0:f0 + 128, tg * 512:(tg + 1) * 512], xt, r=[xres], key=xres)
    gemm(kb, W, K, D, actT, act_res, "feat", epi, wrot, psrot)
    kb.pop()


def ffn_up_phase(kb, C, l, Wd, SC, hT, h_res):
    voff, vec = C["voff"], C["vec"]
    kb.push()
    wrot = Rot(kb, "fw", FFN_WSLOTS, [128, 16, 512], BF16)
    psrot = Rot(kb, "fps", 6, [128, 512], F32, psum=True)
    urot = Rot(kb, "fu", 2, [128, S + 2], F32)
    for ap, res in urot.t:
        kb.memset(ap[:, 0:2], 0.0, w=[res])
    crot = Rot(kb, "fc", 2, [128, S], F32)
    vrot = Rot(kb, "fv", 2, [128, S], F32)
    arot = Rot(kb, "fa", 2, [128, S], BF16)
    Wup = Wd["ffn_w_up"][l]
    ev = [0]
    def issue_group(i0):
        wts = []
        for half in range(2):
            wt, wres = wrot.next()
            c0 = half * D_FF + i0 * 128
            wr = load_w(kb, wt, wres, Wup[:, c0:c0 + 512].rearrange("(c p) n -> p c n", p=128), 16, 512)
            wts.append((wt, wr))
        return wts
    nxt = issue_group(0)
    for i0 in range(0, 48, 4):
        wts = nxt
        if i0 + 4 < 48:
            nxt = issue_group(i0 + 4)
        for fc in range(4):
            i = i0 + fc
            conv = []
            for half in range(2):
                wt, wres = wts[half]
                u, ures = urot.next()
                for tg in range(4):
                    ps, pres = psrot.next()
                    for kc in range(16):
                        kb.mm(ps, wt[:, kc, fc * 128:(fc + 1) * 128], hT[:, kc, tg * 512:(tg + 1) * 512], kc == 0, kc == 15,
                              r=[wres(kc), h_res(kc, tg)], w=[pres])
                    kb.copy(u[:, 2 + tg * 512:2 + (tg + 1) * 512], ps, r=[pres], w=[ures], eng="act")
                ch = half * 48 + i
                wc = lambda k: vec[:, voff[("ffn_conv_w", l, k)] + ch:voff[("ffn_conv_w", l, k)] + ch + 1]
                bc = vec[:, voff[("ffn_conv_b", l)] + ch:voff[("ffn_conv_b", l)] + ch + 1]
                cv, cres = (crot if half == 0 else vrot).next()
                kb.ts(cv, u[:, 2:2 + S], wc(2), ALU.mult, bc, ALU.add, r=[ures, "consts"], w=[cres])
                kb.stt(cv, u[:, 1:1 + S], wc(1), cv, ALU.mult, ALU.add, r=[ures, cres, "consts"], w=[cres])
                kb.stt(cv, u[:, 0:S], wc(0), cv, ALU.mult, ALU.add, r=[ures, cres, "consts"], w=[cres])
                conv.append((cv, cres))
            (cg, cgres), (cvv, cvres) = conv
            kb.act(cg, cg, AF.Gelu_apprx_tanh, r=[cgres], w=[cgres])
            a, ares = arot.next()
            kb.tt(a, cg, cvv, ALU.mult, r=[cgres, cvres], w=[ares], eng="pool")
            kb.dma("sp", SC["aT"][i * 128:(i + 1) * 128, :], a, r=[ares], key=ares)
    kb.pop()


def ffn_down_phase(kb, C, l, Wd, SC):
    kb.push()
    GC = 256
    TH = 1024
    NQ = TH // 512
    wrot = Rot(kb, "gw", 2, [128, 48, GC], BF16)
    psrot = Rot(kb, "gps", 4, [128, 512], F32, psum=True)
    at = kb.sb("ga", [128, 48, TH], BF16)
    xr = Rot(kb, "gx", 5, [128, 512], F32)
    Wdn = Wd["ffn_w_down"][l]
    xT = SC["xT"]
    av = SC["aT"].rearrange("(c p) t -> p c t", p=128)
    pf = Prefetch(kb, xr, [xT[g * GC + fc * 128:g * GC + fc * 128 + 128, th * TH + tq * 512:th * TH + tq * 512 + 512]
                           for th in range(S // TH) for g in range(D // GC) for tq in range(NQ) for fc in range(GC // 128)])
    for th in range(S // TH):
        for tq in range(NQ):
            for c6 in range(6):
                t0 = th * TH + tq * 512
                kb.dma("sp", at[:, c6 * 8:(c6 + 1) * 8, tq * 512:(tq + 1) * 512], av[:, c6 * 8:(c6 + 1) * 8, t0:t0 + 512],
                       w=["ga_%d_%d" % (c6, tq)], key="ga_%d_%d" % (c6, tq))
        for g in range(D // GC):
            wt, wres = wrot.next()
            wr = load_w(kb, wt, wres, Wdn[:, g * GC:(g + 1) * GC].rearrange("(c p) n -> p c n", p=128), 48, GC, nsplit=6)
            for tq in range(NQ):
                for fc in range(GC // 128):
                    ps, pres = psrot.next()
                    for kc in range(48):
                        kb.mm(ps, wt[:, kc, fc * 128:(fc + 1) * 128], at[:, kc, tq * 512:(tq + 1) * 512], kc == 0, kc == 47,
                              r=[wr(kc), "ga_%d_%d" % (kc // 8, tq)], w=[pres])
                    f0 = g * GC + fc * 128
                    t0 = th * TH + tq * 512
                    xt, xres = pf.get()
                    kb.tt(xt, xt, ps, ALU.add, r=[xres, pres], w=[xres])
                    kb.dma("sp", xT[f0:f0 + 128, t0:t0 + 512], xt, r=[xres], key=xres)
    kb.pop()


def ple_phase(kb, C, l, Wd, SC, hT, h_res, pT_in):
    kb.push()
    wgrot = Rot(kb, "ewg", 2, [128, 16, 512], BF16)
    wprot = Rot(kb, "ewp", 2, [128, 2, 512], BF16)
    psg = Rot(kb, "epg", 3, [128, 512], F32, psum=True)
    psp = Rot(kb, "epp", 3, [128, 512], F32, psum=True)
    pT = kb.sb("epT", [128, 2, S], BF16)
    kb.dma("pool", pT, pT_in[l].rearrange("(c p) t -> p c t", p=128), w=["pT"], key="pT")
    sgr = Rot(kb, "esg", 2, [128, 512], F32)
    xr = Rot(kb, "ex", 5, [128, 512], F32)
    xT = SC["xT"]
    pf = Prefetch(kb, xr, [xT[g * 512 + fc * 128:g * 512 + fc * 128 + 128, tg * 512:(tg + 1) * 512]
                           for g in range(4) for fc in range(4) for tg in range(4)])
    def issue_g(g):
        wg, wgres = wgrot.next()
        wgr = load_w(kb, wg, wgres, Wd["ple_w_gate"][l][:, g * 512:(g + 1) * 512].rearrange("(c p) n -> p c n", p=128), 16, 512)
        wp, wpres = wprot.next()
        kb.dma("pool", wp, Wd["ple_w_proj"][l][:, g * 512:(g + 1) * 512].rearrange("(c p) n -> p c n", p=128), w=[wpres], key=wpres)
        return wg, wgr, wp, wpres
    nxt = issue_g(0)
    for g in range(4):
        wg, wgr, wp, wpres = nxt
        if g + 1 < 4:
            nxt = issue_g(g + 1)
        for fc in range(4):
            for tg in range(4):
                pg, pgres = psg.next()
                for kc in range(16):
                    kb.mm(pg, wg[:, kc, fc * 128:(fc + 1) * 128], hT[:, kc, tg * 512:(tg + 1) * 512], kc == 0, kc == 15,
                          r=[wgr(kc), h_res(kc, tg)], w=[pgres])
                pp, ppres = psp.next()
                for kc in range(2):
                    kb.mm(pp, wp[:, kc, fc * 128:(fc + 1) * 128], pT[:, kc, tg * 512:(tg + 1) * 512], kc == 0, kc == 1,
                          r=[wpres, "pT"], w=[ppres])
                sg, sgres = sgr.next()
                kb.act(sg, pg, AF.Sigmoid, r=[pgres], w=[sgres])
                kb.tt(sg, sg, pp, ALU.mult, r=[sgres, ppres], w=[sgres])
                f0 = g * 512 + fc * 128
                xt, xres = pf.get()
                kb.tt(xt, xt, sg, ALU.add, r=[xres, sgres], w=[xres], eng=("pool" if tg % 2 else "dve"))
                kb.dma("sp", xT[f0:f0 + 128, tg * 512:(tg + 1) * 512], xt, r=[xres], key=xres)
    kb.pop()


WEIGHT_NAMES = ("w_in", "nsa_ck_w1", "nsa_ck_w2", "nsa_cv_w1", "nsa_cv_w2", "rnn_w_r", "rnn_w_i",
                "w_merge_gate", "w_branch", "w_out", "ffn_w_up", "ffn_w_down", "ple_w_proj", "ple_w_gate")
WEIGHT_SHAPES = {
    "w_in": [DEPTH, D, D_IN], "nsa_ck_w1": [DEPTH, 2048, 256], "nsa_ck_w2": [DEPTH, 256, 64],
    "nsa_cv_w1": [DEPTH, 2048, 256], "nsa_cv_w2": [DEPTH, 256, 64], "rnn_w_r": [DEPTH, 16, 64, 64],
    "rnn_w_i": [DEPTH, 16, 64, 64], "nsa_pos_cmp": [DEPTH, 32, 64],
    "w_merge_gate": [DEPTH, 4, D, D], "w_branch": [DEPTH, 4, 1024, D], "w_out": [DEPTH, D, D],
    "ffn_w_up": [DEPTH, D, 2 * D_FF], "ffn_w_down": [DEPTH, D_FF, D], "ple_w_proj": [DEPTH, PLE, D],
    "ple_w_gate": [DEPTH, D, D],
}


def build(stage="full", debug=(), nlayers=DEPTH, inject=(), skip=()):
    kb = KB(debug)
    kb.inject = set(inject)
    kb.fin = []
    kb.push()
    xT_in = kb.din("xT_in", [D, S], F32)
    pT_in = kb.din("pT_in", [DEPTH, PLE, S], F32)
    Wd = {n: kb.din(n, WEIGHT_SHAPES[n], F32) for n in WEIGHT_NAMES}
    outT = kb.dout("outT", [D, S], F32)
    C = load_consts(kb)
    SC = alloc_scratch(kb)
    kb.P.barrier()
    voff = C["voff"]
    h_res = lambda c, tg: "hT#%d_%d" % (c, tg)

    def normed(x_src, gcol):
        kb.push()
        hT = kb.sb("hT", [128, 16, S], BF16)
        kb.push()
        nps = Rot(kb, "nps", 2, [128, 512], F32, psum=True)
        norm_phase(kb, C, x_src, gcol, hT, h_res, nps)
        kb.pop()
        return hT

    for l in range(nlayers):
        x_src = xT_in if l == 0 else SC["xT"]
        kb.push()
        bct = kb.sb("bc", [128, C["nb"]], F32)
        kb.dma("sp", bct, C["bc_d"][l], w=["consts"], key="bc")
        C["bc"] = bct
        hT = normed(x_src, voff[("norm_mix", l)])
        if "P" not in skip:
            proj_phase(kb, C, l, Wd["w_in"], hT, h_res, SC)
        if stage != "P" and "G" not in skip:
            gates_phase(kb, C, l, Wd, SC, hT, h_res)
        kb.pop()
        if stage == "P":
            break
        if "D" not in skip:
            rglru_phase(kb, C, l, Wd, SC)
        if stage == "D":
            break
        if "B" not in skip:
            diff_phase(kb, C, l, SC)
        if stage == "B":
            break
        if "C" not in skip:
            nsa_phase(kb, C, l, Wd, SC, C["posT_d"])
        if stage == "C":
            break
        if "A" not in skip:
            ssd_phase(kb, C, l, SC)
        if stage == "A":
            break
        kb.pop()
        merge_phase(kb, C, l, Wd, SC)
        kb.push()
        hT = kb.sb("hT", [128, 16, S], BF16)
        load_actT(kb, hT, SC["mT"], 16, h_res)
        resid_gemm_phase(kb, C, Wd["w_out"][l], D, hT, h_res, x_src, SC["xT"], "o")
        kb.pop()
        hT = normed(SC["xT"], voff[("norm_ffn", l)])
        ffn_up_phase(kb, C, l, Wd, SC, hT, h_res)
        kb.pop()
        ffn_down_phase(kb, C, l, Wd, SC)
        hT = normed(SC["xT"], voff[("norm_ple", l)])
        ple_phase(kb, C, l, Wd, SC, hT, h_res, pT_in)
        kb.pop()
    if stage in ("full", "rest"):
        kb.push()
        nps = Rot(kb, "nps", 2, [128, 512], F32, psum=True)
        norm_phase(kb, C, SC["xT"], voff[("norm_final", 0)], None, h_res, nps, out_f32=outT)
        kb.pop()
    else:
        kb.push()
        t = kb.sb("dummy", [128, 512], F32)
        kb.memset(t, 0.0, w=["dummy"])
        kb.fin.append(kb.dma("sp", outT[0:128, 0:512], t, r=["dummy"], key="dummy"))
        kb.pop()
    kb.pop()
    kb.P.emit(final_waits=kb.fin)
    return kb


def make_in_maps(inputs, kb):
    hc = host_consts()
    vec = pack_vec(inputs)
    bc = pack_bc(inputs)
    shared = {"c_ones_f": hc["ones_f"], "c_ident_f": hc["ident_f"], "c_pswap": hc["pswap"], "c_triu_f": hc["triu_f"],
              "c_vec": vec, "c_bc": bc, "c_rope": hc["rope"], "c_tri": hc["tri"], "c_wlo": hc["wlo"],
              "c_forced": hc["forced"], "c_future": hc["future"], "c_ovl": hc["ovl"], "c_cmp_pen": hc["cmp_pen"], "c_esel": hc["esel"],
              "c_posT": np.ascontiguousarray(np.asarray(inputs["nsa_pos_cmp"], np.float32).transpose(0, 2, 1))}
    for n in WEIGHT_NAMES:
        shared[n] = np.ascontiguousarray(np.asarray(inputs[n], np.float32))
    for k in list(shared):
        if k not in kb.ins:
            del shared[k]
    maps = []
    x = np.asarray(inputs["x"], np.float32)
    p = np.asarray(inputs["p"], np.float32)
    for b in range(x.shape[0]):
        m = dict(shared)
        m["xT_in"] = np.ascontiguousarray(x[b].T)
        m["pT_in"] = np.ascontiguousarray(p[:, b].transpose(0, 2, 1))
        maps.append(m)
    return maps


def kernel(**inputs):
    kb = build("full")
    maps = make_in_maps(inputs, kb)
    res = run_bass_kernel_spmd(kb.nc, maps, core_ids=list(range(8)))
    out = np.stack([np.ascontiguousarray(r["outT"].T) for r in res.results], 0)
    return out.astype(np.float32)
```

```python
import math
import numpy as np
import ml_dtypes
import concourse.bass as bass
import concourse.mybir as mybir
from concourse.bass_utils import run_bass_kernel_spmd

F32 = mybir.dt.float32
BF16 = mybir.dt.bfloat16
AF = mybir.ActivationFunctionType
ALU = mybir.AluOpType
AX = mybir.AxisListType

ENGS = ("pe", "act", "dve", "pool", "sp")
SEM_EPOCH = 30000

D = 2048
S = 2048
DEPTH = 2
D_IN = 10304
D_FF = 6144
PLE = 256
EPS = 1e-6
NEG = -30000.0
PLAN_ONLY = None
FFN_WSLOTS = 4
ROPE_ADD_ENG = "dve"


class Prog:
    def __init__(self, nc):
        self.nc = nc
        self.ops = []
        self.last_w = {}
        self.readers = {}
        self.dma_last = {}
        self.dma_since = []
        self.last_on = {}
        self.epoch_op = None

    def add(self, eng, fn, reads=(), writes=(), dma=None):
        deps = set()
        if self.epoch_op is not None:
            deps.add(self.epoch_op)
        for r in reads:
            if r in self.last_w:
                deps.add(self.last_w[r])
        for w in writes:
            if w in self.last_w:
                deps.add(self.last_w[w])
            for x in self.readers.get(w, ()):
                deps.add(x)
        idx = len(self.ops)
        if dma is not None:
            if dma in self.dma_last:
                deps.add(self.dma_last[dma])
            self.dma_last[dma] = idx
            self.dma_since.append(idx)
        else:
            self.last_on[eng] = idx
        self.ops.append(dict(eng=eng, fn=fn, deps=deps, dma=dma))
        for r in reads:
            self.readers.setdefault(r, []).append(idx)
        for w in writes:
            self.last_w[w] = idx
            self.readers[w] = []
        return idx

    def barrier(self):
        deps = set(self.last_on.values()) | set(self.dma_since)
        if self.epoch_op is not None:
            deps.add(self.epoch_op)
        idx = len(self.ops)
        self.ops.append(dict(eng="sp", fn=lambda e: e.nop(), deps=deps, dma=None, barrier=True))
        self.last_on["sp"] = idx
        self.dma_since = []
        self.epoch_op = idx
        return idx

    def emit(self, final_waits=()):
        nc = self.nc
        ops = self.ops
        n = len(ops)
        needed = [False] * n
        for i, o in enumerate(ops):
            pruned = set()
            for d in o["deps"]:
                od = ops[d]
                if od["dma"] is None and od["eng"] == "pe" and o["eng"] == "pe" and o["dma"] is None:
                    continue
                pruned.add(d)
            o["deps"] = pruned
            for d in pruned:
                needed[d] = True
        for i in final_waits:
            needed[i] = True
        sem_objs = {}
        ctxs = []

        def get_sem(key):
            if key not in sem_objs:
                c = nc.semaphore("s_%d" % len(sem_objs))
                s = c.__enter__()
                ctxs.append(c)
                sem_objs[key] = s
            return sem_objs[key]

        last_use = {}
        for i, o in enumerate(ops):
            if o["dma"] is not None:
                last_use[o["dma"]] = i
        free_slots = []
        key2slot = {}
        nslots = [0]
        cnt = {}
        for i, o in enumerate(ops):
            if o.get("barrier"):
                for k in [k for k in key2slot if last_use[k] < i]:
                    free_slots.append(key2slot.pop(k))
            if not needed[i]:
                o["sig"] = None
                continue
            if o["dma"] is not None:
                k = o["dma"]
                if k not in key2slot:
                    fs = [x for x in free_slots if x[2] == o["eng"]]
                    if fs:
                        free_slots.remove(fs[-1])
                        key2slot[k] = fs[-1]
                    else:
                        key2slot[k] = ("dmaslot", nslots[0], o["eng"])
                        nslots[0] += 1
                key = key2slot[k]
                c = cnt.get(key, 0) + 16
                cnt[key] = c
                o["sig"] = (key, get_sem(key), 16, c)
            else:
                base = ("eng", o["eng"])
                ep = cnt.get((base, "ep"), 0)
                c = cnt.get((base, ep), 0) + 1
                if c > SEM_EPOCH:
                    ep += 1
                    cnt[(base, "ep")] = ep
                    c = 1
                cnt[(base, ep)] = c
                o["sig"] = ((base, ep), get_sem((base, ep)), 1, c)
        self.n_sems = len(sem_objs)
        by_eng = {e: [] for e in ENGS}
        for i, o in enumerate(ops):
            by_eng[o["eng"]].append(i)
        with nc.Block() as block:
            def make(engname):
                def body(eng):
                    waited = {}
                    for i in by_eng[engname]:
                        o = ops[i]
                        need = {}
                        for d in o["deps"]:
                            k, s, _, v = ops[d]["sig"]
                            if v > need.get(k, (None, 0))[1]:
                                need[k] = (s, v)
                        for k, (s, v) in need.items():
                            if waited.get(k, 0) >= v:
                                continue
                            waited[k] = v
                            eng.wait_ge(s, v)
                        ins = o["fn"](eng)
                        if o["sig"] is not None:
                            _, s, inc, v = o["sig"]
                            ins.then_inc(s, inc)
                    if engname == "sp":
                        for i in final_waits:
                            _, s, _, v = ops[i]["sig"]
                            eng.wait_ge(s, v)
                return body
            block.tensor(make("pe"))
            block.scalar(make("act"))
            block.vector(make("dve"))
            block.gpsimd(make("pool"))
            block.sync(make("sp"))
        for c in reversed(ctxs):
            c.__exit__(None, None, None)


class KB:
    def __init__(self, debug=()):
        self.nc = bass.Bass("TRN2", target_bir_lowering=False)
        self.P = Prog(self.nc)
        self.debug = set(debug)
        self.ins = {}
        self.outs = {}
        self.n = 0
        self.stack = []

    def din(self, name, shape, dt=F32):
        ap = self.nc.dram_tensor(name, list(shape), dt, kind="ExternalInput").ap()
        self.ins[name] = ap
        return ap

    def dout(self, name, shape, dt=F32):
        ap = self.nc.dram_tensor(name, list(shape), dt, kind="ExternalOutput").ap()
        self.outs[name] = ap
        return ap

    def dtmp(self, name, shape, dt=F32):
        if name in getattr(self, "inject", ()):
            return self.din(name, shape, dt)
        if name in self.debug:
            return self.dout(name, shape, dt)
        return self.nc.dram_tensor(name, list(shape), dt, kind="Internal").ap()

    def push(self):
        self.stack.append([])

    def pop(self):
        for c in reversed(self.stack.pop()):
            c.__exit__(None, None, None)
        self.P.barrier()

    def sb(self, name, shape, dt=F32):
        self.n += 1
        c = self.nc.sbuf_tensor("%s_%d" % (name, self.n), list(shape), dt)
        t = c.__enter__()
        self.stack[-1].append(c)
        return t.ap()

    def psum(self, name, shape, dt=F32):
        self.n += 1
        c = self.nc.psum_tensor("%s_%d" % (name, self.n), list(shape), dt)
        t = c.__enter__()
        self.stack[-1].append(c)
        return t.ap()

    def dma(self, q, out, in_, r=(), w=(), key=None):
        assert key is not None
        return self.P.add(q, lambda e: e.dma_start(out=out, in_=in_), reads=r, writes=w, dma=key)

    def mm(self, out, lhsT, rhs, start, stop, r=(), w=()):
        return self.P.add("pe", lambda e: e.matmul(out, lhsT=lhsT, rhs=rhs, start=start, stop=stop), reads=r, writes=w)

    def tr(self, out, in_, ident, r=(), w=()):
        return self.P.add("pe", lambda e: e.transpose(out, in_, ident), reads=r, writes=w)

    def act(self, out, in_, func, r=(), w=(), bias=None, scale=1.0, accum_out=None):
        def fn(e):
            kw = {}
            if bias is not None:
                kw["bias"] = bias
            if accum_out is not None:
                kw["accum_out"] = accum_out
            return e.activation(out=out, in_=in_, func=func, scale=scale, **kw)
        return self.P.add("act", fn, reads=r, writes=w)

    def tt(self, out, in0, in1, op, r=(), w=(), eng="dve"):
        return self.P.add(eng, lambda e: e.tensor_tensor(out=out, in0=in0, in1=in1, op=op), reads=r, writes=w)

    def ts(self, out, in0, s1, op0, s2=None, op1=None, r=(), w=(), eng="dve", accum_out=None):
        def fn(e):
            kw = {}
            if op1 is not None:
                kw["op1"] = op1
            if accum_out is not None:
                kw["accum_out"] = accum_out
            return e.tensor_scalar(out=out, in0=in0, scalar1=s1, scalar2=s2, op0=op0, **kw)
        return self.P.add(eng, fn, reads=r, writes=w)

    def stt(self, out, in0, scalar, in1, op0, op1, r=(), w=()):
        return self.P.add("dve", lambda e: e.scalar_tensor_tensor(out=out, in0=in0, scalar=scalar, in1=in1, op0=op0, op1=op1), reads=r, writes=w)

    def copy(self, out, in_, r=(), w=(), eng="dve"):
        if eng == "act":
            return self.act(out, in_, AF.Copy, r=r, w=w)
        return self.P.add(eng, lambda e: e.tensor_copy(out=out, in_=in_), reads=r, writes=w)

    def memset(self, ap, val, w=(), eng="dve"):
        return self.P.add(eng, lambda e: e.memset(ap, val), writes=w)

    def recip(self, out, in_, r=(), w=()):
        return self.P.add("dve", lambda e: e.reciprocal(out=out, in_=in_), reads=r, writes=w)


class Rot:
    def __init__(self, kb, name, n, shape, dt=F32, psum=False):
        self.t = []
        for i in range(n):
            ap = kb.psum(name, shape, dt) if psum else kb.sb(name, shape, dt)
            self.t.append((ap, "%s#%d_%d" % (name, i, kb.n)))
        self.i = 0

    def next(self):
        x = self.t[self.i % len(self.t)]
        self.i += 1
        return x


class Prefetch:
    def __init__(self, kb, rot, srcs, q="sp", ahead=2):
        self.kb, self.rot, self.srcs, self.q, self.ahead = kb, rot, srcs, q, ahead
        self.issued = []
        self.i = 0

    def _issue(self):
        k = len(self.issued)
        if k < len(self.srcs):
            t, res = self.rot.next()
            self.kb.dma(self.q, t, self.srcs[k], w=[res], key=res)
            self.issued.append((t, res))

    def get(self):
        while len(self.issued) < min(len(self.srcs), self.i + 1 + self.ahead):
            self._issue()
        x = self.issued[self.i]
        self.i += 1
        return x


def load_w(kb, wt, wres, src, KC, ncol, nsplit=4):
    nsplit = max(1, min(nsplit, KC))
    step = (KC + nsplit - 1) // nsplit
    for i, k0 in enumerate(range(0, KC, step)):
        k1 = min(KC, k0 + step)
        kb.dma("pool", wt[:, k0:k1, :ncol], src[:, k0:k1, :], w=["%s/k%d" % (wres, i)], key="%s/k%d" % (wres, i))
    return lambda kc: "%s/k%d" % (wres, kc // step)


def gemm(kb, W, K, F, actT, act_res, orient, epi, wrot, psrot, T=S, gcols=512):
    KC = K // 128
    Wv = W.rearrange("(c p) n -> p c n", p=128)
    for g0 in range(0, F, gcols):
        gw = min(gcols, F - g0)
        wt, wres = wrot.next()
        wr = load_w(kb, wt, wres, Wv[:, :, g0:g0 + gw], KC, gw)
        if orient == "feat":
            assert gw % 128 == 0
            for fc in range(gw // 128):
                for tg in range(T // 512):
                    ps, pres = psrot.next()
                    for kc in range(KC):
                        kb.mm(ps, wt[:, kc, fc * 128:(fc + 1) * 128], actT[:, kc, tg * 512:(tg + 1) * 512],
                              kc == 0, kc == KC - 1, r=[wr(kc), act_res(kc, tg)], w=[pres])
                    epi(g0 + fc * 128, tg, ps, pres)
        else:
            for tt in range(T // 128):
                ps, pres = psrot.next()
                for kc in range(KC):
                    kb.mm(ps[:, :gw], actT[:, kc, tt * 128:(tt + 1) * 128], wt[:, kc, :gw],
                          kc == 0, kc == KC - 1, r=[wr(kc), act_res(kc, tt // 4)], w=[pres])
                epi(g0, gw, tt, ps[:, :gw], pres)


def norm_phase(kb, C, xT, gcol, hT, h_res, psrot, out_f32=None):
    xrot = Rot(kb, "nx", 2, [128, 16, 512], F32)
    sqrot = Rot(kb, "nsq", 2, [128, 512], F32)
    rs_rot = Rot(kb, "nrs", 2, [128, 512], F32)
    orot = Rot(kb, "nout", 2, [128, 512], F32) if out_f32 is not None else None
    xv = xT.rearrange("(c p) t -> p c t", p=128)
    for tg in range(4):
        xt, xres = xrot.next()
        kb.dma("sp", xt, xv[:, :, tg * 512:(tg + 1) * 512], r=["xT#%d_%d" % (c, tg) for c in range(16)], w=[xres], key=xres)
        ps, pres = psrot.next()
        for c in range(16):
            sq, sres = sqrot.next()
            kb.act(sq, xt[:, c, :], AF.Square, r=[xres], w=[sres])
            kb.mm(ps, C["ones_f"], sq, c == 0, c == 15, r=[sres, "consts"], w=[pres])
        rs, rres = rs_rot.next()
        kb.ts(rs, ps, 1.0 / D, ALU.mult, EPS, ALU.add, r=[pres], w=[rres])
        kb.act(rs, rs, AF.Sqrt, r=[rres], w=[rres])
        kb.recip(rs, rs, r=[rres], w=[rres])
        for c in range(16):
            if out_f32 is None:
                kb.stt(hT[:, c, tg * 512:(tg + 1) * 512], xt[:, c, :], C["vec"][:, gcol + c:gcol + c + 1], rs,
                       ALU.mult, ALU.mult, r=[xres, rres, "consts"], w=[h_res(c, tg)])
            else:
                o, ores = orot.next()
                kb.stt(o, xt[:, c, :], C["vec"][:, gcol + c:gcol + c + 1], rs,
                       ALU.mult, ALU.mult, r=[xres, rres, "consts"], w=[ores])
                kb.fin.append(kb.dma("sp", out_f32[c * 128:(c + 1) * 128, tg * 512:(tg + 1) * 512], o,
                                     r=[ores], w=["outT"], key=ores))


IN_W = (1024, 1536, 16, 1024, 1024, 1024, 1024, 256, 256, 256, 256, 256, 256, 48, 1024, 1024)
IN_NAMES = ("a_z", "a_xbc", "a_dt", "b_q", "b_k", "b_v", "c_q", "c_kc", "c_vc", "c_ks", "c_vs", "c_kw", "c_vw", "c_g", "d_gate", "d_x")
IN_OFF = {}
_o = 0
for _n, _w in zip(IN_NAMES, IN_W):
    IN_OFF[_n] = (_o, _w)
    _o += _w

VEC_SPEC = [("norm_mix", 2048), ("norm_ffn", 2048), ("norm_ple", 2048),
            ("ssd_conv_b", 1536), ("rnn_conv_b", 1024), ("rnn_b_r", 1024), ("rnn_b_i", 1024),
            ("rnn_lambda", 1024), ("ffn_conv_b", 12288)]
VEC_MULTI = [("ssd_conv_w", 4, 1536), ("rnn_conv_w", 4, 1024), ("ffn_conv_w", 3, 12288)]
BC_SPEC = [("ssd_dt_bias_rep", 256), ("ssd_a_log_rep", 256), ("ssd_d_rep", 1024), ("ssd_norm", 1024), ("diff_norm", 128),
           ("diff_lq1", 64), ("diff_lk1", 64), ("diff_lq2", 64), ("diff_lk2", 64)]


def vec_layout():
    off = {}
    o = 0
    for l in range(DEPTH):
        for n, f in VEC_SPEC:
            off[(n, l)] = o
            o += f // 128
        for n, k, f in VEC_MULTI:
            for kk in range(k):
                off[(n, l, kk)] = o
                o += f // 128
    off[("norm_final", 0)] = o
    o += 16
    return off, o


def bc_layout():
    off = {}
    o = 0
    for n, f in BC_SPEC:
        for l in range(DEPTH):
            off[(n, l)] = o
        o += f
    return off, o


def host_consts():
    c = {}
    c["ones_f"] = np.ones((128, 128), np.float32)
    c["ident_f"] = np.eye(128, dtype=np.float32)
    c["triu_f"] = np.triu(np.ones((128, 128), np.float32))
    r = np.arange(128)
    sw = np.where((r % 64) < 32, r + 32, r - 32)
    ps = np.zeros((128, 128), np.float32)
    ps[sw, r] = 1.0
    c["pswap"] = ps
    half = 32
    inv = 10000.0 ** (-np.arange(half, dtype=np.float32) / half)
    ang = np.arange(S, dtype=np.float32)[None, :] * inv[:, None]
    cos = np.cos(ang).astype(np.float32)
    sin = np.sin(ang).astype(np.float32)
    cos2 = np.concatenate([cos, cos, cos, cos], 0)
    sin2 = np.concatenate([-sin, sin, -sin, sin], 0)
    kk = np.arange(128)[:, None]
    qq = np.arange(128)[None, :]
    c["tri"] = np.where(qq >= kk, 0.0, NEG).astype(np.float32)
    c["wlo"] = np.where(qq < kk, 0.0, NEG).astype(np.float32)
    nn = np.arange(128)[:, None]
    tq = np.arange(S)[None, :]
    c["cmp_pen"] = np.where((tq >= 16 * nn + 31) & (nn < 127), 0.0, NEG).astype(np.float32)
    cs = np.arange(127)[:, None] * 16
    ss_ = np.arange(32)[None, :] * 64
    ov = np.clip(np.minimum(cs + 32, ss_ + 64) - np.maximum(cs, ss_), 0, None) / 32.0
    c["ovl"] = np.concatenate([ov, np.zeros((1, 32))], 0).astype(np.float32)
    pos = np.arange(S).reshape(16, 128).T
    cur = pos // 64
    bid = np.arange(32)[None, None, :]
    forced_m = (bid == cur[:, :, None]) | (bid == 0)
    future_m = bid > cur[:, :, None]
    c["forced"] = np.where(forced_m, 1e4, -3e38).astype(np.float32)
    c["future"] = np.where(future_m, -1e30, 3e38).astype(np.float32)
    es = np.zeros((32, 16, 128), np.float32)
    for j in range(16):
        for k in range(128):
            es[2 * j + (k >= 64), j, k] = 1.0
    c["esel"] = es
    c["rope"] = np.stack([cos2 * 0.125, sin2 * 0.125, cos2, sin2], 1).astype(np.float32)
    return c


def pack_vec(inputs):
    off, nv = vec_layout()
    v = np.zeros((128, nv), np.float32)
    for l in range(DEPTH):
        for n, f in VEC_SPEC:
            v[:, off[(n, l)]:off[(n, l)] + f // 128] = np.asarray(inputs[n][l], np.float32).reshape(f // 128, 128).T
        for n, k, f in VEC_MULTI:
            for kk in range(k):
                v[:, off[(n, l, kk)]:off[(n, l, kk)] + f // 128] = np.asarray(inputs[n][l][kk], np.float32).reshape(f // 128, 128).T
    o = off[("norm_final", 0)]
    v[:, o:o + 16] = np.asarray(inputs["norm_final"], np.float32).reshape(16, 128).T
    return v


def pack_bc(inputs):
    off, nb = bc_layout()
    v = np.zeros((DEPTH, 128, nb), np.float32)
    for l in range(DEPTH):
        for n, f in BC_SPEC:
            if n == "ssd_dt_bias_rep":
                a = np.tile(np.asarray(inputs["ssd_dt_bias"][l], np.float32), 16)
            elif n == "ssd_a_log_rep":
                a = np.tile(np.asarray(inputs["ssd_a_log"][l], np.float32), 16)
            elif n == "ssd_d_rep":
                a = np.repeat(np.asarray(inputs["ssd_d"][l], np.float32), 64)
            else:
                a = np.asarray(inputs[n][l], np.float32)
            v[l, :, off[(n, l)]:off[(n, l)] + f] = a.reshape(1, f)
    return v


def load_consts(kb):
    C = {}
    hc = host_consts()
    voff, nv = vec_layout()
    boff, nb = bc_layout()
    C["voff"], C["boff"] = voff, boff
    specs = [("ones_f", [128, 128], F32), ("ident_f", [128, 128], F32), ("pswap", [128, 128], F32), ("triu_f", [128, 128], F32),
             ("vec", [128, nv], F32)]
    for name, shape, dt in specs:
        d = kb.din("c_" + name, shape, dt)
        t = kb.sb("c_" + name, shape, dt)
        kb.dma("sp", t, d, w=["consts"], key="c_" + name)
        C[name] = t
    for name in ("ident", "pswap"):
        src = C["ident_f"] if name == "ident" else C["pswap"]
        t = kb.sb("c_" + name + "_b", [128, 128], BF16)
        kb.copy(t, src, r=["consts"], w=["consts2"])
        C[name + "_b"] = t
        C[name] = t
    C["rope_d"] = kb.din("c_rope", [128, 4, S], F32)
    C["bc_d"] = kb.din("c_bc", [DEPTH, 128, nb], F32)
    C["nb"] = nb
    for name, shape in (("forced", [128, 16, 32]), ("future", [128, 16, 32]), ("cmp_pen", [128, S]), ("esel", [32, 16, 128])):
        C[name + "_d"] = kb.din("c_" + name, shape, F32)
    C["ovl_d"] = kb.din("c_ovl", [128, 32], F32)
    C["posT_d"] = kb.din("c_posT", [DEPTH, 64, 32], F32)
    for name, shape in (("tri", [128, 128]), ("wlo", [128, 128])):
        d = kb.din("c_" + name, shape, F32)
        t = kb.sb("c_" + name + "_b", shape, BF16)
        kb.dma("pool", t, d, w=["consts2"], key="c_" + name)
        C[name + "_b"] = t
        C[name] = t
    return C


def proj_phase(kb, C, l, w_in, hT, h_res, SC):
    kb.push()
    wrot = Rot(kb, "pw", 2, [128, 16, 512], BF16)
    psrot = Rot(kb, "pps", 4, [128, 512], F32, psum=True)
    ps2rot = Rot(kb, "pps2", 2, [128, 512], F32, psum=True)
    rope = kb.sb("rope", [128, 4, S], F32)
    kb.dma("sp", rope, C["rope_d"], w=["rope"], key="rope")
    st32 = Rot(kb, "pst32", 3, [128, 512], F32)
    st16 = Rot(kb, "pst16", 3, [128, 512], BF16)
    xb16 = Rot(kb, "pxb", 3, [128, 512], BF16)
    t1r = Rot(kb, "pt1", 2, [128, 512], F32)
    t2r = Rot(kb, "pt2", 2, [128, 512], F32)
    cnt = [0]
    pending = []

    def flush_pending():
        while pending:
            pending.pop(0)()

    def evac_eng():
        cnt[0] += 1
        return "act" if cnt[0] % 2 else "dve"

    def feat_store(dst, dt, func=None):
        def epi(f0, tg, ps, pres, base):
            st, sres = (st32 if dt == F32 else st16).next()
            if func is not None:
                kb.act(st, ps, func, r=[pres], w=[sres])
            else:
                kb.copy(st, ps, r=[pres], w=[sres], eng=evac_eng())
            fo = f0 - base
            kb.dma("sp", dst[fo:fo + 128, tg * 512:(tg + 1) * 512], st, r=[sres], key=sres)
        return epi

    def feat_rope(dst, qk):
        ci, si = (0, 1) if qk == "q" else (2, 3)

        def epi(f0, tg, ps, pres, base):
            xb, xres = xb16.next()
            kb.copy(xb, ps, r=[pres], w=[xres], eng="act")
            flush_pending()

            def rest(xb=xb, xres=xres, f0=f0, tg=tg, base=base):
                p2, p2res = ps2rot.next()
                kb.mm(p2, C["pswap_b"], xb, True, True, r=[xres, "consts2"], w=[p2res])
                t1, t1res = t1r.next()
                t2, t2res = t2r.next()
                st, sres = st16.next()
                kb.tt(t1, xb, rope[:, ci, tg * 512:(tg + 1) * 512], ALU.mult, r=[xres, "rope"], w=[t1res])
                kb.tt(t2, p2, rope[:, si, tg * 512:(tg + 1) * 512], ALU.mult, r=[p2res, "rope"], w=[t2res])
                kb.tt(st, t1, t2, ALU.add, r=[t1res, t2res], w=[sres], eng=ROPE_ADD_ENG)
                fo = f0 - base
                kb.dma("sp", dst[fo:fo + 128, tg * 512:(tg + 1) * 512], st, r=[sres], key=sres)
            pending.append(rest)
        return epi

    def tok_store(dst, dt, func=None):
        def epi(c0, cw, tt, ps, pres, base):
            st, sres = (st32 if dt == F32 else st16).next()
            if func is not None:
                kb.act(st[:, :cw], ps, func, r=[pres], w=[sres])
            else:
                kb.copy(st[:, :cw], ps, r=[pres], w=[sres], eng=evac_eng())
            co = c0 - base
            kb.dma("sp", dst[tt * 128:(tt + 1) * 128, co:co + cw], st[:, :cw], r=[sres], key=sres)
        return epi

    plan = [
        ("a_z", "tok", tok_store(SC["zs"], F32, AF.Silu)),
        ("a_xbc", "feat", feat_store(SC["xbcT"], F32)),
        ("a_dt", "tok", tok_store(SC["dt"], F32)),
        ("b_q", "feat", feat_rope(SC["bqT"], "q")),
        ("b_k", "feat", feat_rope(SC["bkT"], "k")),
        ("b_v", "tok", tok_store(SC["bv"], BF16)),
        ("c_q", "feat", feat_rope(SC["cqT"], "q")),
        ("c_kc", "feat", feat_rope(SC["ckcT"], "k")),
        ("c_vc", "feat", feat_store(SC["cvcT"], BF16)),
        ("c_ks", "feat", feat_rope(SC["cksT"], "k")),
        ("c_vs", "tok", tok_store(SC["cvs"], BF16)),
        ("c_kw", "feat", feat_rope(SC["ckwT"], "k")),
        ("c_vw", "tok", tok_store(SC["cvw"], BF16)),
        ("c_g", "tok", tok_store(SC["cg"], F32, AF.Sigmoid)),
        ("d_gate", "feat", feat_store(SC["dgT"], F32, AF.Gelu_apprx_tanh)),
        ("d_x", "feat", feat_store(SC["dxT"], F32)),
    ]
    for name, orient, epi in plan:
        if PLAN_ONLY is not None and name not in PLAN_ONLY:
            continue
        c0, cw = IN_OFF[name]
        if orient == "feat":
            gemm(kb, w_in[l][:, c0:c0 + cw], D, cw, hT, h_res, "feat",
                 lambda f0, tg, ps, pres, epi=epi: epi(f0, tg, ps, pres, 0), wrot, psrot)
        else:
            gemm(kb, w_in[l][:, c0:c0 + cw], D, cw, hT, h_res, "tok",
                 lambda g0, gw, tt, ps, pres, epi=epi: epi(g0, gw, tt, ps, pres, 0), wrot, psrot)
        flush_pending()
    kb.pop()


def alloc_scratch(kb):
    SC = {}
    SC["xT"] = kb.dtmp("xT", [D, S], F32)
    SC["zs"] = kb.dtmp("zs", [S, 1024], F32)
    SC["xbcT"] = kb.dtmp("xbcT", [1536, S], F32)
    SC["dt"] = kb.dtmp("dt", [S, 16], F32)
    SC["bqT"] = kb.dtmp("bqT", [1024, S], BF16)
    SC["bkT"] = kb.dtmp("bkT", [1024, S], BF16)
    SC["bv"] = kb.dtmp("bv", [S, 1024], BF16)
    SC["cqT"] = kb.dtmp("cqT", [1024, S], BF16)
    SC["ckcT"] = kb.dtmp("ckcT", [256, S], BF16)
    SC["cvcT"] = kb.dtmp("cvcT", [256, S], BF16)
    SC["cksT"] = kb.dtmp("cksT", [256, S], BF16)
    SC["cvs"] = kb.dtmp("cvs", [S, 256], BF16)
    SC["ckwT"] = kb.dtmp("ckwT", [256, S], BF16)
    SC["cvw"] = kb.dtmp("cvw", [S, 256], BF16)
    SC["cg"] = kb.dtmp("cg", [S, 48], F32)
    SC["dgT"] = kb.dtmp("dgT", [1024, S], F32)
    SC["dxT"] = kb.dtmp("dxT", [1024, S], F32)
    SC["oT"] = kb.dtmp("oT", [4, 1024, S], BF16)
    SC["xs"] = kb.dtmp("xs", [S, 1024], F32)
    SC["Btok"] = kb.dtmp("Btok", [S, 256], BF16)
    SC["mT"] = kb.dtmp("mT", [D, S], BF16)
    SC["gT"] = kb.dtmp("gT", [4, D, S], BF16)
    SC["aT"] = kb.dtmp("aT", [D_FF, S], BF16)
    return SC


def rglru_phase(kb, C, l, Wd, SC):
    kb.push()
    voff = C["voff"]
    vec = C["vec"]
    psr = Rot(kb, "dps", 4, [128, 512], F32, psum=True)
    wbd = kb.sb("dwbd", [128, 2, 8, 128], BF16)
    kb.memset(wbd, 0.0, w=["wbd"])
    for wi, wn in enumerate(("rnn_w_r", "rnn_w_i")):
        src = Wd[wn][l].rearrange("(c two) i o -> two i c o", two=2)
        for hh in range(2):
            kb.dma("pool", wbd[hh * 64:(hh + 1) * 64, wi, :, hh * 64:(hh + 1) * 64], src[hh], r=[], w=["wbd"], key="wbd%d%d" % (wi, hh))
    cl = kb.sb("dcl", [128, 8], F32)
    lam = vec[:, voff[("rnn_lambda", l)]:voff[("rnn_lambda", l)] + 8]
    kb.act(cl, lam, AF.Exp, r=["consts"], w=["cl"], scale=-1.0)
    kb.act(cl, cl, AF.Ln, r=["cl"], w=["cl"], bias=1.0)
    kb.ts(cl, cl, -8.0, ALU.mult, r=["cl"], w=["cl"])
    xrot = Rot(kb, "dx", 2, [128, S + 3], F32)
    grot = Rot(kb, "dg", 2, [128, S], F32)
    for ap, res in xrot.t:
        kb.memset(ap[:, 0:3], 0.0, w=[res])
    wk = [dict(xc=kb.sb("dxc", [128, S], F32), xcb=kb.sb("dxcb", [128, S], BF16), rr=kb.sb("dr", [128, S], F32),
               ig=kb.sb("dig", [128, S], F32), aa=kb.sb("da", [128, S], F32), tmp=kb.sb("dtmp", [128, S], F32),
               hh=kb.sb("dh", [128, S], F32)) for _ in range(2)]
    orot = Rot(kb, "do", 2, [128, S], BF16)

    class _XP(Prefetch):
        def _issue(self):
            k = len(self.issued)
            if k < len(self.srcs):
                t, res = self.rot.next()
                self.kb.dma(self.q, t[:, 3:], self.srcs[k], w=[res], key=res)
                self.issued.append((t, res))
    x_pf = _XP(kb, xrot, [SC["dxT"][c * 128:(c + 1) * 128, :] for c in range(8)], ahead=1)
    g_pf = Prefetch(kb, grot, [SC["dgT"][c * 128:(c + 1) * 128, :] for c in range(8)], ahead=1)
    for c in range(8):
        W_ = wk[c % 2]
        xc, xcb, rr, ig, aa, tmp, hh_ = W_["xc"], W_["xcb"], W_["rr"], W_["ig"], W_["aa"], W_["tmp"], W_["hh"]
        sfx = "%d" % (c % 2)
        x, xres = x_pf.get()
        g, gres = g_pf.get()
        wcol = lambda k: vec[:, voff[("rnn_conv_w", l, k)] + c:voff[("rnn_conv_w", l, k)] + c + 1]
        bcol = vec[:, voff[("rnn_conv_b", l)] + c:voff[("rnn_conv_b", l)] + c + 1]
        kb.act(xc, x[:, 3:3 + S], AF.Identity, r=[xres, "consts"], w=["xc" + sfx], scale=wcol(3), bias=bcol)
        for k in range(3):
            kb.stt(xc, x[:, k:k + S], wcol(k), xc, ALU.mult, ALU.add, r=[xres, "xc" + sfx, "consts"], w=["xc" + sfx])
        kb.copy(xcb, xc, r=["xc" + sfx], w=["xcb" + sfx], eng="act")
        for wi, (dst, dres, bn) in enumerate(((rr, "rr" + sfx, "rnn_b_r"), (ig, "ig" + sfx, "rnn_b_i"))):
            bias = vec[:, voff[(bn, l)] + c:voff[(bn, l)] + c + 1]
            for tg in range(4):
                ps, pres = psr.next()
                kb.mm(ps, wbd[:, wi, c, :], xcb[:, tg * 512:(tg + 1) * 512], True, True, r=["wbd", "xcb" + sfx], w=[pres])
                kb.act(dst[:, tg * 512:(tg + 1) * 512], ps, AF.Sigmoid, r=[pres, "consts"], w=[dres], bias=bias)
        kb.act(aa, rr, AF.Exp, r=["rr" + sfx, "cl"], w=["aa" + sfx], scale=cl[:, c:c + 1])
        kb.act(tmp, aa, AF.Square, r=["aa" + sfx], w=["tmp" + sfx])
        kb.act(tmp, tmp, AF.Sqrt, r=["tmp" + sfx], w=["tmp" + sfx], scale=-1.0, bias=1.0)
        kb.tt(ig, ig, xc, ALU.mult, r=["ig" + sfx, "xc" + sfx], w=["ig" + sfx], eng="pool")
        kb.tt(tmp, tmp, ig, ALU.mult, r=["tmp" + sfx, "ig" + sfx], w=["tmp" + sfx])
        kb.P.add("dve", lambda e, hh_=hh_, aa=aa, tmp=tmp: e.tensor_tensor_scan(out=hh_, data0=aa, data1=tmp, initial=0.0, op0=ALU.mult, op1=ALU.add),
                 reads=["aa" + sfx, "tmp" + sfx], writes=["hh" + sfx])
        o, ores = orot.next()
        kb.tt(o, hh_, g, ALU.mult, r=["hh" + sfx, gres], w=[ores], eng="pool")
        kb.dma("sp", SC["oT"][3, c * 128:(c + 1) * 128, :], o, r=[ores], key=ores)
    kb.pop()


def rms_rstd(kb, out, ss, n, r, w):
    kb.ts(out, ss, 1.0 / n, ALU.mult, EPS, ALU.add, r=r, w=w)
    kb.act(out, out, AF.Sqrt, r=w, w=w)
    kb.recip(out, out, r=w, w=w)


def diff_phase(kb, C, l, SC):
    kb.push()
    boff, bc = C["boff"], C["bc"]
    lam_init = 0.8 - 0.6 * math.exp(-0.3 * l)
    lt = kb.sb("blt", [128, 64], F32)
    ls = kb.sb("bls", [128, 4], F32)
    for i, (a, b) in enumerate((("diff_lq1", "diff_lk1"), ("diff_lq2", "diff_lk2"))):
        oa, ob_ = boff[(a, l)], boff[(b, l)]
        kb.tt(lt, bc[:, oa:oa + 64], bc[:, ob_:ob_ + 64], ALU.mult, r=["consts"], w=["lt"])
        kb.P.add("dve", lambda e, i=i: e.tensor_reduce(out=ls[:, i:i + 1], in_=lt, axis=AX.X, op=ALU.add), reads=["lt"], writes=["ls"])
    kb.act(ls[:, 0:2], ls[:, 0:2], AF.Exp, r=["ls"], w=["ls"])
    kb.tt(ls[:, 2:3], ls[:, 1:2], ls[:, 0:1], ALU.subtract, r=["ls"], w=["ls"])
    kb.ts(ls[:, 2:3], ls[:, 2:3], -lam_init, ALU.add, r=["ls"], w=["ls"])
    neglam = ls[:, 2:3]
    gn = kb.sb("bgn", [128, 128], F32)
    og = boff[("diff_norm", l)]
    kb.ts(gn, bc[:, og:og + 128], 1.0 - lam_init, ALU.mult, r=["consts"], w=["gn"])

    qrot = Rot(kb, "bq", 2, [128, S], BF16)
    krot = Rot(kb, "bk", 2, [128, 2, S], BF16)
    for ap, res in krot.t:
        kb.memset(ap, 0.0, w=[res])
    vrot = Rot(kb, "bvv", 2, [128, 16, 129], BF16)
    for ap, res in vrot.t:
        kb.memset(ap[:, :, 128:129], 1.0, w=[res])
    pss = Rot(kb, "bps", 3, [128, 512], F32, psum=True)
    pso_all = kb.psum("bpo", [128, 4, 512], F32)
    pst = kb.psum("bpt", [128, 512], BF16)
    Pbuf = [kb.sb("bP%d" % i, [128, 16, 512], BF16) for i in range(2)]
    Osr = Rot(kb, "bO", 2, [128, 2, 4, 129], F32)
    sm = kb.sb("bsm", [128, 2, 4], F32)
    ssq = kb.sb("bssq", [128, 4], F32)
    rstd = kb.sb("brstd", [128, 4], F32)
    o1 = kb.sb("bo1", [128, 4, 128], F32)
    t2 = kb.sb("bt2", [128, 4, 128], F32)
    obr2 = Rot(kb, "bob", 2, [128, 4, 128], BF16)
    otr = Rot(kb, "bot", 2, [128, 512], BF16)
    ev = [0]
    cur = {}
    deferred = []

    def load_head(h):
        qT, qres = qrot.next()
        kT, kres = krot.next()
        v, vres = vrot.next()
        kb.dma("sp", qT, SC["bqT"][h * 128:(h + 1) * 128, :], w=[qres], key=qres)
        for m_ in range(2):
            kb.dma("sp", kT[m_ * 64:(m_ + 1) * 64, m_, :], SC["bkT"][h * 128 + m_ * 64:h * 128 + (m_ + 1) * 64, :], w=[kres], key=kres + "m%d" % m_)
        kb.dma("sp", v[:, :, 0:128], SC["bv"][:, h * 128:(h + 1) * 128].rearrange("(j p) e -> p j e", p=128), w=[vres], key=vres)
        return dict(qT=qT, qres=qres, kT=kT, kres=kres, v=v, vres=vres)

    def combine(h, qg, Osb, ores):
        rO = [ores + "m0", ores + "m1"]
        kb.recip(sm, Osb[:, :, :, 128], r=rO, w=["sm"])
        kb.ts(sm[:, 1, :], sm[:, 1, :], neglam, ALU.mult, r=["sm", "ls"], w=["sm"])
        kb.tt(o1, Osb[:, 0, :, 0:128], sm[:, 0, :].unsqueeze(2).broadcast_to([128, 4, 128]), ALU.mult, r=rO + ["sm"], w=["o1"])
        kb.tt(t2, Osb[:, 1, :, 0:128], sm[:, 1, :].unsqueeze(2).broadcast_to([128, 4, 128]), ALU.mult, r=rO + ["sm"], w=["t2"])
        kb.tt(o1, o1, t2, ALU.add, r=["o1", "t2"], w=["o1"], eng="pool")
        kb.tt(t2, o1, o1, ALU.mult, r=["o1"], w=["t2"], eng="pool")
        kb.P.add("dve", lambda e: e.tensor_reduce(out=ssq, in_=t2, axis=AX.X, op=ALU.add), reads=["t2"], writes=["ssq"])
        kb.act(rstd, ssq, AF.Ln, r=["ssq"], w=["rstd"], scale=1.0 / 128, bias=EPS)
        kb.act(rstd, rstd, AF.Exp, r=["rstd"], w=["rstd"], scale=-0.5)
        kb.tt(o1, o1, rstd.unsqueeze(2).broadcast_to([128, 4, 128]), ALU.mult, r=["o1", "rstd"], w=["o1"])
        ob, obres = obr2.next()
        kb.tt(ob, o1, gn.unsqueeze(1).broadcast_to([128, 4, 128]), ALU.mult, r=["o1", "gn"], w=[obres], eng="pool")

        def pe_part(ob=ob, obres=obres, h=h, qg=qg):
            for qt in range(4):
                kb.tr(pst[:, qt * 128:(qt + 1) * 128], ob[:, qt, :], C["ident_b"], r=[obres, "consts2"], w=["pst"])
            ot, otres = otr.next()
            kb.copy(ot, pst, r=["pst"], w=[otres], eng="dve")
            kb.dma("sp", SC["oT"][1, h * 128:(h + 1) * 128, qg * 512:(qg + 1) * 512], ot, r=[otres], key=otres)
        deferred.append(pe_part)

    groups = [(h, qg, m) for h in range(8) for qg in range(4) for m in range(2)]

    def score_steps(gi):
        h, qg, m = groups[gi]
        P = Pbuf[gi % 2]
        steps = []
        for j in range(4 * qg + 4):
            def step(j=j):
                if (h, qg, m, j) == (0, 0, 0, 0):
                    cur[0] = load_head(0)
                if (qg, m, j) == (1, 0, 0) and h + 1 < 8:
                    cur[h + 1] = load_head(h + 1)
                H = cur[h]
                r = j - 4 * qg
                c0 = 128 * r if r > 0 else 0
                ps, pres = pss.next()
                kb.mm(ps[:, c0:], H["kT"][:, m, j * 128:(j + 1) * 128], H["qT"][:, qg * 512 + c0:(qg + 1) * 512],
                      True, r < 0, r=[H["kres"], H["qres"]], w=[pres])
                if r >= 0:
                    kb.mm(ps[:, c0:c0 + 128], C["ident_b"], C["tri_b"], False, True, r=["consts2"], w=[pres])
                kb.act(P[:, j, c0:], ps[:, c0:], AF.Exp, r=[pres], w=["bP%d_%d" % (gi % 2, j)])
            steps.append(step)
        return steps

    def pv_steps(gi):
        h, qg, m = groups[gi]
        P = Pbuf[gi % 2]
        st = (gi % 2) * 2
        steps = []
        for qt in range(4):
            T = 4 * qg + qt
            bank, col = (st, qt * 129) if qt < 3 else (st + 1, 0)
            for j in range(T + 1):
                def step(qt=qt, j=j, T=T, bank=bank, col=col):
                    H = cur[h]
                    kb.mm(pso_all[:, bank, col:col + 129], P[:, j, qt * 128:(qt + 1) * 128], H["v"][:, j, :], j == 0, j == T,
                          r=["bP%d_%d" % (gi % 2, j), H["vres"]], w=["bpo%d" % bank])
                steps.append(step)

        def fin():
            while deferred:
                deferred.pop(0)()
            if m == 0:
                cur[(h, qg)] = Osr.next()
            Osb, ores = cur[(h, qg)]
            eng = "dve"
            kb.copy(Osb[:, m, 0:3, :], pso_all[:, st, 0:387].rearrange("p (q c) -> p q c", c=129), r=["bpo%d" % st], w=[ores + "m%d" % m], eng=eng)
            kb.copy(Osb[:, m, 3, :], pso_all[:, st + 1, 0:129], r=["bpo%d" % (st + 1)], w=[ores + "m%d" % m], eng=eng)
            if m == 1:
                combine(h, qg, Osb, ores)
        steps.append(fin)
        return steps

    for gi in range(len(groups) + 1):
        A = score_steps(gi) if gi < len(groups) else []
        B = pv_steps(gi - 1) if gi > 0 else []
        merge_steps(A, B)
    while deferred:
        deferred.pop(0)()
    kb.pop()


def merge_steps(A, B):
    na, nb = len(A), len(B)
    ia = ib = 0
    while ia < na or ib < nb:
        if ia < na and (ib >= nb or ia * max(nb, 1) <= ib * max(na, 1)):
            A[ia]()
            ia += 1
        else:
            B[ib]()
            ib += 1


def run_pipe(items, stage1, stage2, depth=1):
    q = []
    for it in items:
        q.append((it, stage1(it)))
        if len(q) > depth:
            stage2(*q.pop(0))
    while q:
        stage2(*q.pop(0))


def nsa_phase(kb, C, l, Wd, SC, posT_d):
    kb.push()
    NC_ = 127
    pss = Rot(kb, "cps", 3, [128, 512], F32, psum=True)
    pst = kb.psum("cpt", [128, 1024], BF16)
    kcT2 = kb.sb("ckcT2", [128, 2, 4, 128], BF16)
    vcx = kb.sb("cvcx", [128, 4, 97], BF16)
    kb.memset(kcT2, 0.0, w=["kcT2"])
    kb.memset(vcx, 0.0, w=["vcx"])
    kb.memset(vcx[:, :, 64:65], 1.0, w=["vcx"])
    for g in range(4):
        kb.dma("pool", vcx[:, g, 65:97], C["ovl_d"], w=["vcx"], key="ovl%d" % g)
    posT = kb.sb("cposT", [64, 32], F32)
    kb.dma("sp", posT, posT_d[l], w=["posT"], key="posT")
    srcT = kb.sb("csrc", [64, 4, S], BF16)
    w1 = kb.sb("cw1", [64, 32, 256], BF16)
    w2k = kb.sb("cw2k", [128, 2, 128], BF16)
    w2v = kb.sb("cw2v", [128, 2, 64], BF16)
    ktmp = kb.sb("cktmp", [64, 32, 128], BF16)
    hidT = kb.sb("chid", [128, 2, 128], BF16)
    for kv in range(2):
        src_d = SC["ckcT"] if kv == 0 else SC["cvcT"]
        kb.dma("sp", srcT, src_d.rearrange("(g d) t -> d g t", d=64), w=["srcT"], key="srcT")
        wn1, wn2 = (("nsa_ck_w1", "nsa_ck_w2") if kv == 0 else ("nsa_cv_w1", "nsa_cv_w2"))
        kb.dma("pool", w1, Wd[wn1][l].rearrange("(l d) h -> d l h", d=64), w=["w1"], key="w1")
        w2v_src = Wd[wn2][l].rearrange("(c p) d -> p c d", p=128)
        if kv == 0:
            kb.dma("pool", w2k[:, :, 0:64], w2v_src, w=["w2k"], key="w2ka")
            kb.dma("pool", w2k[:, :, 64:128], w2v_src, w=["w2k"], key="w2kb")
        else:
            kb.dma("pool", w2v, w2v_src, w=["w2v"], key="w2v")
        posb = kb.sb("cposb%d" % kv, [64, 32], BF16)
        kb.copy(posb, posT, r=["posT"], w=["posb%d" % kv])
        hpos = kb.sb("chpos%d" % kv, [128, 2], F32)
        for hc in range(2):
            ps, pres = pss.next()
            for ll in range(32):
                kb.mm(ps[:, 0:1], w1[:, ll, hc * 128:(hc + 1) * 128], posb[:, ll:ll + 1], ll == 0, ll == 31, r=["w1", "posb%d" % kv], w=[pres])
            kb.copy(hpos[:, hc:hc + 1], ps[:, 0:1], r=[pres], w=["hpos%d" % kv])
        for g in range(4):
            for hc in range(2):
                ps, pres = pss.next()
                for ll in range(32):
                    kb.mm(ps[:, 0:NC_], w1[:, ll, hc * 128:(hc + 1) * 128], srcT[:, g, ll:ll + 16 * (NC_ - 1) + 1:16], ll == 0, ll == 31,
                          r=["w1", "srcT"], w=[pres])
                kb.act(hidT[:, hc, 0:NC_], ps[:, 0:NC_], AF.Gelu_apprx_tanh, r=[pres, "hpos%d" % kv], w=["hidT"], bias=hpos[:, hc:hc + 1])
            ps, pres = pss.next()
            if kv == 0:
                for hc in range(2):
                    kb.mm(ps[:, 0:NC_], w2k[:, hc, :], hidT[:, hc, 0:NC_], hc == 0, hc == 1, r=["w2k", "hidT"], w=[pres])
                kb.copy(kcT2[0:64, 0, g, 0:NC_], ps[0:64, 0:NC_], r=[pres], w=["kcT2"])
                kb.copy(kcT2[64:128, 1, g, 0:NC_], ps[64:128, 0:NC_], r=[pres], w=["kcT2"])
            else:
                for hc in range(2):
                    kb.mm(ps[0:NC_, 0:64], hidT[:, hc, 0:NC_], w2v[:, hc, :], hc == 0, hc == 1, r=["w2v", "hidT"], w=[pres])
                kb.copy(vcx[0:NC_, g, 0:64], ps[0:NC_, 0:64], r=[pres], w=["vcx"])
    gates = kb.sb("cgate", [128, 16, 48], F32)
    kb.dma("sp", gates, SC["cg"].rearrange("(t p) c -> p t c", p=128), w=["gates"], key="gates")
    qrot = Rot(kb, "cq", 2, [128, 2, S], BF16)
    ksr = Rot(kb, "cks", 2, [128, 2, S], BF16)
    kwr = Rot(kb, "ckw", 2, [128, 2, S], BF16)
    for rot in (ksr, kwr):
        for ap, res in rot.t:
            kb.memset(ap, 0.0, w=[res])
    vsr = Rot(kb, "cvs", 2, [128, 16, 65], BF16)
    vwr = Rot(kb, "cvw", 2, [128, 16, 65], BF16)
    for rot in (vsr, vwr):
        for ap, res in rot.t:
            kb.memset(ap[:, :, 64:65], 1.0, w=[res])
    pso_all = kb.psum("cpo", [128, 4, 512], F32)
    pso_res = ["cpo%d" % i for i in range(4)]
    pT = Rot(kb, "cpT", 4, [128, 512], BF16)
    oaccr = Rot(kb, "coacc", 2, [128, 4, 4, 64], F32)
    imp = kb.sb("cimp", [128, 4, 32], F32)
    dn = kb.sb("cdn", [128, 4], F32)
    sc_ = kb.sb("csc", [128, 4], F32)
    tmpo = kb.sb("ctmpo", [128, 4, 64], F32)
    tmpi = kb.sb("ctmpi", [128, 4, 32], F32)
    top8 = kb.sb("ctop8", [128, 4, 8], F32)
    penq = kb.sb("cpenq", [128, 4, 32], BF16)
    penT = kb.sb("cpenT", [128, 512], BF16)
    kb.memset(penT, 0.0, w=["penT"])
    otr = Rot(kb, "cot", 2, [128, 2, 512], BF16)
    forced = kb.sb("cforced", [128, 16, 32], F32)
    future = kb.sb("cfuture", [128, 16, 32], F32)
    cmp_pen = kb.sb("ccmp_pen", [128, S], BF16)
    esel = kb.sb("cesel", [128, 16, 128], BF16)
    kb.memset(esel, 0.0, w=["consts2"])
    kb.dma("sp", forced, C["forced_d"], w=["consts"], key="cforced")
    kb.dma("sp", future, C["future_d"], w=["consts"], key="cfuture")
    kb.dma("pool", cmp_pen, C["cmp_pen_d"], w=["consts2"], key="ccmp_pen")
    kb.dma("pool", esel[0:32], C["esel_d"], w=["consts2"], key="cesel")
    cur = {}

    def load_group(g):
        qT, qres = qrot.next()
        kb.dma("sp", qT, SC["cqT"][g * 256:(g + 1) * 256, :].rearrange("(c p) t -> p c t", p=128), w=[qres], key=qres)
        ks, ksres = ksr.next()
        kw, kwres = kwr.next()
        for half in range(2):
            kb.dma("sp", ks[half * 64:(half + 1) * 64, half, :], SC["cksT"][g * 64:(g + 1) * 64, :], w=[ksres], key=ksres + "h%d" % half)
            kb.dma("sp", kw[half * 64:(half + 1) * 64, half, :], SC["ckwT"][g * 64:(g + 1) * 64, :], w=[kwres], key=kwres + "h%d" % half)
        vs, vsres = vsr.next()
        vw, vwres = vwr.next()
        kb.dma("sp", vs[:, :, 0:64], SC["cvs"][:, g * 64:(g + 1) * 64].rearrange("(j p) e -> p j e", p=128), w=[vsres], key=vsres)
        kb.dma("sp", vw[:, :, 0:64], SC["cvw"][:, g * 64:(g + 1) * 64].rearrange("(j p) e -> p j e", p=128), w=[vwres], key=vwres)
        return dict(qT=qT, qres=qres, ks=ks, ksres=ksres, kw=kw, kwres=kwres, vs=vs, vsres=vsres, vw=vw, vwres=vwres)

    def evac(O, den, rres, g, qg, jh, branch, first):
        oacc, oares = cur[("oacc", g, qg)]
        hh = g * 4 + jh
        kb.ts(dn, den, 1e-30, ALU.max, r=rres, w=["dn"])
        kb.recip(dn, dn, r=["dn"], w=["dn"])
        kb.tt(sc_, dn, gates[:, 4 * qg:4 * qg + 4, hh * 3 + branch], ALU.mult, r=["dn", "gates"], w=["sc"])
        dst = oacc[:, :, jh, :]
        dres = oares + "j%d" % jh
        sb_ = sc_.unsqueeze(2).broadcast_to([128, 4, 64])
        if first:
            kb.tt(dst, O, sb_, ALU.mult, r=rres + ["sc"], w=[dres])
        else:
            kb.tt(tmpo, O, sb_, ALU.mult, r=rres + ["sc"], w=["tmpo"])
            kb.tt(dst, dst, tmpo, ALU.add, r=[dres, "tmpo"], w=[dres], eng="pool")

    def topk_dve(g, qg):
        kb.tt(imp, imp, forced[:, 4 * qg:4 * qg + 4, :], ALU.max, r=["imp", "consts"], w=["imp"])
        kb.tt(imp, imp, future[:, 4 * qg:4 * qg + 4, :], ALU.min, r=["imp", "consts"], w=["imp"])
        for qt in range(4):
            kb.P.add("dve", lambda e, qt=qt: e.max(out=top8[:, qt, :], in_=imp[:, qt, :]), reads=["imp"], writes=["top8"])
        kb.tt(penq, imp, top8[:, :, 7:8].broadcast_to([128, 4, 32]), ALU.is_ge, r=["imp", "top8"], w=["penq"])
        kb.ts(penq, penq, -NEG, ALU.mult, NEG, ALU.add, r=["penq"], w=["penq"])

    def topk_pe(g, qg):
        for qt in range(4):
            kb.tr(pst[0:32, qt * 128:(qt + 1) * 128], penq[:, qt, :], C["ident_b"], r=["penq", "consts2"], w=["pst"])
        kb.copy(penT[0:32, :], pst[0:32, 0:512], r=["pst"], w=["penT"])

    ocbr = Rot(kb, "cocb", 2, [128, 4, 256], BF16)

    def writeout(g, qg):
        oacc, oares = cur[("oacc", g, qg)]
        ocb, ocres = ocbr.next()
        cur[("ocb", g, qg)] = (ocb, ocres)
        kb.copy(ocb, oacc.rearrange("p q j d -> p q (j d)"), r=[oares + "j%d" % jh for jh in range(4)], w=[ocres], eng="dve")

    def writeout_pe(g, qg):
        ocb, ocres = cur[("ocb", g, qg)]
        for qt in range(4):
            for c in range(2):
                kb.tr(pst[:, c * 512 + qt * 128:c * 512 + (qt + 1) * 128], ocb[:, qt, c * 128:(c + 1) * 128], C["ident_b"], r=[ocres, "consts2"], w=["pst"])
        ot, otres = otr.next()
        kb.copy(ot, pst.rearrange("p (c q) -> p c q", c=2), r=["pst"], w=[otres])
        for c in range(2):
            kb.dma("sp", SC["oT"][2, g * 256 + c * 128:g * 256 + (c + 1) * 128, qg * 512:(qg + 1) * 512], ot[:, c, :], r=[otres], key=otres + "c%d" % c)

    def stage1(it):
        kind, g, qg, jh = it[0], it[1], it[2], it[3]
        if kind == "sync":
            return None
        if kind == "cmp" and g == 0 and qg == 0 and jh == 0:
            cur[0] = load_group(0)
        if kind == "cmp" and qg == 1 and jh == 0 and g + 1 < 4:
            cur[g + 1] = load_group(g + 1)
        if kind == "cmp" and jh == 0:
            cur[("oacc", g, qg)] = oaccr.next()
        G = cur[g]
        c, hb = jh // 2, (jh % 2) * 64
        qT, qres = G["qT"], G["qres"]
        ps, pres = pss.next()
        p, ptres = pT.next()
        if kind == "cmp":
            kb.mm(ps, kcT2[:, jh % 2, g, :], qT[:, c, qg * 512:(qg + 1) * 512], True, False, r=["kcT2", qres], w=[pres])
            kb.mm(ps, C["ident_b"], cmp_pen[:, qg * 512:(qg + 1) * 512], False, True, r=["consts2"], w=[pres])
            kb.act(p, ps, AF.Exp, r=[pres], w=[ptres])
        elif kind == "slc":
            j = it[4]
            if jh == 0 and j == 0:
                topk_pe(g, qg)
            r = j - 4 * qg
            c0 = 128 * r if r > 0 else 0
            kb.mm(ps[:, c0:], G["ks"][:, jh % 2, j * 128:(j + 1) * 128], qT[:, c, qg * 512 + c0:(qg + 1) * 512], True, False,
                  r=[G["ksres"], qres], w=[pres])
            kb.mm(ps[:, c0:], esel[:, j, :], penT[:, c0:], False, r < 0, r=["consts2", "penT"], w=[pres])
            if r >= 0:
                kb.mm(ps[:, c0:c0 + 128], C["ident_b"], C["tri_b"], False, True, r=["consts2"], w=[pres])
            kb.act(p[:, c0:], ps[:, c0:], AF.Exp, r=[pres], w=[ptres])
        else:
            j, wk, r = it[4]
            if wk == "lo":
                ca, cb, pen = 0, 128 * (r + 1), C["wlo_b"]
            else:
                ca, cb, pen = 128 * r, 512, C["tri_b"]
            kb.mm(ps[:, ca:cb], G["kw"][:, jh % 2, j * 128:(j + 1) * 128], qT[:, c, qg * 512 + ca:qg * 512 + cb], True, False,
                  r=[G["kwres"], qres], w=[pres])
            kb.mm(ps[:, 128 * r:128 * r + 128], C["ident_b"], pen, False, True, r=["consts2"], w=[pres])
            kb.act(p[:, ca:cb], ps[:, ca:cb], AF.Exp, r=[pres], w=[ptres])
        return p, ptres

    def stage2(it, st):
        kind, g, qg, jh = it[0], it[1], it[2], it[3]
        if kind == "sync":
            it[4](g, qg)
            return
        p, ptres = st
        G = cur[g]
        if kind == "cmp":
            bank = pso_all[:, jh, :]
            for qt in range(4):
                kb.mm(bank[:, qt * 97:(qt + 1) * 97], p[:, qt * 128:(qt + 1) * 128], vcx[:, g, :], True, True, r=[ptres, "vcx"], w=[pso_res[jh]])
            a = bank[:, 0:388].rearrange("p (q c) -> p q c", c=97)
            evac(a[:, :, 0:64], a[:, :, 64], [pso_res[jh]], g, qg, jh, 0, True)
            dnb = dn.unsqueeze(2).broadcast_to([128, 4, 32])
            if jh == 0:
                kb.tt(imp, a[:, :, 65:97], dnb, ALU.mult, r=[pso_res[jh], "dn"], w=["imp"])
            else:
                kb.tt(tmpi, a[:, :, 65:97], dnb, ALU.mult, r=[pso_res[jh], "dn"], w=["tmpi"])
                kb.tt(imp, imp, tmpi, ALU.add, r=["imp", "tmpi"], w=["imp"], eng="pool")
            return
        if kind == "slc":
            j = it[4]
            v, vres = G["vs"], G["vsres"]
            for qt in range(4):
                T = 4 * qg + qt
                if j <= T:
                    kb.mm(pso_all[:, qt, :65], p[:, qt * 128:(qt + 1) * 128], v[:, j, :], j == 0, j == T, r=[ptres, vres], w=[pso_res[qt]])
            done = (j == 4 * qg + 3)
            branch = 1
        else:
            j, wk, r = it[4]
            v, vres = G["vw"], G["vwres"]
            uses = [qt for qt in range(4) if (qt <= r if wk == "lo" else qt >= r)]
            for qt in uses:
                first = (wk == "lo" and r == qt) or (wk == "hi" and r == 0 and qg == 0)
                last = (wk == "hi" and r == qt)
                kb.mm(pso_all[:, qt, :65], p[:, qt * 128:(qt + 1) * 128], v[:, j, :], first, last, r=[ptres, vres], w=[pso_res[qt]])
            done = (wk == "hi" and r == 3)
            branch = 2
        if done:
            evac(pso_all[:, :, 0:64], pso_all[:, :, 64], pso_res, g, qg, jh, branch, False)

    items = []
    prev_gq = None
    for g in range(4):
        for qg in range(4):
            for jh in range(4):
                items.append(("cmp", g, qg, jh))
            items.append(("sync", g, qg, 0, topk_dve))
            if prev_gq is not None:
                items.append(("sync", prev_gq[0], prev_gq[1], 0, writeout_pe))
            prev_gq = (g, qg)
            for jh in range(4):
                tiles = [(4 * qg - 4 + rp, "lo", rp) for rp in range(4) if qg > 0] + [(4 * qg + r, "hi", r) for r in range(4)]
                for t in tiles:
                    items.append(("win", g, qg, jh, t))
            for jh in range(4):
                for j in range(4 * qg + 4):
                    items.append(("slc", g, qg, jh, j))
            items.append(("sync", g, qg, 0, writeout))
    items.append(("sync", prev_gq[0], prev_gq[1], 0, writeout_pe))
    run_pipe(items, stage1, stage2, depth=2)
    kb.pop()


def ssd_phase(kb, C, l, SC):
    voff, vec, boff, bc = C["voff"], C["vec"], C["boff"], C["bc"]
    kb.push()
    BT = kb.sb("aBT", [128, 2, S], BF16)
    CT = kb.sb("aCT", [128, 2, S], BF16)
    kb.push()
    xrot = Rot(kb, "ax", 2, [128, S + 3], F32)
    for ap, res in xrot.t:
        kb.memset(ap[:, 0:3], 0.0, w=[res])
    xcr = Rot(kb, "axc", 2, [128, S], F32)
    ptr = Rot(kb, "aptr", 4, [128, 4, 128], F32, psum=True)
    st32 = Rot(kb, "ast32", 4, [128, 4, 128], F32)
    st16 = Rot(kb, "ast16", 4, [128, 4, 128], BF16)
    ev = [0]
    class _XP(Prefetch):
        def _issue(self):
            k = len(self.issued)
            if k < len(self.srcs):
                t, res = self.rot.next()
                self.kb.dma(self.q, t[:, 3:], self.srcs[k], w=[res], key=res)
                self.issued.append((t, res))
    x_pf = _XP(kb, xrot, [SC["xbcT"][c * 128:(c + 1) * 128, :] for c in range(12)], ahead=1)
    for c in range(12):
        x, xres = x_pf.get()
        xc, xcres = xcr.next()
        wcol = lambda k: vec[:, voff[("ssd_conv_w", l, k)] + c:voff[("ssd_conv_w", l, k)] + c + 1]
        bcol = vec[:, voff[("ssd_conv_b", l)] + c:voff[("ssd_conv_b", l)] + c + 1]
        kb.act(xc, x[:, 3:3 + S], AF.Identity, r=[xres, "consts"], w=[xcres], scale=wcol(3), bias=bcol)
        for k in range(3):
            kb.stt(xc, x[:, k:k + S], wcol(k), xc, ALU.mult, ALU.add, r=[xres, xcres, "consts"], w=[xcres])
        if c >= 8:
            dstT = BT if c < 10 else CT
            kb.act(dstT[:, c % 2, :], xc, AF.Silu, r=[xcres], w=["BCT%d" % c])
        kb.act(xc, xc, AF.Silu, r=[xcres], w=[xcres])
        if c < 10:
            for t4 in range(4):
                pt, ptres = ptr.next()
                for i in range(4):
                    tt_ = t4 * 4 + i
                    kb.tr(pt[:, i, :], xc[:, tt_ * 128:(tt_ + 1) * 128], C["ident_f"], r=[xcres, "consts"], w=[ptres])
                eng = "act"
                if c < 8:
                    st, sres = st32.next()
                    kb.copy(st, pt, r=[ptres], w=[sres], eng=eng)
                    kb.dma("sp", SC["xs"][t4 * 512:(t4 + 1) * 512, c * 128:(c + 1) * 128].rearrange("(i p) ch -> p i ch", p=128), st, r=[sres], key=sres)
                else:
                    st, sres = st16.next()
                    kb.copy(st, pt, r=[ptres], w=[sres], eng=eng)
                    kb.dma("sp", SC["Btok"][t4 * 512:(t4 + 1) * 512, (c - 8) * 128:(c - 7) * 128].rearrange("(i p) ch -> p i ch", p=128), st, r=[sres], key=sres)
    kb.pop()
    dts = kb.sb("adts", [128, 256], F32)
    adt = kb.sb("aadt", [128, 256], F32)
    acs = kb.sb("aacs", [128, 256], F32)
    eacs = kb.sb("aeacs", [128, 256], F32)
    dst_ = kb.sb("adst", [128, 256], F32)
    etot = kb.sb("aetot", [128, 256], F32)
    aexp = kb.sb("aaexp", [128, 256], F32)
    kb.push()
    ps1 = kb.psum("aps1", [128, 512], F32)
    ps2 = kb.psum("aps2", [128, 512], F32)
    kb.dma("sp", dts.rearrange("p (t h) -> p t h", h=16), SC["dt"].rearrange("(t p) h -> p t h", p=128), w=["dts"], key="dts")
    ob = boff[("ssd_dt_bias_rep", l)]
    kb.tt(dts, dts, bc[:, ob:ob + 256], ALU.add, r=["dts", "consts"], w=["dts"])
    kb.act(dts, dts, AF.Exp, r=["dts"], w=["dts"])
    kb.act(dts, dts, AF.Ln, r=["dts"], w=["dts"], bias=1.0)
    oa = boff[("ssd_a_log_rep", l)]
    kb.act(aexp, bc[:, oa:oa + 256], AF.Exp, r=["consts"], w=["aexp"])
    kb.stt(adt, dts, -1.0, aexp, ALU.mult, ALU.mult, r=["dts", "aexp"], w=["adt"])
    for t in range(16):
        kb.mm(ps1[:, t * 16:(t + 1) * 16], C["triu_f"], adt[:, t * 16:(t + 1) * 16], True, True, r=["adt", "consts"], w=["ps1"])
        kb.mm(ps2[:, t * 16:(t + 1) * 16], C["ones_f"], adt[:, t * 16:(t + 1) * 16], True, True, r=["adt", "consts"], w=["ps2"])
    kb.copy(acs, ps1[:, 0:256], r=["ps1"], w=["acs"])
    kb.act(eacs, acs, AF.Exp, r=["acs"], w=["eacs"])
    kb.tt(dst_, ps2[:, 0:256], acs, ALU.subtract, r=["ps2", "acs"], w=["dst"])
    kb.act(dst_, dst_, AF.Exp, r=["dst"], w=["dst"])
    kb.copy(etot, ps2[:, 0:256], r=["ps2"], w=["etot"])
    kb.act(etot, etot, AF.Exp, r=["etot"], w=["etot"])
    kb.pop()
    MTall = kb.sb("aMTall", [128, 16, 16, 128], BF16)
    kb.push()
    psg = Rot(kb, "apsg", 2, [128, 1024], F32, psum=True)
    pcb = Rot(kb, "apcb", 2, [128, 512], F32, psum=True)
    Xr = Rot(kb, "aX", 2, [128, 8, 128], F32)
    dr = Rot(kb, "ad", 2, [128, 8, 128], F32)
    cbr = Rot(kb, "acbm", 2, [128, 128], F32)
    triu_b8 = C["triu_f"].unsqueeze(1).broadcast_to([128, 8, 128])
    for t in range(16):
        tsl = slice(t * 128, (t + 1) * 128)
        for g in range(2):
            cols = slice(t * 16 + g * 8, t * 16 + g * 8 + 8)
            pc, pcres = pcb.next()
            kb.mm(pc[:, 0:128], BT[:, g, tsl], CT[:, g, tsl], True, True, r=["BCT%d" % (8 + g), "BCT%d" % (10 + g)], w=[pcres])
            cbm, cbres = cbr.next()
            kb.tt(cbm, pc[:, 0:128], C["triu_f"], ALU.mult, r=[pcres, "consts"], w=[cbres])
            xx, xxres = Xr.next()
            kb.tt(xx, triu_b8, adt[:, cols].unsqueeze(2).broadcast_to([128, 8, 128]), ALU.mult, r=["adt", "consts"], w=[xxres])
            pg, pgres = psg.next()
            for hf in range(2):
                kb.mm(pg[:, hf * 512:(hf + 1) * 512], C["ones_f"], xx[:, hf * 4:(hf + 1) * 4, :], True, True, r=[xxres, "consts"], w=[pgres])
            dd, ddres = dr.next()
            kb.tt(dd, pg.rearrange("p (h l) -> p h l", l=128), acs[:, cols].unsqueeze(2).broadcast_to([128, 8, 128]), ALU.subtract,
                  r=[pgres, "acs"], w=[ddres])
            kb.act(dd, dd, AF.Relu, r=[ddres], w=[ddres], scale=-1.0)
            kb.act(dd, dd, AF.Exp, r=[ddres], w=[ddres], scale=-1.0)
            kb.tt(MTall[:, t, g * 8:(g + 1) * 8, :], dd, cbm.unsqueeze(1).broadcast_to([128, 8, 128]), ALU.mult,
                  r=[ddres, cbres], w=["MT%d" % t], eng="pool")
    kb.pop()
    yd = kb.psum("ayd", [128, 1024], F32)
    yo = kb.psum("ayo", [128, 1024], F32)
    psS = kb.psum("apsS", [128, 512], F32)
    pst = kb.psum("apst", [128, 1024], BF16)
    HT = kb.sb("aHT", [128, 1024], F32)
    HTb = kb.sb("aHTb", [128, 1024], BF16)
    xsr = Rot(kb, "axs", 3, [128, 1024], F32)
    zsr = Rot(kb, "azs", 3, [128, 1024], F32)
    btr = Rot(kb, "abt", 3, [128, 256], BF16)
    xdtr = Rot(kb, "axdt", 2, [128, 16, 64], BF16)
    xdtpr = Rot(kb, "axdtp", 2, [128, 16, 64], BF16)
    yoff = kb.sb("ayoff", [128, 1024], F32)
    yr = Rot(kb, "ay", 2, [128, 1024], F32)
    tmpr = Rot(kb, "atmp", 2, [128, 1024], F32)
    junk = kb.sb("ajunk", [128, 1024], F32)
    obr = Rot(kb, "aob", 2, [128, 1024], BF16)
    smr = Rot(kb, "asm", 2, [128, 4], F32)
    stg = Rot(kb, "astg", 2, [128, 8, 512], BF16)
    od = boff[("ssd_d_rep", l)]
    on = boff[("ssd_norm", l)]
    stg_cur = None
    ssd_deferred = []
    xs_pf = Prefetch(kb, xsr, [SC["xs"][t * 128:(t + 1) * 128, :] for t in range(16)], ahead=1)
    zs_pf = Prefetch(kb, zsr, [SC["zs"][t * 128:(t + 1) * 128, :] for t in range(16)], ahead=1)
    bt_pf = Prefetch(kb, btr, [SC["Btok"][t * 128:(t + 1) * 128, :] for t in range(16)], ahead=1)
    for t in range(16):
        tsl = slice(t * 128, (t + 1) * 128)
        xs, xsres = xs_pf.get()
        zs, zsres = zs_pf.get()
        bt, btres = bt_pf.get()
        xs3 = xs.rearrange("p (h d) -> p h d", d=64)
        bc16 = lambda tab: tab[:, t * 16:(t + 1) * 16].unsqueeze(2).broadcast_to([128, 16, 64])
        xdt, xdres = xdtr.next()
        xdtp, xpres = xdtpr.next()
        kb.tt(xdt, xs3, bc16(dts), ALU.mult, r=[xsres, "dts"], w=[xdres])
        kb.tt(xdtp, xdt, bc16(dst_), ALU.mult, r=[xdres, "dst"], w=[xpres], eng="pool")
        tmp, tmpres = tmpr.next()
        kb.tt(tmp, xs, bc[:, od:od + 1024], ALU.mult, r=[xsres, "consts"], w=[tmpres], eng="pool")
        for h in range(16):
            kb.mm(yd[:, h * 64:(h + 1) * 64], MTall[:, t, h, :], xdt[:, h, :], True, True, r=["MT%d" % t, xdres], w=["yd%d" % (h // 8)])
        y, yres = yr.next()
        if t > 0:
            for g in range(2):
                kb.mm(yo[:, g * 512:(g + 1) * 512], CT[:, g, tsl], HTb[:, g * 512:(g + 1) * 512], True, True,
                      r=["BCT%d" % (10 + g), "HTb%d" % g], w=["yo%d" % g])
            kb.tt(yoff.rearrange("p (h d) -> p h d", d=64), yo.rearrange("p (h d) -> p h d", d=64), bc16(eacs), ALU.mult,
                  r=["yo0", "yo1", "eacs"], w=["yoff"])
            kb.tt(y, yd, yoff, ALU.add, r=["yd0", "yd1", "yoff"], w=[yres])
        else:
            kb.copy(y, yd, r=["yd0", "yd1"], w=[yres])
        if t < 15:
            for g in range(2):
                kb.mm(psS, bt[:, g * 128:(g + 1) * 128], xdtp[:, g * 8:(g + 1) * 8, :], True, True, r=[btres, xpres], w=["psS"])
                hsl = slice(g * 512, (g + 1) * 512)
                if t == 0:
                    kb.copy(HT[:, hsl], psS, r=["psS"], w=["HT%d" % g])
                else:
                    et = etot[:, t * 16 + g * 8:t * 16 + g * 8 + 8].unsqueeze(2).broadcast_to([128, 8, 64])
                    kb.tt(HT[:, hsl].rearrange("p (h d) -> p h d", d=64), HT[:, hsl].rearrange("p (h d) -> p h d", d=64), et, ALU.mult,
                          r=["HT%d" % g, "etot"], w=["HT%d" % g])
                    kb.tt(HT[:, hsl], HT[:, hsl], psS, ALU.add, r=["HT%d" % g, "psS"], w=["HT%d" % g])
                kb.copy(HTb[:, hsl], HT[:, hsl], r=["HT%d" % g], w=["HTb%d" % g], eng="act")
        while ssd_deferred:
            ssd_deferred.pop(0)()
        sm, smres = smr.next()
        kb.tt(y, y, tmp, ALU.add, r=[yres, tmpres], w=[yres])
        kb.tt(y, y, zs, ALU.mult, r=[yres, zsres], w=[yres])
        kb.act(junk, y, AF.Square, r=[yres], w=["junk", smres + "ss"], accum_out=sm[:, 0:1])
        rms_rstd(kb, sm[:, 1:2], sm[:, 0:1], 1024, [smres + "ss"], [smres + "rstd"])
        ob16, obres = obr.next()
        kb.stt(ob16, y, sm[:, 1:2], bc[:, on:on + 1024], ALU.mult, ALU.mult, r=[yres, smres + "rstd", "consts"], w=[obres])
        if t % 4 == 0:
            stg_cur = stg.next()

        def pe_tail(t=t, ob16=ob16, obres=obres, stg_cur=stg_cur):
            for c in range(8):
                kb.tr(pst[:, c * 128:(c + 1) * 128], ob16[:, c * 128:(c + 1) * 128], C["ident_b"], r=[obres, "consts2"], w=["pst"])
            sg_, sgres_ = stg_cur
            kb.copy(sg_[:, :, (t % 4) * 128:(t % 4 + 1) * 128], pst.rearrange("p (c q) -> p c q", q=128), r=["pst"], w=[sgres_], eng="act")
            if t % 4 == 3:
                t4 = t // 4
                kb.dma("sp", SC["oT"][0].rearrange("(c p) t -> p c t", p=128)[:, :, t4 * 512:(t4 + 1) * 512], sg_, r=[sgres_], key=sgres_)
        ssd_deferred.append(pe_tail)
    while ssd_deferred:
        ssd_deferred.pop(0)()
    kb.pop()


def load_actT(kb, dst, dram, KC, res_fn, q="sp", T=S):
    v = dram.rearrange("(c p) t -> p c t", p=128)
    for c in range(KC):
        kb.dma(q, dst[:, c, :T], v[:, c, :], w=[res_fn(c, tg) for tg in range(4)], key="ld_%s_%d" % (res_fn(c, 0), c))


def gates_phase(kb, C, l, Wd, SC, hT, h_res):
    kb.push()
    wrot = Rot(kb, "gw", 2, [128, 16, 512], BF16)
    psrot = Rot(kb, "gps", 4, [128, 512], F32, psum=True)
    st16 = Rot(kb, "gst", 3, [128, 512], BF16)
    for n in range(4):
        def epi(f0, tg, ps, pres, n=n):
            st, sres = st16.next()
            kb.act(st, ps, AF.Sigmoid, r=[pres], w=[sres])
            kb.dma("sp", SC["gT"][n, f0:f0 + 128, tg * 512:(tg + 1) * 512], st, r=[sres], key=sres)
        gemm(kb, Wd["w_merge_gate"][l, n], D, D, hT, h_res, "feat", epi, wrot, psrot)
    kb.pop()


def merge_phase(kb, C, l, Wd, SC):
    kb.push()
    GC = 256
    wbrot = Rot(kb, "mwb", 3, [128, 8, GC], BF16)
    psb = Rot(kb, "mpb", 6, [128, 512], F32, psum=True)
    oT = kb.sb("moT", [128, 4, 8, S], BF16)
    for n in range(4):
        kb.dma("sp", oT[:, n], SC["oT"][n].rearrange("(c p) t -> p c t", p=128), w=["moT%d" % n], key="moT%d" % n)
    gtr = Rot(kb, "mgt", 4, [128, S], BF16)
    gpf = Prefetch(kb, gtr, [SC["gT"][n, g * GC + fc * 128:g * GC + fc * 128 + 128, :]
                             for g in range(D // GC) for n in range(4) for fc in range(GC // 128)], ahead=2)
    tmr = Rot(kb, "mtm", 6, [128, 512], F32)
    acc = kb.sb("macc", [128, GC // 128, S], F32)
    str_ = Rot(kb, "mst", 3, [128, 512], BF16)
    wpf = Prefetch(kb, wbrot, [Wd["w_branch"][l, n][:, g * GC:(g + 1) * GC].rearrange("(c p) n -> p c n", p=128)
                               for g in range(D // GC) for n in range(4)], q="pool", ahead=1)
    for g in range(D // GC):
        for n in range(4):
            wb, wbres = wpf.get()
            for fc in range(GC // 128):
                f0 = g * GC + fc * 128
                gtf, gtres = gpf.get()
                for tg in range(4):
                    gt = gtf[:, tg * 512:(tg + 1) * 512]
                    pb, pbres = psb.next()
                    for kc in range(8):
                        kb.mm(pb, wb[:, kc, fc * 128:(fc + 1) * 128], oT[:, n, kc, tg * 512:(tg + 1) * 512], kc == 0, kc == 7,
                              r=[wbres, "moT%d" % n], w=[pbres])
                    ares = "macc#%d_%d" % (fc, tg)
                    asl = acc[:, fc, tg * 512:(tg + 1) * 512]
                    if n == 0:
                        kb.tt(asl, gt, pb, ALU.mult, r=[gtres, pbres], w=[ares])
                    else:
                        tm, tmres = tmr.next()
                        kb.tt(tm, gt, pb, ALU.mult, r=[gtres, pbres], w=[tmres])
                        aeng = "pool" if (fc * 4 + tg) % 2 else "dve"
                        if n < 3:
                            kb.tt(asl, asl, tm, ALU.add, r=[ares, tmres], w=[ares], eng=aeng)
                        else:
                            st, sres = str_.next()
                            kb.tt(st, asl, tm, ALU.add, r=[ares, tmres], w=[sres], eng=aeng)
                            kb.dma("sp", SC["mT"][f0:f0 + 128, tg * 512:(tg + 1) * 512], st, r=[sres], key=sres)
    kb.pop()


def resid_gemm_phase(kb, C, W, K, actT, act_res, x_src, x_dst, tag):
    kb.push()
    wrot = Rot(kb, tag + "w", 2, [128, K // 128, 512], BF16)
    psrot = Rot(kb, tag + "ps", 4, [128, 512], F32, psum=True)
    xr = Rot(kb, tag + "x", 5, [128, 512], F32)
    pf = Prefetch(kb, xr, [x_src[f0:f0 + 128, tg * 512:(tg + 1) * 512] for f0 in range(0, D, 128) for tg in range(4)])

    def epi(f0, tg, ps, pres):
        xt, xres = pf.get()
        kb.tt(xt, xt, ps, ALU.add, r=[xres, pres], w=[xres])
        kb.dma("sp", x_dst[f0:f0 + 128, tg * 512:(tg + 1) * 512], xt, r=[xres], key=xres)
    gemm(kb, W, K, D, actT, act_res, "feat", epi, wrot, psrot)
    kb.pop()


def ffn_up_phase(kb, C, l, Wd, SC, hT, h_res):
    voff, vec = C["voff"], C["vec"]
    kb.push()
    wrot = Rot(kb, "fw", FFN_WSLOTS, [128, 16, 512], BF16)
    psrot = Rot(kb, "fps", 6, [128, 512], F32, psum=True)
    urot = Rot(kb, "fu", 2, [128, S + 2], F32)
    for ap, res in urot.t:
        kb.memset(ap[:, 0:2], 0.0, w=[res])
    crot = Rot(kb, "fc", 2, [128, S], F32)
    vrot = Rot(kb, "fv", 2, [128, S], F32)
    arot = Rot(kb, "fa", 2, [128, S], BF16)
    Wup = Wd["ffn_w_up"][l]
    ev = [0]
    def issue_group(i0):
        wts = []
        for half in range(2):
            wt, wres = wrot.next()
            c0 = half * D_FF + i0 * 128
            wr = load_w(kb, wt, wres, Wup[:, c0:c0 + 512].rearrange("(c p) n -> p c n", p=128), 16, 512)
            wts.append((wt, wr))
        return wts
    nxt = issue_group(0)
    for i0 in range(0, 48, 4):
        wts = nxt
        if i0 + 4 < 48:
            nxt = issue_group(i0 + 4)
        for fc in range(4):
            i = i0 + fc
            conv = []
            for half in range(2):
                wt, wres = wts[half]
                u, ures = urot.next()
                for tg in range(4):
                    ps, pres = psrot.next()
                    for kc in range(16):
                        kb.mm(ps, wt[:, kc, fc * 128:(fc + 1) * 128], hT[:, kc, tg * 512:(tg + 1) * 512], kc == 0, kc == 15,
                              r=[wres(kc), h_res(kc, tg)], w=[pres])
                    kb.copy(u[:, 2 + tg * 512:2 + (tg + 1) * 512], ps, r=[pres], w=[ures], eng="act")
                ch = half * 48 + i
                wc = lambda k: vec[:, voff[("ffn_conv_w", l, k)] + ch:voff[("ffn_conv_w", l, k)] + ch + 1]
                bc = vec[:, voff[("ffn_conv_b", l)] + ch:voff[("ffn_conv_b", l)] + ch + 1]
                cv, cres = (crot if half == 0 else vrot).next()
                kb.ts(cv, u[:, 2:2 + S], wc(2), ALU.mult, bc, ALU.add, r=[ures, "consts"], w=[cres])
                kb.stt(cv, u[:, 1:1 + S], wc(1), cv, ALU.mult, ALU.add, r=[ures, cres, "consts"], w=[cres])
                kb.stt(cv, u[:, 0:S], wc(0), cv, ALU.mult, ALU.add, r=[ures, cres, "consts"], w=[cres])
                conv.append((cv, cres))
            (cg, cgres), (cvv, cvres) = conv
            kb.act(cg, cg, AF.Gelu_apprx_tanh, r=[cgres], w=[cgres])
            a, ares = arot.next()
            kb.tt(a, cg, cvv, ALU.mult, r=[cgres, cvres], w=[ares], eng="pool")
            kb.dma("sp", SC["aT"][i * 128:(i + 1) * 128, :], a, r=[ares], key=ares)
    kb.pop()


def ffn_down_phase(kb, C, l, Wd, SC):
    kb.push()
    GC = 256
    TH = 1024
    NQ = TH // 512
    wrot = Rot(kb, "gw", 2, [128, 48, GC], BF16)
    psrot = Rot(kb, "gps", 4, [128, 512], F32, psum=True)
    at = kb.sb("ga", [128, 48, TH], BF16)
    xr = Rot(kb, "gx", 5, [128, 512], F32)
    Wdn = Wd["ffn_w_down"][l]
    xT = SC["xT"]
    av = SC["aT"].rearrange("(c p) t -> p c t", p=128)
    pf = Prefetch(kb, xr, [xT[g * GC + fc * 128:g * GC + fc * 128 + 128, th * TH + tq * 512:th * TH + tq * 512 + 512]
                           for th in range(S // TH) for g in range(D // GC) for tq in range(NQ) for fc in range(GC // 128)])
    for th in range(S // TH):
        for tq in range(NQ):
            for c6 in range(6):
                t0 = th * TH + tq * 512
                kb.dma("sp", at[:, c6 * 8:(c6 + 1) * 8, tq * 512:(tq + 1) * 512], av[:, c6 * 8:(c6 + 1) * 8, t0:t0 + 512],
                       w=["ga_%d_%d" % (c6, tq)], key="ga_%d_%d" % (c6, tq))
        for g in range(D // GC):
            wt, wres = wrot.next()
            wr = load_w(kb, wt, wres, Wdn[:, g * GC:(g + 1) * GC].rearrange("(c p) n -> p c n", p=128), 48, GC, nsplit=6)
            for tq in range(NQ):
                for fc in range(GC // 128):
                    ps, pres = psrot.next()
                    for kc in range(48):
                        kb.mm(ps, wt[:, kc, fc * 128:(fc + 1) * 128], at[:, kc, tq * 512:(tq + 1) * 512], kc == 0, kc == 47,
                              r=[wr(kc), "ga_%d_%d" % (kc // 8, tq)], w=[pres])
                    f0 = g * GC + fc * 128
                    t0 = th * TH + tq * 512
                    xt, xres = pf.get()
                    kb.tt(xt, xt, ps, ALU.add, r=[xres, pres], w=[xres])
                    kb.dma("sp", xT[f0:f0 + 128, t0:t0 + 512], xt, r=[xres], key=xres)
    kb.pop()


def ple_phase(kb, C, l, Wd, SC, hT, h_res, pT_in):
    kb.push()
    wgrot = Rot(kb, "ewg", 2, [128, 16, 512], BF16)
    wprot = Rot(kb, "ewp", 2, [128, 2, 512], BF16)
    psg = Rot(kb, "epg", 3, [128, 512], F32, psum=True)
    psp = Rot(kb, "epp", 3, [128, 512], F32, psum=True)
    pT = kb.sb("epT", [128, 2, S], BF16)
    kb.dma("pool", pT, pT_in[l].rearrange("(c p) t -> p c t", p=128), w=["pT"], key="pT")
    sgr = Rot(kb, "esg", 2, [128, 512], F32)
    xr = Rot(kb, "ex", 5, [128, 512], F32)
    xT = SC["xT"]
    pf = Prefetch(kb, xr, [xT[g * 512 + fc * 128:g * 512 + fc * 128 + 128, tg * 512:(tg + 1) * 512]
                           for g in range(4) for fc in range(4) for tg in range(4)])
    def issue_g(g):
        wg, wgres = wgrot.next()
        wgr = load_w(kb, wg, wgres, Wd["ple_w_gate"][l][:, g * 512:(g + 1) * 512].rearrange("(c p) n -> p c n", p=128), 16, 512)
        wp, wpres = wprot.next()
        kb.dma("pool", wp, Wd["ple_w_proj"][l][:, g * 512:(g + 1) * 512].rearrange("(c p) n -> p c n", p=128), w=[wpres], key=wpres)
        return wg, wgr, wp, wpres
    nxt = issue_g(0)
    for g in range(4):
        wg, wgr, wp, wpres = nxt
        if g + 1 < 4:
            nxt = issue_g(g + 1)
        for fc in range(4):
            for tg in range(4):
                pg, pgres = psg.next()
                for kc in range(16):
                    kb.mm(pg, wg[:, kc, fc * 128:(fc + 1) * 128], hT[:, kc, tg * 512:(tg + 1) * 512], kc == 0, kc == 15,
                          r=[wgr(kc), h_res(kc, tg)], w=[pgres])
                pp, ppres = psp.next()
                for kc in range(2):
                    kb.mm(pp, wp[:, kc, fc * 128:(fc + 1) * 128], pT[:, kc, tg * 512:(tg + 1) * 512], kc == 0, kc == 1,
                          r=[wpres, "pT"], w=[ppres])
                sg, sgres = sgr.next()
                kb.act(sg, pg, AF.Sigmoid, r=[pgres], w=[sgres])
                kb.tt(sg, sg, pp, ALU.mult, r=[sgres, ppres], w=[sgres])
                f0 = g * 512 + fc * 128
                xt, xres = pf.get()
                kb.tt(xt, xt, sg, ALU.add, r=[xres, sgres], w=[xres], eng=("pool" if tg % 2 else "dve"))
                kb.dma("sp", xT[f0:f0 + 128, tg * 512:(tg + 1) * 512], xt, r=[xres], key=xres)
    kb.pop()


WEIGHT_NAMES = ("w_in", "nsa_ck_w1", "nsa_ck_w2", "nsa_cv_w1", "nsa_cv_w2", "rnn_w_r", "rnn_w_i",
                "w_merge_gate", "w_branch", "w_out", "ffn_w_up", "ffn_w_down", "ple_w_proj", "ple_w_gate")
WEIGHT_SHAPES = {
    "w_in": [DEPTH, D, D_IN], "nsa_ck_w1": [DEPTH, 2048, 256], "nsa_ck_w2": [DEPTH, 256, 64],
    "nsa_cv_w1": [DEPTH, 2048, 256], "nsa_cv_w2": [DEPTH, 256, 64], "rnn_w_r": [DEPTH, 16, 64, 64],
    "rnn_w_i": [DEPTH, 16, 64, 64], "nsa_pos_cmp": [DEPTH, 32, 64],
    "w_merge_gate": [DEPTH, 4, D, D], "w_branch": [DEPTH, 4, 1024, D], "w_out": [DEPTH, D, D],
    "ffn_w_up": [DEPTH, D, 2 * D_FF], "ffn_w_down": [DEPTH, D_FF, D], "ple_w_proj": [DEPTH, PLE, D],
    "ple_w_gate": [DEPTH, D, D],
}


def build(stage="full", debug=(), nlayers=DEPTH, inject=(), skip=()):
    kb = KB(debug)
    kb.inject = set(inject)
    kb.fin = []
    kb.push()
    xT_in = kb.din("xT_in", [D, S], F32)
    pT_in = kb.din("pT_in", [DEPTH, PLE, S], F32)
    Wd = {n: kb.din(n, WEIGHT_SHAPES[n], F32) for n in WEIGHT_NAMES}
    outT = kb.dout("outT", [D, S], F32)
    C = load_consts(kb)
    SC = alloc_scratch(kb)
    kb.P.barrier()
    voff = C["voff"]
    h_res = lambda c, tg: "hT#%d_%d" % (c, tg)

    def normed(x_src, gcol):
        kb.push()
        hT = kb.sb("hT", [128, 16, S], BF16)
        kb.push()
        nps = Rot(kb, "nps", 2, [128, 512], F32, psum=True)
        norm_phase(kb, C, x_src, gcol, hT, h_res, nps)
        kb.pop()
        return hT

    for l in range(nlayers):
        x_src = xT_in if l == 0 else SC["xT"]
        kb.push()
        bct = kb.sb("bc", [128, C["nb"]], F32)
        kb.dma("sp", bct, C["bc_d"][l], w=["consts"], key="bc")
        C["bc"] = bct
        hT = normed(x_src, voff[("norm_mix", l)])
        if "P" not in skip:
            proj_phase(kb, C, l, Wd["w_in"], hT, h_res, SC)
        if stage != "P" and "G" not in skip:
            gates_phase(kb, C, l, Wd, SC, hT, h_res)
        kb.pop()
        if stage == "P":
            break
        if "D" not in skip:
            rglru_phase(kb, C, l, Wd, SC)
        if stage == "D":
            break
        if "B" not in skip:
            diff_phase(kb, C, l, SC)
        if stage == "B":
            break
        if "C" not in skip:
            nsa_phase(kb, C, l, Wd, SC, C["posT_d"])
        if stage == "C":
            break
        if "A" not in skip:
            ssd_phase(kb, C, l, SC)
        if stage == "A":
            break
        kb.pop()
        merge_phase(kb, C, l, Wd, SC)
        kb.push()
        hT = kb.sb("hT", [128, 16, S], BF16)
        load_actT(kb, hT, SC["mT"], 16, h_res)
        resid_gemm_phase(kb, C, Wd["w_out"][l], D, hT, h_res, x_src, SC["xT"], "o")
        kb.pop()
        hT = normed(SC["xT"], voff[("norm_ffn", l)])
        ffn_up_phase(kb, C, l, Wd, SC, hT, h_res)
        kb.pop()
        ffn_down_phase(kb, C, l, Wd, SC)
        hT = normed(SC["xT"], voff[("norm_ple", l)])
        ple_phase(kb, C, l, Wd, SC, hT, h_res, pT_in)
        kb.pop()
    if stage in ("full", "rest"):
        kb.push()
        nps = Rot(kb, "nps", 2, [128, 512], F32, psum=True)
        norm_phase(kb, C, SC["xT"], voff[("norm_final", 0)], None, h_res, nps, out_f32=outT)
        kb.pop()
    else:
        kb.push()
        t = kb.sb("dummy", [128, 512], F32)
        kb.memset(t, 0.0, w=["dummy"])
        kb.fin.append(kb.dma("sp", outT[0:128, 0:512], t, r=["dummy"], key="dummy"))
        kb.pop()
    kb.pop()
    kb.P.emit(final_waits=kb.fin)
    return kb


def make_in_maps(inputs, kb):
    hc = host_consts()
    vec = pack_vec(inputs)
    bc = pack_bc(inputs)
    shared = {"c_ones_f": hc["ones_f"], "c_ident_f": hc["ident_f"], "c_pswap": hc["pswap"], "c_triu_f": hc["triu_f"],
              "c_vec": vec, "c_bc": bc, "c_rope": hc["rope"], "c_tri": hc["tri"], "c_wlo": hc["wlo"],
              "c_forced": hc["forced"], "c_future": hc["future"], "c_ovl": hc["ovl"], "c_cmp_pen": hc["cmp_pen"], "c_esel": hc["esel"],
              "c_posT": np.ascontiguousarray(np.asarray(inputs["nsa_pos_cmp"], np.float32).transpose(0, 2, 1))}
    for n in WEIGHT_NAMES:
        shared[n] = np.ascontiguousarray(np.asarray(inputs[n], np.float32))
    for k in list(shared):
        if k not in kb.ins:
            del shared[k]
    maps = []
    x = np.asarray(inputs["x"], np.float32)
    p = np.asarray(inputs["p"], np.float32)
    for b in range(x.shape[0]):
        m = dict(shared)
        m["xT_in"] = np.ascontiguousarray(x[b].T)
        m["pT_in"] = np.ascontiguousarray(p[:, b].transpose(0, 2, 1))
        maps.append(m)
    return maps


def kernel(**inputs):
    kb = build("full")
    maps = make_in_maps(inputs, kb)
    res = run_bass_kernel_spmd(kb.nc, maps, core_ids=list(range(8)))
    out = np.stack([np.ascontiguousarray(r["outT"].T) for r in res.results], 0)
    return out.astype(np.float32)
```

```python
import math
import numpy as np
import ml_dtypes
import concourse.bass as bass
import concourse.mybir as mybir
from concourse.bass_utils import run_bass_kernel_spmd

F32 = mybir.dt.float32
BF16 = mybir.dt.bfloat16
AF = mybir.ActivationFunctionType
ALU = mybir.AluOpType
AX = mybir.AxisListType

ENGS = ("pe", "act", "dve", "pool", "sp")
SEM_EPOCH = 30000

D = 2048
S = 2048
DEPTH = 2
D_IN = 10304
D_FF = 6144
PLE = 256
EPS = 1e-6
NEG = -30000.0
PLAN_ONLY = None
FFN_WSLOTS = 4
ROPE_ADD_ENG = "dve"


class Prog:
    def __init__(self, nc):
        self.nc = nc
        self.ops = []
        self.last_w = {}
        self.readers = {}
        self.dma_last = {}
        self.dma_since = []
        self.last_on = {}
        self.epoch_op = None

    def add(self, eng, fn, reads=(), writes=(), dma=None):
        deps = set()
        if self.epoch_op is not None:
            deps.add(self.epoch_op)
        for r in reads:
            if r in self.last_w:
                deps.add(self.last_w[r])
        for w in writes:
            if w in self.last_w:
                deps.add(self.last_w[w])
            for x in self.readers.get(w, ()):
                deps.add(x)
        idx = len(self.ops)
        if dma is not None:
            if dma in self.dma_last:
                deps.add(self.dma_last[dma])
            self.dma_last[dma] = idx
            self.dma_since.append(idx)
        else:
            self.last_on[eng] = idx
        self.ops.append(dict(eng=eng, fn=fn, deps=deps, dma=dma))
        for r in reads:
            self.readers.setdefault(r, []).append(idx)
        for w in writes:
            self.last_w[w] = idx
            self.readers[w] = []
        return idx

    def barrier(self):
        deps = set(self.last_on.values()) | set(self.dma_since)
        if self.epoch_op is not None:
            deps.add(self.epoch_op)
        idx = len(self.ops)
        self.ops.append(dict(eng="sp", fn=lambda e: e.nop(), deps=deps, dma=None, barrier=True))
        self.last_on["sp"] = idx
        self.dma_since = []
        self.epoch_op = idx
        return idx

    def emit(self, final_waits=()):
        nc = self.nc
        ops = self.ops
        n = len(ops)
        needed = [False] * n
        for i, o in enumerate(ops):
            pruned = set()
            for d in o["deps"]:
                od = ops[d]
                if od["dma"] is None and od["eng"] == "pe" and o["eng"] == "pe" and o["dma"] is None:
                    continue
                pruned.add(d)
            o["deps"] = pruned
            for d in pruned:
                needed[d] = True
        for i in final_waits:
            needed[i] = True
        sem_objs = {}
        ctxs = []

        def get_sem(key):
            if key not in sem_objs:
                c = nc.semaphore("s_%d" % len(sem_objs))
                s = c.__enter__()
                ctxs.append(c)
                sem_objs[key] = s
            return sem_objs[key]

        last_use = {}
        for i, o in enumerate(ops):
            if o["dma"] is not None:
                last_use[o["dma"]] = i
        free_slots = []
        key2slot = {}
        nslots = [0]
        cnt = {}
        for i, o in enumerate(ops):
            if o.get("barrier"):
                for k in [k for k in key2slot if last_use[k] < i]:
                    free_slots.append(key2slot.pop(k))
            if not needed[i]:
                o["sig"] = None
                continue
            if o["dma"] is not None:
                k = o["dma"]
                if k not in key2slot:
                    fs = [x for x in free_slots if x[2] == o["eng"]]
                    if fs:
                        free_slots.remove(fs[-1])
                        key2slot[k] = fs[-1]
                    else:
                        key2slot[k] = ("dmaslot", nslots[0], o["eng"])
                        nslots[0] += 1
                key = key2slot[k]
                c = cnt.get(key, 0) + 16
                cnt[key] = c
                o["sig"] = (key, get_sem(key), 16, c)
            else:
                base = ("eng", o["eng"])
                ep = cnt.get((base, "ep"), 0)
                c = cnt.get((base, ep), 0) + 1
                if c > SEM_EPOCH:
                    ep += 1
                    cnt[(base, "ep")] = ep
                    c = 1
                cnt[(base, ep)] = c
                o["sig"] = ((base, ep), get_sem((base, ep)), 1, c)
        self.n_sems = len(sem_objs)
        by_eng = {e: [] for e in ENGS}
        for i, o in enumerate(ops):
            by_eng[o["eng"]].append(i)
        with nc.Block() as block:
            def make(engname):
                def body(eng):
                    waited = {}
                    for i in by_eng[engname]:
                        o = ops[i]
                        need = {}
                        for d in o["deps"]:
                            k, s, _, v = ops[d]["sig"]
                            if v > need.get(k, (None, 0))[1]:
                                need[k] = (s, v)
                        for k, (s, v) in need.items():
                            if waited.get(k, 0) >= v:
                                continue
                            waited[k] = v
                            eng.wait_ge(s, v)
                        ins = o["fn"](eng)
                        if o["sig"] is not None:
                            _, s, inc, v = o["sig"]
                            ins.then_inc(s, inc)
                    if engname == "sp":
                        for i in final_waits:
                            _, s, _, v = ops[i]["sig"]
                            eng.wait_ge(s, v)
                return body
            block.tensor(make("pe"))
            block.scalar(make("act"))
            block.vector(make("dve"))
            block.gpsimd(make("pool"))
            block.sync(make("sp"))
        for c in reversed(ctxs):
            c.__exit__(None, None, None)


class KB:
    def __init__(self, debug=()):
        self.nc = bass.Bass("TRN2", target_bir_lowering=False)
        self.P = Prog(self.nc)
        self.debug = set(debug)
        self.ins = {}
        self.outs = {}
        self.n = 0
        self.stack = []

    def din(self, name, shape, dt=F32):
        ap = self.nc.dram_tensor(name, list(shape), dt, kind="ExternalInput").ap()
        self.ins[name] = ap
        return ap

    def dout(self, name, shape, dt=F32):
        ap = self.nc.dram_tensor(name, list(shape), dt, kind="ExternalOutput").ap()
        self.outs[name] = ap
        return ap

    def dtmp(self, name, shape, dt=F32):
        if name in getattr(self, "inject", ()):
            return self.din(name, shape, dt)
        if name in self.debug:
            return self.dout(name, shape, dt)
        return self.nc.dram_tensor(name, list(shape), dt, kind="Internal").ap()

    def push(self):
        self.stack.append([])

    def pop(self):
        for c in reversed(self.stack.pop()):
            c.__exit__(None, None, None)
        self.P.barrier()

    def sb(self, name, shape, dt=F32):
        self.n += 1
        c = self.nc.sbuf_tensor("%s_%d" % (name, self.n), list(shape), dt)
        t = c.__enter__()
        self.stack[-1].append(c)
        return t.ap()

    def psum(self, name, shape, dt=F32):
        self.n += 1
        c = self.nc.psum_tensor("%s_%d" % (name, self.n), list(shape), dt)
        t = c.__enter__()
        self.stack[-1].append(c)
        return t.ap()

    def dma(self, q, out, in_, r=(), w=(), key=None):
        assert key is not None
        return self.P.add(q, lambda e: e.dma_start(out=out, in_=in_), reads=r, writes=w, dma=key)

    def mm(self, out, lhsT, rhs, start, stop, r=(), w=()):
        return self.P.add("pe", lambda e: e.matmul(out, lhsT=lhsT, rhs=rhs, start=start, stop=stop), reads=r, writes=w)

    def tr(self, out, in_, ident, r=(), w=()):
        return self.P.add("pe", lambda e: e.transpose(out, in_, ident), reads=r, writes=w)

    def act(self, out, in_, func, r=(), w=(), bias=None, scale=1.0, accum_out=None):
        def fn(e):
            kw = {}
            if bias is not None:
                kw["bias"] = bias
            if accum_out is not None:
                kw["accum_out"] = accum_out
            return e.activation(out=out, in_=in_, func=func, scale=scale, **kw)
        return self.P.add("act", fn, reads=r, writes=w)

    def tt(self, out, in0, in1, op, r=(), w=(), eng="dve"):
        return self.P.add(eng, lambda e: e.tensor_tensor(out=out, in0=in0, in1=in1, op=op), reads=r, writes=w)

    def ts(self, out, in0, s1, op0, s2=None, op1=None, r=(), w=(), eng="dve", accum_out=None):
        def fn(e):
            kw = {}
            if op1 is not None:
                kw["op1"] = op1
            if accum_out is not None:
                kw["accum_out"] = accum_out
            return e.tensor_scalar(out=out, in0=in0, scalar1=s1, scalar2=s2, op0=op0, **kw)
        return self.P.add(eng, fn, reads=r, writes=w)

    def stt(self, out, in0, scalar, in1, op0, op1, r=(), w=()):
        return self.P.add("dve", lambda e: e.scalar_tensor_tensor(out=out, in0=in0, scalar=scalar, in1=in1, op0=op0, op1=op1), reads=r, writes=w)

    def copy(self, out, in_, r=(), w=(), eng="dve"):
        if eng == "act":
            return self.act(out, in_, AF.Copy, r=r, w=w)
        return self.P.add(eng, lambda e: e.tensor_copy(out=out, in_=in_), reads=r, writes=w)

    def memset(self, ap, val, w=(), eng="dve"):
        return self.P.add(eng, lambda e: e.memset(ap, val), writes=w)

    def recip(self, out, in_, r=(), w=()):
        return self.P.add("dve", lambda e: e.reciprocal(out=out, in_=in_), reads=r, writes=w)


class Rot:
    def __init__(self, kb, name, n, shape, dt=F32, psum=False):
        self.t = []
        for i in range(n):
            ap = kb.psum(name, shape, dt) if psum else kb.sb(name, shape, dt)
            self.t.append((ap, "%s#%d_%d" % (name, i, kb.n)))
        self.i = 0

    def next(self):
        x = self.t[self.i % len(self.t)]
        self.i += 1
        return x


class Prefetch:
    def __init__(self, kb, rot, srcs, q="sp", ahead=2):
        self.kb, self.rot, self.srcs, self.q, self.ahead = kb, rot, srcs, q, ahead
        self.issued = []
        self.i = 0

    def _issue(self):
        k = len(self.issued)
        if k < len(self.srcs):
            t, res = self.rot.next()
            self.kb.dma(self.q, t, self.srcs[k], w=[res], key=res)
            self.issued.append((t, res))

    def get(self):
        while len(self.issued) < min(len(self.srcs), self.i + 1 + self.ahead):
            self._issue()
        x = self.issued[self.i]
        self.i += 1
        return x


def load_w(kb, wt, wres, src, KC, ncol, nsplit=4):
    nsplit = max(1, min(nsplit, KC))
    step = (KC + nsplit - 1) // nsplit
    for i, k0 in enumerate(range(0, KC, step)):
        k1 = min(KC, k0 + step)
        kb.dma("pool", wt[:, k0:k1, :ncol], src[:, k0:k1, :], w=["%s/k%d" % (wres, i)], key="%s/k%d" % (wres, i))
    return lambda kc: "%s/k%d" % (wres, kc // step)


def gemm(kb, W, K, F, actT, act_res, orient, epi, wrot, psrot, T=S, gcols=512):
    KC = K // 128
    Wv = W.rearrange("(c p) n -> p c n", p=128)
    for g0 in range(0, F, gcols):
        gw = min(gcols, F - g0)
        wt, wres = wrot.next()
        wr = load_w(kb, wt, wres, Wv[:, :, g0:g0 + gw], KC, gw)
        if orient == "feat":
            assert gw % 128 == 0
            for fc in range(gw // 128):
                for tg in range(T // 512):
                    ps, pres = psrot.next()
                    for kc in range(KC):
                        kb.mm(ps, wt[:, kc, fc * 128:(fc + 1) * 128], actT[:, kc, tg * 512:(tg + 1) * 512],
                              kc == 0, kc == KC - 1, r=[wr(kc), act_res(kc, tg)], w=[pres])
                    epi(g0 + fc * 128, tg, ps, pres)
        else:
            for tt in range(T // 128):
                ps, pres = psrot.next()
                for kc in range(KC):
                    kb.mm(ps[:, :gw], actT[:, kc, tt * 128:(tt + 1) * 128], wt[:, kc, :gw],
                          kc == 0, kc == KC - 1, r=[wr(kc), act_res(kc, tt // 4)], w=[pres])
                epi(g0, gw, tt, ps[:, :gw], pres)


def norm_phase(kb, C, xT, gcol, hT, h_res, psrot, out_f32=None):
    xrot = Rot(kb, "nx", 2, [128, 16, 512], F32)
    sqrot = Rot(kb, "nsq", 2, [128, 512], F32)
    rs_rot = Rot(kb, "nrs", 2, [128, 512], F32)
    orot = Rot(kb, "nout", 2, [128, 512], F32) if out_f32 is not None else None
    xv = xT.rearrange("(c p) t -> p c t", p=128)
    for tg in range(4):
        xt, xres = xrot.next()
        kb.dma("sp", xt, xv[:, :, tg * 512:(tg + 1) * 512], r=["xT#%d_%d" % (c, tg) for c in range(16)], w=[xres], key=xres)
        ps, pres = psrot.next()
        for c in range(16):
            sq, sres = sqrot.next()
            kb.act(sq, xt[:, c, :], AF.Square, r=[xres], w=[sres])
            kb.mm(ps, C["ones_f"], sq, c == 0, c == 15, r=[sres, "consts"], w=[pres])
        rs, rres = rs_rot.next()
        kb.ts(rs, ps, 1.0 / D, ALU.mult, EPS, ALU.add, r=[pres], w=[rres])
        kb.act(rs, rs, AF.Sqrt, r=[rres], w=[rres])
        kb.recip(rs, rs, r=[rres], w=[rres])
        for c in range(16):
            if out_f32 is None:
                kb.stt(hT[:, c, tg * 512:(tg + 1) * 512], xt[:, c, :], C["vec"][:, gcol + c:gcol + c + 1], rs,
                       ALU.mult, ALU.mult, r=[xres, rres, "consts"], w=[h_res(c, tg)])
            else:
                o, ores = orot.next()
                kb.stt(o, xt[:, c, :], C["vec"][:, gcol + c:gcol + c + 1], rs,
                       ALU.mult, ALU.mult, r=[xres, rres, "consts"], w=[ores])
                kb.fin.append(kb.dma("sp", out_f32[c * 128:(c + 1) * 128, tg * 512:(tg + 1) * 512], o,
                                     r=[ores], w=["outT"], key=ores))


IN_W = (1024, 1536, 16, 1024, 1024, 1024, 1024, 256, 256, 256, 256, 256, 256, 48, 1024, 1024)
IN_NAMES = ("a_z", "a_xbc", "a_dt", "b_q", "b_k", "b_v", "c_q", "c_kc", "c_vc", "c_ks", "c_vs", "c_kw", "c_vw", "c_g", "d_gate", "d_x")
IN_OFF = {}
_o = 0
for _n, _w in zip(IN_NAMES, IN_W):
    IN_OFF[_n] = (_o, _w)
    _o += _w

VEC_SPEC = [("norm_mix", 2048), ("norm_ffn", 2048), ("norm_ple", 2048),
            ("ssd_conv_b", 1536), ("rnn_conv_b", 1024), ("rnn_b_r", 1024), ("rnn_b_i", 1024),
            ("rnn_lambda", 1024), ("ffn_conv_b", 12288)]
VEC_MULTI = [("ssd_conv_w", 4, 1536), ("rnn_conv_w", 4, 1024), ("ffn_conv_w", 3, 12288)]
BC_SPEC = [("ssd_dt_bias_rep", 256), ("ssd_a_log_rep", 256), ("ssd_d_rep", 1024), ("ssd_norm", 1024), ("diff_norm", 128),
           ("diff_lq1", 64), ("diff_lk1", 64), ("diff_lq2", 64), ("diff_lk2", 64)]


def vec_layout():
    off = {}
    o = 0
    for l in range(DEPTH):
        for n, f in VEC_SPEC:
            off[(n, l)] = o
            o += f // 128
        for n, k, f in VEC_MULTI:
            for kk in range(k):
                off[(n, l, kk)] = o
                o += f // 128
    off[("norm_final", 0)] = o
    o += 16
    return off, o


def bc_layout():
    off = {}
    o = 0
    for n, f in BC_SPEC:
        for l in range(DEPTH):
            off[(n, l)] = o
        o += f
    return off, o


def host_consts():
    c = {}
    c["ones_f"] = np.ones((128, 128), np.float32)
    c["ident_f"] = np.eye(128, dtype=np.float32)
    c["triu_f"] = np.triu(np.ones((128, 128), np.float32))
    r = np.arange(128)
    sw = np.where((r % 64) < 32, r + 32, r - 32)
    ps = np.zeros((128, 128), np.float32)
    ps[sw, r] = 1.0
    c["pswap"] = ps
    half = 32
    inv = 10000.0 ** (-np.arange(half, dtype=np.float32) / half)
    ang = np.arange(S, dtype=np.float32)[None, :] * inv[:, None]
    cos = np.cos(ang).astype(np.float32)
    sin = np.sin(ang).astype(np.float32)
    cos2 = np.concatenate([cos, cos, cos, cos], 0)
    sin2 = np.concatenate([-sin, sin, -sin, sin], 0)
    kk = np.arange(128)[:, None]
    qq = np.arange(128)[None, :]
    c["tri"] = np.where(qq >= kk, 0.0, NEG).astype(np.float32)
    c["wlo"] = np.where(qq < kk, 0.0, NEG).astype(np.float32)
    nn = np.arange(128)[:, None]
    tq = np.arange(S)[None, :]
    c["cmp_pen"] = np.where((tq >= 16 * nn + 31) & (nn < 127), 0.0, NEG).astype(np.float32)
    cs = np.arange(127)[:, None] * 16
    ss_ = np.arange(32)[None, :] * 64
    ov = np.clip(np.minimum(cs + 32, ss_ + 64) - np.maximum(cs, ss_), 0, None) / 32.0
    c["ovl"] = np.concatenate([ov, np.zeros((1, 32))], 0).astype(np.float32)
    pos = np.arange(S).reshape(16, 128).T
    cur = pos // 64
    bid = np.arange(32)[None, None, :]
    forced_m = (bid == cur[:, :, None]) | (bid == 0)
    future_m = bid > cur[:, :, None]
    c["forced"] = np.where(forced_m, 1e4, -3e38).astype(np.float32)
    c["future"] = np.where(future_m, -1e30, 3e38).astype(np.float32)
    es = np.zeros((32, 16, 128), np.float32)
    for j in range(16):
        for k in range(128):
            es[2 * j + (k >= 64), j, k] = 1.0
    c["esel"] = es
    c["rope"] = np.stack([cos2 * 0.125, sin2 * 0.125, cos2, sin2], 1).astype(np.float32)
    return c


def pack_vec(inputs):
    off, nv = vec_layout()
    v = np.zeros((128, nv), np.float32)
    for l in range(DEPTH):
        for n, f in VEC_SPEC:
            v[:, off[(n, l)]:off[(n, l)] + f // 128] = np.asarray(inputs[n][l], np.float32).reshape(f // 128, 128).T
        for n, k, f in VEC_MULTI:
            for kk in range(k):
                v[:, off[(n, l, kk)]:off[(n, l, kk)] + f // 128] = np.asarray(inputs[n][l][kk], np.float32).reshape(f // 128, 128).T
    o = off[("norm_final", 0)]
    v[:, o:o + 16] = np.asarray(inputs["norm_final"], np.float32).reshape(16, 128).T
    return v


def pack_bc(inputs):
    off, nb = bc_layout()
    v = np.zeros((DEPTH, 128, nb), np.float32)
    for l in range(DEPTH):
        for n, f in BC_SPEC:
            if n == "ssd_dt_bias_rep":
                a = np.tile(np.asarray(inputs["ssd_dt_bias"][l], np.float32), 16)
            elif n == "ssd_a_log_rep":
                a = np.tile(np.asarray(inputs["ssd_a_log"][l], np.float32), 16)
            elif n == "ssd_d_rep":
                a = np.repeat(np.asarray(inputs["ssd_d"][l], np.float32), 64)
            else:
                a = np.asarray(inputs[n][l], np.float32)
            v[l, :, off[(n, l)]:off[(n, l)] + f] = a.reshape(1, f)
    return v


def load_consts(kb):
    C = {}
    hc = host_consts()
    voff, nv = vec_layout()
    boff, nb = bc_layout()
    C["voff"], C["boff"] = voff, boff
    specs = [("ones_f", [128, 128], F32), ("ident_f", [128, 128], F32), ("pswap", [128, 128], F32), ("triu_f", [128, 128], F32),
             ("vec", [128, nv], F32)]
    for name, shape, dt in specs:
        d = kb.din("c_" + name, shape, dt)
        t = kb.sb("c_" + name, shape, dt)
        kb.dma("sp", t, d, w=["consts"], key="c_" + name)
        C[name] = t
    for name in ("ident", "pswap"):
        src = C["ident_f"] if name == "ident" else C["pswap"]
        t = kb.sb("c_" + name + "_b", [128, 128], BF16)
        kb.copy(t, src, r=["consts"], w=["consts2"])
        C[name + "_b"] = t
        C[name] = t
    C["rope_d"] = kb.din("c_rope", [128, 4, S], F32)
    C["bc_d"] = kb.din("c_bc", [DEPTH, 128, nb], F32)
    C["nb"] = nb
    for name, shape in (("forced", [128, 16, 32]), ("future", [128, 16, 32]), ("cmp_pen", [128, S]), ("esel", [32, 16, 128])):
        C[name + "_d"] = kb.din("c_" + name, shape, F32)
    C["ovl_d"] = kb.din("c_ovl", [128, 32], F32)
    C["posT_d"] = kb.din("c_posT", [DEPTH, 64, 32], F32)
    for name, shape in (("tri", [128, 128]), ("wlo", [128, 128])):
        d = kb.din("c_" + name, shape, F32)
        t = kb.sb("c_" + name + "_b", shape, BF16)
        kb.dma("pool", t, d, w=["consts2"], key="c_" + name)
        C[name + "_b"] = t
        C[name] = t
    return C


def proj_phase(kb, C, l, w_in, hT, h_res, SC):
    kb.push()
    wrot = Rot(kb, "pw", 2, [128, 16, 512], BF16)
    psrot = Rot(kb, "pps", 4, [128, 512], F32, psum=True)
    ps2rot = Rot(kb, "pps2", 2, [128, 512], F32, psum=True)
    rope = kb.sb("rope", [128, 4, S], F32)
    kb.dma("sp", rope, C["rope_d"], w=["rope"], key="rope")
    st32 = Rot(kb, "pst32", 3, [128, 512], F32)
    st16 = Rot(kb, "pst16", 3, [128, 512], BF16)
    xb16 = Rot(kb, "pxb", 3, [128, 512], BF16)
    t1r = Rot(kb, "pt1", 2, [128, 512], F32)
    t2r = Rot(kb, "pt2", 2, [128, 512], F32)
    cnt = [0]
    pending = []

    def flush_pending():
        while pending:
            pending.pop(0)()

    def evac_eng():
        cnt[0] += 1
        return "act" if cnt[0] % 2 else "dve"

    def feat_store(dst, dt, func=None):
        def epi(f0, tg, ps, pres, base):
            st, sres = (st32 if dt == F32 else st16).next()
            if func is not None:
                kb.act(st, ps, func, r=[pres], w=[sres])
            else:
                kb.copy(st, ps, r=[pres], w=[sres], eng=evac_eng())
            fo = f0 - base
            kb.dma("sp", dst[fo:fo + 128, tg * 512:(tg + 1) * 512], st, r=[sres], key=sres)
        return epi

    def feat_rope(dst, qk):
        ci, si = (0, 1) if qk == "q" else (2, 3)

        def epi(f0, tg, ps, pres, base):
            xb, xres = xb16.next()
            kb.copy(xb, ps, r=[pres], w=[xres], eng="act")
            flush_pending()

            def rest(xb=xb, xres=xres, f0=f0, tg=tg, base=base):
                p2, p2res = ps2rot.next()
                kb.mm(p2, C["pswap_b"], xb, True, True, r=[xres, "consts2"], w=[p2res])
                t1, t1res = t1r.next()
                t2, t2res = t2r.next()
                st, sres = st16.next()
                kb.tt(t1, xb, rope[:, ci, tg * 512:(tg + 1) * 512], ALU.mult, r=[xres, "rope"], w=[t1res])
                kb.tt(t2, p2, rope[:, si, tg * 512:(tg + 1) * 512], ALU.mult, r=[p2res, "rope"], w=[t2res])
                kb.tt(st, t1, t2, ALU.add, r=[t1res, t2res], w=[sres], eng=ROPE_ADD_ENG)
                fo = f0 - base
                kb.dma("sp", dst[fo:fo + 128, tg * 512:(tg + 1) * 512], st, r=[sres], key=sres)
            pending.append(rest)
        return epi

    def tok_store(dst, dt, func=None):
        def epi(c0, cw, tt, ps, pres, base):
            st, sres = (st32 if dt == F32 else st16).next()
            if func is not None:
                kb.act(st[:, :cw], ps, func, r=[pres], w=[sres])
            else:
                kb.copy(st[:, :cw], ps, r=[pres], w=[sres], eng=evac_eng())
            co = c0 - base
            kb.dma("sp", dst[tt * 128:(tt + 1) * 128, co:co + cw], st[:, :cw], r=[sres], key=sres)
        return epi

    plan = [
        ("a_z", "tok", tok_store(SC["zs"], F32, AF.Silu)),
        ("a_xbc", "feat", feat_store(SC["xbcT"], F32)),
        ("a_dt", "tok", tok_store(SC["dt"], F32)),
        ("b_q", "feat", feat_rope(SC["bqT"], "q")),
        ("b_k", "feat", feat_rope(SC["bkT"], "k")),
        ("b_v", "tok", tok_store(SC["bv"], BF16)),
        ("c_q", "feat", feat_rope(SC["cqT"], "q")),
        ("c_kc", "feat", feat_rope(SC["ckcT"], "k")),
        ("c_vc", "feat", feat_store(SC["cvcT"], BF16)),
        ("c_ks", "feat", feat_rope(SC["cksT"], "k")),
        ("c_vs", "tok", tok_store(SC["cvs"], BF16)),
        ("c_kw", "feat", feat_rope(SC["ckwT"], "k")),
        ("c_vw", "tok", tok_store(SC["cvw"], BF16)),
        ("c_g", "tok", tok_store(SC["cg"], F32, AF.Sigmoid)),
        ("d_gate", "feat", feat_store(SC["dgT"], F32, AF.Gelu_apprx_tanh)),
        ("d_x", "feat", feat_store(SC["dxT"], F32)),
    ]
    for name, orient, epi in plan:
        if PLAN_ONLY is not None and name not in PLAN_ONLY:
            continue
        c0, cw = IN_OFF[name]
        if orient == "feat":
            gemm(kb, w_in[l][:, c0:c0 + cw], D, cw, hT, h_res, "feat",
                 lambda f0, tg, ps, pres, epi=epi: epi(f0, tg, ps, pres, 0), wrot, psrot)
        else:
            gemm(kb, w_in[l][:, c0:c0 + cw], D, cw, hT, h_res, "tok",
                 lambda g0, gw, tt, ps, pres, epi=epi: epi(g0, gw, tt, ps, pres, 0), wrot, psrot)
        flush_pending()
    kb.pop()


def alloc_scratch(kb):
    SC = {}
    SC["xT"] = kb.dtmp("xT", [D, S], F32)
    SC["zs"] = kb.dtmp("zs", [S, 1024], F32)
    SC["xbcT"] = kb.dtmp("xbcT", [1536, S], F32)
    SC["dt"] = kb.dtmp("dt", [S, 16], F32)
    SC["bqT"] = kb.dtmp("bqT", [1024, S], BF16)
    SC["bkT"] = kb.dtmp("bkT", [1024, S], BF16)
    SC["bv"] = kb.dtmp("bv", [S, 1024], BF16)
    SC["cqT"] = kb.dtmp("cqT", [1024, S], BF16)
    SC["ckcT"] = kb.dtmp("ckcT", [256, S], BF16)
    SC["cvcT"] = kb.dtmp("cvcT", [256, S], BF16)
    SC["cksT"] = kb.dtmp("cksT", [256, S], BF16)
    SC["cvs"] = kb.dtmp("cvs", [S, 256], BF16)
    SC["ckwT"] = kb.dtmp("ckwT", [256, S], BF16)
    SC["cvw"] = kb.dtmp("cvw", [S, 256], BF16)
    SC["cg"] = kb.dtmp("cg", [S, 48], F32)
    SC["dgT"] = kb.dtmp("dgT", [1024, S], F32)
    SC["dxT"] = kb.dtmp("dxT", [1024, S], F32)
    SC["oT"] = kb.dtmp("oT", [4, 1024, S], BF16)
    SC["xs"] = kb.dtmp("xs", [S, 1024], F32)
    SC["Btok"] = kb.dtmp("Btok", [S, 256], BF16)
    SC["mT"] = kb.dtmp("mT", [D, S], BF16)
    SC["gT"] = kb.dtmp("gT", [4, D, S], BF16)
    SC["aT"] = kb.dtmp("aT", [D_FF, S], BF16)
    return SC


def rglru_phase(kb, C, l, Wd, SC):
    kb.push()
    voff = C["voff"]
    vec = C["vec"]
    psr = Rot(kb, "dps", 4, [128, 512], F32, psum=True)
    wbd = kb.sb("dwbd", [128, 2, 8, 128], BF16)
    kb.memset(wbd, 0.0, w=["wbd"])
    for wi, wn in enumerate(("rnn_w_r", "rnn_w_i")):
        src = Wd[wn][l].rearrange("(c two) i o -> two i c o", two=2)
        for hh in range(2):
            kb.dma("pool", wbd[hh * 64:(hh + 1) * 64, wi, :, hh * 64:(hh + 1) * 64], src[hh], r=[], w=["wbd"], key="wbd%d%d" % (wi, hh))
    cl = kb.sb("dcl", [128, 8], F32)
    lam = vec[:, voff[("rnn_lambda", l)]:voff[("rnn_lambda", l)] + 8]
    kb.act(cl, lam, AF.Exp, r=["consts"], w=["cl"], scale=-1.0)
    kb.act(cl, cl, AF.Ln, r=["cl"], w=["cl"], bias=1.0)
    kb.ts(cl, cl, -8.0, ALU.mult, r=["cl"], w=["cl"])
    xrot = Rot(kb, "dx", 2, [128, S + 3], F32)
    grot = Rot(kb, "dg", 2, [128, S], F32)
    for ap, res in xrot.t:
        kb.memset(ap[:, 0:3], 0.0, w=[res])
    wk = [dict(xc=kb.sb("dxc", [128, S], F32), xcb=kb.sb("dxcb", [128, S], BF16), rr=kb.sb("dr", [128, S], F32),
               ig=kb.sb("dig", [128, S], F32), aa=kb.sb("da", [128, S], F32), tmp=kb.sb("dtmp", [128, S], F32),
               hh=kb.sb("dh", [128, S], F32)) for _ in range(2)]
    orot = Rot(kb, "do", 2, [128, S], BF16)

    class _XP(Prefetch):
        def _issue(self):
            k = len(self.issued)
            if k < len(self.srcs):
                t, res = self.rot.next()
                self.kb.dma(self.q, t[:, 3:], self.srcs[k], w=[res], key=res)
                self.issued.append((t, res))
    x_pf = _XP(kb, xrot, [SC["dxT"][c * 128:(c + 1) * 128, :] for c in range(8)], ahead=1)
    g_pf = Prefetch(kb, grot, [SC["dgT"][c * 128:(c + 1) * 128, :] for c in range(8)], ahead=1)
    for c in range(8):
        W_ = wk[c % 2]
        xc, xcb, rr, ig, aa, tmp, hh_ = W_["xc"], W_["xcb"], W_["rr"], W_["ig"], W_["aa"], W_["tmp"], W_["hh"]
        sfx = "%d" % (c % 2)
        x, xres = x_pf.get()
        g, gres = g_pf.get()
        wcol = lambda k: vec[:, voff[("rnn_conv_w", l, k)] + c:voff[("rnn_conv_w", l, k)] + c + 1]
        bcol = vec[:, voff[("rnn_conv_b", l)] + c:voff[("rnn_conv_b", l)] + c + 1]
        kb.act(xc, x[:, 3:3 + S], AF.Identity, r=[xres, "consts"], w=["xc" + sfx], scale=wcol(3), bias=bcol)
        for k in range(3):
            kb.stt(xc, x[:, k:k + S], wcol(k), xc, ALU.mult, ALU.add, r=[xres, "xc" + sfx, "consts"], w=["xc" + sfx])
        kb.copy(xcb, xc, r=["xc" + sfx], w=["xcb" + sfx], eng="act")
        for wi, (dst, dres, bn) in enumerate(((rr, "rr" + sfx, "rnn_b_r"), (ig, "ig" + sfx, "rnn_b_i"))):
            bias = vec[:, voff[(bn, l)] + c:voff[(bn, l)] + c + 1]
            for tg in range(4):
                ps, pres = psr.next()
                kb.mm(ps, wbd[:, wi, c, :], xcb[:, tg * 512:(tg + 1) * 512], True, True, r=["wbd", "xcb" + sfx], w=[pres])
                kb.act(dst[:, tg * 512:(tg + 1) * 512], ps, AF.Sigmoid, r=[pres, "consts"], w=[dres], bias=bias)
        kb.act(aa, rr, AF.Exp, r=["rr" + sfx, "cl"], w=["aa" + sfx], scale=cl[:, c:c + 1])
        kb.act(tmp, aa, AF.Square, r=["aa" + sfx], w=["tmp" + sfx])
        kb.act(tmp, tmp, AF.Sqrt, r=["tmp" + sfx], w=["tmp" + sfx], scale=-1.0, bias=1.0)
        kb.tt(ig, ig, xc, ALU.mult, r=["ig" + sfx, "xc" + sfx], w=["ig" + sfx], eng="pool")
        kb.tt(tmp, tmp, ig, ALU.mult, r=["tmp" + sfx, "ig" + sfx], w=["tmp" + sfx])
        kb.P.add("dve", lambda e, hh_=hh_, aa=aa, tmp=tmp: e.tensor_tensor_scan(out=hh_, data0=aa, data1=tmp, initial=0.0, op0=ALU.mult, op1=ALU.add),
                 reads=["aa" + sfx, "tmp" + sfx], writes=["hh" + sfx])
        o, ores = orot.next()
        kb.tt(o, hh_, g, ALU.mult, r=["hh" + sfx, gres], w=[ores], eng="pool")
        kb.dma("sp", SC["oT"][3, c * 128:(c + 1) * 128, :], o, r=[ores], key=ores)
    kb.pop()


def rms_rstd(kb, out, ss, n, r, w):
    kb.ts(out, ss, 1.0 / n, ALU.mult, EPS, ALU.add, r=r, w=w)
    kb.act(out, out, AF.Sqrt, r=w, w=w)
    kb.recip(out, out, r=w, w=w)


def diff_phase(kb, C, l, SC):
    kb.push()
    boff, bc = C["boff"], C["bc"]
    lam_init = 0.8 - 0.6 * math.exp(-0.3 * l)
    lt = kb.sb("blt", [128, 64], F32)
    ls = kb.sb("bls", [128, 4], F32)
    for i, (a, b) in enumerate((("diff_lq1", "diff_lk1"), ("diff_lq2", "diff_lk2"))):
        oa, ob_ = boff[(a, l)], boff[(b, l)]
        kb.tt(lt, bc[:, oa:oa + 64], bc[:, ob_:ob_ + 64], ALU.mult, r=["consts"], w=["lt"])
        kb.P.add("dve", lambda e, i=i: e.tensor_reduce(out=ls[:, i:i + 1], in_=lt, axis=AX.X, op=ALU.add), reads=["lt"], writes=["ls"])
    kb.act(ls[:, 0:2], ls[:, 0:2], AF.Exp, r=["ls"], w=["ls"])
    kb.tt(ls[:, 2:3], ls[:, 1:2], ls[:, 0:1], ALU.subtract, r=["ls"], w=["ls"])
    kb.ts(ls[:, 2:3], ls[:, 2:3], -lam_init, ALU.add, r=["ls"], w=["ls"])
    neglam = ls[:, 2:3]
    gn = kb.sb("bgn", [128, 128], F32)
    og = boff[("diff_norm", l)]
    kb.ts(gn, bc[:, og:og + 128], 1.0 - lam_init, ALU.mult, r=["consts"], w=["gn"])

    qrot = Rot(kb, "bq", 2, [128, S], BF16)
    krot = Rot(kb, "bk", 2, [128, 2, S], BF16)
    for ap, res in krot.t:
        kb.memset(ap, 0.0, w=[res])
    vrot = Rot(kb, "bvv", 2, [128, 16, 129], BF16)
    for ap, res in vrot.t:
        kb.memset(ap[:, :, 128:129], 1.0, w=[res])
    pss = Rot(kb, "bps", 3, [128, 512], F32, psum=True)
    pso_all = kb.psum("bpo", [128, 4, 512], F32)
    pst = kb.psum("bpt", [128, 512], BF16)
    Pbuf = [kb.sb("bP%d" % i, [128, 16, 512], BF16) for i in range(2)]
    Osr = Rot(kb, "bO", 2, [128, 2, 4, 129], F32)
    sm = kb.sb("bsm", [128, 2, 4], F32)
    ssq = kb.sb("bssq", [128, 4], F32)
    rstd = kb.sb("brstd", [128, 4], F32)
    o1 = kb.sb("bo1", [128, 4, 128], F32)
    t2 = kb.sb("bt2", [128, 4, 128], F32)
    obr2 = Rot(kb, "bob", 2, [128, 4, 128], BF16)
    otr = Rot(kb, "bot", 2, [128, 512], BF16)
    ev = [0]
    cur = {}
    deferred = []

    def load_head(h):
        qT, qres = qrot.next()
        kT, kres = krot.next()
        v, vres = vrot.next()
        kb.dma("sp", qT, SC["bqT"][h * 128:(h + 1) * 128, :], w=[qres], key=qres)
        for m_ in range(2):
            kb.dma("sp", kT[m_ * 64:(m_ + 1) * 64, m_, :], SC["bkT"][h * 128 + m_ * 64:h * 128 + (m_ + 1) * 64, :], w=[kres], key=kres + "m%d" % m_)
        kb.dma("sp", v[:, :, 0:128], SC["bv"][:, h * 128:(h + 1) * 128].rearrange("(j p) e -> p j e", p=128), w=[vres], key=vres)
        return dict(qT=qT, qres=qres, kT=kT, kres=kres, v=v, vres=vres)

    def combine(h, qg, Osb, ores):
        rO = [ores + "m0", ores + "m1"]
        kb.recip(sm, Osb[:, :, :, 128], r=rO, w=["sm"])
        kb.ts(sm[:, 1, :], sm[:, 1, :], neglam, ALU.mult, r=["sm", "ls"], w=["sm"])
        kb.tt(o1, Osb[:, 0, :, 0:128], sm[:, 0, :].unsqueeze(2).broadcast_to([128, 4, 128]), ALU.mult, r=rO + ["sm"], w=["o1"])
        kb.tt(t2, Osb[:, 1, :, 0:128], sm[:, 1, :].unsqueeze(2).broadcast_to([128, 4, 128]), ALU.mult, r=rO + ["sm"], w=["t2"])
        kb.tt(o1, o1, t2, ALU.add, r=["o1", "t2"], w=["o1"], eng="pool")
        kb.tt(t2, o1, o1, ALU.mult, r=["o1"], w=["t2"], eng="pool")
        kb.P.add("dve", lambda e: e.tensor_reduce(out=ssq, in_=t2, axis=AX.X, op=ALU.add), reads=["t2"], writes=["ssq"])
        kb.act(rstd, ssq, AF.Ln, r=["ssq"], w=["rstd"], scale=1.0 / 128, bias=EPS)
        kb.act(rstd, rstd, AF.Exp, r=["rstd"], w=["rstd"], scale=-0.5)
        kb.tt(o1, o1, rstd.unsqueeze(2).broadcast_to([128, 4, 128]), ALU.mult, r=["o1", "rstd"], w=["o1"])
        ob, obres = obr2.next()
        kb.tt(ob, o1, gn.unsqueeze(1).broadcast_to([128, 4, 128]), ALU.mult, r=["o1", "gn"], w=[obres], eng="pool")

        def pe_part(ob=ob, obres=obres, h=h, qg=qg):
            for qt in range(4):
                kb.tr(pst[:, qt * 128:(qt + 1) * 128], ob[:, qt, :], C["ident_b"], r=[obres, "consts2"], w=["pst"])
            ot, otres = otr.next()
            kb.copy(ot, pst, r=["pst"], w=[otres], eng="dve")
            kb.dma("sp", SC["oT"][1, h * 128:(h + 1) * 128, qg * 512:(qg + 1) * 512], ot, r=[otres], key=otres)
        deferred.append(pe_part)

    groups = [(h, qg, m) for h in range(8) for qg in range(4) for m in range(2)]

    def score_steps(gi):
        h, qg, m = groups[gi]
        P = Pbuf[gi % 2]
        steps = []
        for j in range(4 * qg + 4):
            def step(j=j):
                if (h, qg, m, j) == (0, 0, 0, 0):
                    cur[0] = load_head(0)
                if (qg, m, j) == (1, 0, 0) and h + 1 < 8:
                    cur[h + 1] = load_head(h + 1)
                H = cur[h]
                r = j - 4 * qg
                c0 = 128 * r if r > 0 else 0
                ps, pres = pss.next()
                kb.mm(ps[:, c0:], H["kT"][:, m, j * 128:(j + 1) * 128], H["qT"][:, qg * 512 + c0:(qg + 1) * 512],
                      True, r < 0, r=[H["kres"], H["qres"]], w=[pres])
                if r >= 0:
                    kb.mm(ps[:, c0:c0 + 128], C["ident_b"], C["tri_b"], False, True, r=["consts2"], w=[pres])
                kb.act(P[:, j, c0:], ps[:, c0:], AF.Exp, r=[pres], w=["bP%d_%d" % (gi % 2, j)])
            steps.append(step)
        return steps

    def pv_steps(gi):
        h, qg, m = groups[gi]
        P = Pbuf[gi % 2]
        st = (gi % 2) * 2
        steps = []
        for qt in range(4):
            T = 4 * qg + qt
            bank, col = (st, qt * 129) if qt < 3 else (st + 1, 0)
            for j in range(T + 1):
                def step(qt=qt, j=j, T=T, bank=bank, col=col):
                    H = cur[h]
                    kb.mm(pso_all[:, bank, col:col + 129], P[:, j, qt * 128:(qt + 1) * 128], H["v"][:, j, :], j == 0, j == T,
                          r=["bP%d_%d" % (gi % 2, j), H["vres"]], w=["bpo%d" % bank])
                steps.append(step)

        def fin():
            while deferred:
                deferred.pop(0)()
            if m == 0:
                cur[(h, qg)] = Osr.next()
            Osb, ores = cur[(h, qg)]
            eng = "dve"
            kb.copy(Osb[:, m, 0:3, :], pso_all[:, st, 0:387].rearrange("p (q c) -> p q c", c=129), r=["bpo%d" % st], w=[ores + "m%d" % m], eng=eng)
            kb.copy(Osb[:, m, 3, :], pso_all[:, st + 1, 0:129], r=["bpo%d" % (st + 1)], w=[ores + "m%d" % m], eng=eng)
            if m == 1:
                combine(h, qg, Osb, ores)
        steps.append(fin)
        return steps

    for gi in range(len(groups) + 1):
        A = score_steps(gi) if gi < len(groups) else []
        B = pv_steps(gi - 1) if gi > 0 else []
        merge_steps(A, B)
    while deferred:
        deferred.pop(0)()
    kb.pop()


def merge_steps(A, B):
    na, nb = len(A), len(B)
    ia = ib = 0
    while ia < na or ib < nb:
        if ia < na and (ib >= nb or ia * max(nb, 1) <= ib * max(na, 1)):
            A[ia]()
            ia += 1
        else:
            B[ib]()
            ib += 1


def run_pipe(items, stage1, stage2, depth=1):
    q = []
    for it in items:
        q.append((it, stage1(it)))
        if len(q) > depth:
            stage2(*q.pop(0))
    while q:
        stage2(*q.pop(0))


def nsa_phase(kb, C, l, Wd, SC, posT_d):
    kb.push()
    NC_ = 127
    pss = Rot(kb, "cps", 3, [128, 512], F32, psum=True)
    pst = kb.psum("cpt", [128, 1024], BF16)
    kcT2 = kb.sb("ckcT2", [128, 2, 4, 128], BF16)
    vcx = kb.sb("cvcx", [128, 4, 97], BF16)
    kb.memset(kcT2, 0.0, w=["kcT2"])
    kb.memset(vcx, 0.0, w=["vcx"])
    kb.memset(vcx[:, :, 64:65], 1.0, w=["vcx"])
    for g in range(4):
        kb.dma("pool", vcx[:, g, 65:97], C["ovl_d"], w=["vcx"], key="ovl%d" % g)
    posT = kb.sb("cposT", [64, 32], F32)
    kb.dma("sp", posT, posT_d[l], w=["posT"], key="posT")
    srcT = kb.sb("csrc", [64, 4, S], BF16)
    w1 = kb.sb("cw1", [64, 32, 256], BF16)
    w2k = kb.sb("cw2k", [128, 2, 128], BF16)
    w2v = kb.sb("cw2v", [128, 2, 64], BF16)
    ktmp = kb.sb("cktmp", [64, 32, 128], BF16)
    hidT = kb.sb("chid", [128, 2, 128], BF16)
    for kv in range(2):
        src_d = SC["ckcT"] if kv == 0 else SC["cvcT"]
        kb.dma("sp", srcT, src_d.rearrange("(g d) t -> d g t", d=64), w=["srcT"], key="srcT")
        wn1, wn2 = (("nsa_ck_w1", "nsa_ck_w2") if kv == 0 else ("nsa_cv_w1", "nsa_cv_w2"))
        kb.dma("pool", w1, Wd[wn1][l].rearrange("(l d) h -> d l h", d=64), w=["w1"], key="w1")
        w2v_src = Wd[wn2][l].rearrange("(c p) d -> p c d", p=128)
        if kv == 0:
            kb.dma("pool", w2k[:, :, 0:64], w2v_src, w=["w2k"], key="w2ka")
            kb.dma("pool", w2k[:, :, 64:128], w2v_src, w=["w2k"], key="w2kb")
        else:
            kb.dma("pool", w2v, w2v_src, w=["w2v"], key="w2v")
        posb = kb.sb("cposb%d" % kv, [64, 32], BF16)
        kb.copy(posb, posT, r=["posT"], w=["posb%d" % kv])
        hpos = kb.sb("chpos%d" % kv, [128, 2], F32)
        for hc in range(2):
            ps, pres = pss.next()
            for ll in range(32):
                kb.mm(ps[:, 0:1], w1[:, ll, hc * 128:(hc + 1) * 128], posb[:, ll:ll + 1], ll == 0, ll == 31, r=["w1", "posb%d" % kv], w=[pres])
            kb.copy(hpos[:, hc:hc + 1], ps[:, 0:1], r=[pres], w=["hpos%d" % kv])
        for g in range(4):
            for hc in range(2):
                ps, pres = pss.next()
                for ll in range(32):
                    kb.mm(ps[:, 0:NC_], w1[:, ll, hc * 128:(hc + 1) * 128], srcT[:, g, ll:ll + 16 * (NC_ - 1) + 1:16], ll == 0, ll == 31,
                          r=["w1", "srcT"], w=[pres])
                kb.act(hidT[:, hc, 0:NC_], ps[:, 0:NC_], AF.Gelu_apprx_tanh, r=[pres, "hpos%d" % kv], w=["hidT"], bias=hpos[:, hc:hc + 1])
            ps, pres = pss.next()
            if kv == 0:
                for hc in range(2):
                    kb.mm(ps[:, 0:NC_], w2k[:, hc, :], hidT[:, hc, 0:NC_], hc == 0, hc == 1, r=["w2k", "hidT"], w=[pres])
                kb.copy(kcT2[0:64, 0, g, 0:NC_], ps[0:64, 0:NC_], r=[pres], w=["kcT2"])
                kb.copy(kcT2[64:128, 1, g, 0:NC_], ps[64:128, 0:NC_], r=[pres], w=["kcT2"])
            else:
                for hc in range(2):
                    kb.mm(ps[0:NC_, 0:64], hidT[:, hc, 0:NC_], w2v[:, hc, :], hc == 0, hc == 1, r=["w2v", "hidT"], w=[pres])
                kb.copy(vcx[0:NC_, g, 0:64], ps[0:NC_, 0:64], r=[pres], w=["vcx"])
    gates = kb.sb("cgate", [128, 16, 48], F32)
    kb.dma("sp", gates, SC["cg"].rearrange("(t p) c -> p t c", p=128), w=["gates"], key="gates")
    qrot = Rot(kb, "cq", 2, [128, 2, S], BF16)
    ksr = Rot(kb, "cks", 2, [128, 2, S], BF16)
    kwr = Rot(kb, "ckw", 2, [128, 2, S], BF16)
    for rot in (ksr, kwr):
        for ap, res in rot.t:
            kb.memset(ap, 0.0, w=[res])
    vsr = Rot(kb, "cvs", 2, [128, 16, 65], BF16)
    vwr = Rot(kb, "cvw", 2, [128, 16, 65], BF16)
    for rot in (vsr, vwr):
        for ap, res in rot.t:
            kb.memset(ap[:, :, 64:65], 1.0, w=[res])
    pso_all = kb.psum("cpo", [128, 4, 512], F32)
    pso_res = ["cpo%d" % i for i in range(4)]
    pT = Rot(kb, "cpT", 4, [128, 512], BF16)
    oaccr = Rot(kb, "coacc", 2, [128, 4, 4, 64], F32)
    imp = kb.sb("cimp", [128, 4, 32], F32)
    dn = kb.sb("cdn", [128, 4], F32)
    sc_ = kb.sb("csc", [128, 4], F32)
    tmpo = kb.sb("ctmpo", [128, 4, 64], F32)
    tmpi = kb.sb("ctmpi", [128, 4, 32], F32)
    top8 = kb.sb("ctop8", [128, 4, 8], F32)
    penq = kb.sb("cpenq", [128, 4, 32], BF16)
    penT = kb.sb("cpenT", [128, 512], BF16)
    kb.memset(penT, 0.0, w=["penT"])
    otr = Rot(kb, "cot", 2, [128, 2, 512], BF16)
    forced = kb.sb("cforced", [128, 16, 32], F32)
    future = kb.sb("cfuture", [128, 16, 32], F32)
    cmp_pen = kb.sb("ccmp_pen", [128, S], BF16)
    esel = kb.sb("cesel", [128, 16, 128], BF16)
    kb.memset(esel, 0.0, w=["consts2"])
    kb.dma("sp", forced, C["forced_d"], w=["consts"], key="cforced")
    kb.dma("sp", future, C["future_d"], w=["consts"], key="cfuture")
    kb.dma("pool", cmp_pen, C["cmp_pen_d"], w=["consts2"], key="ccmp_pen")
    kb.dma("pool", esel[0:32], C["esel_d"], w=["consts2"], key="cesel")
    cur = {}

    def load_group(g):
        qT, qres = qrot.next()
        kb.dma("sp", qT, SC["cqT"][g * 256:(g + 1) * 256, :].rearrange("(c p) t -> p c t", p=128), w=[qres], key=qres)
        ks, ksres = ksr.next()
        kw, kwres = kwr.next()
        for half in range(2):
            kb.dma("sp", ks[half * 64:(half + 1) * 64, half, :], SC["cksT"][g * 64:(g + 1) * 64, :], w=[ksres], key=ksres + "h%d" % half)
            kb.dma("sp", kw[half * 64:(half + 1) * 64, half, :], SC["ckwT"][g * 64:(g + 1) * 64, :], w=[kwres], key=kwres + "h%d" % half)
        vs, vsres = vsr.next()
        vw, vwres = vwr.next()
        kb.dma("sp", vs[:, :, 0:64], SC["cvs"][:, g * 64:(g + 1) * 64].rearrange("(j p) e -> p j e", p=128), w=[vsres], key=vsres)
        kb.dma("sp", vw[:, :, 0:64], SC["cvw"][:, g * 64:(g + 1) * 64].rearrange("(j p) e -> p j e", p=128), w=[vwres], key=vwres)
        return dict(qT=qT, qres=qres, ks=ks, ksres=ksres, kw=kw, kwres=kwres, vs=vs, vsres=vsres, vw=vw, vwres=vwres)

    def evac(O, den, rres, g, qg, jh, branch, first):
        oacc, oares = cur[("oacc", g, qg)]
        hh = g * 4 + jh
        kb.ts(dn, den, 1e-30, ALU.max, r=rres, w=["dn"])
        kb.recip(dn, dn, r=["dn"], w=["dn"])
        kb.tt(sc_, dn, gates[:, 4 * qg:4 * qg + 4, hh * 3 + branch], ALU.mult, r=["dn", "gates"], w=["sc"])
        dst = oacc[:, :, jh, :]
        dres = oares + "j%d" % jh
        sb_ = sc_.unsqueeze(2).broadcast_to([128, 4, 64])
        if first:
            kb.tt(dst, O, sb_, ALU.mult, r=rres + ["sc"], w=[dres])
        else:
            kb.tt(tmpo, O, sb_, ALU.mult, r=rres + ["sc"], w=["tmpo"])
            kb.tt(dst, dst, tmpo, ALU.add, r=[dres, "tmpo"], w=[dres], eng="pool")

    def topk_dve(g, qg):
        kb.tt(imp, imp, forced[:, 4 * qg:4 * qg + 4, :], ALU.max, r=["imp", "consts"], w=["imp"])
        kb.tt(imp, imp, future[:, 4 * qg:4 * qg + 4, :], ALU.min, r=["imp", "consts"], w=["imp"])
        for qt in range(4):
            kb.P.add("dve", lambda e, qt=qt: e.max(out=top8[:, qt, :], in_=imp[:, qt, :]), reads=["imp"], writes=["top8"])
        kb.tt(penq, imp, top8[:, :, 7:8].broadcast_to([128, 4, 32]), ALU.is_ge, r=["imp", "top8"], w=["penq"])
        kb.ts(penq, penq, -NEG, ALU.mult, NEG, ALU.add, r=["penq"], w=["penq"])

    def topk_pe(g, qg):
        for qt in range(4):
            kb.tr(pst[0:32, qt * 128:(qt + 1) * 128], penq[:, qt, :], C["ident_b"], r=["penq", "consts2"], w=["pst"])
        kb.copy(penT[0:32, :], pst[0:32, 0:512], r=["pst"], w=["penT"])

    ocbr = Rot(kb, "cocb", 2, [128, 4, 256], BF16)

    def writeout(g, qg):
        oacc, oares = cur[("oacc", g, qg)]
        ocb, ocres = ocbr.next()
        cur[("ocb", g, qg)] = (ocb, ocres)
        kb.copy(ocb, oacc.rearrange("p q j d -> p q (j d)"), r=[oares + "j%d" % jh for jh in range(4)], w=[ocres], eng="dve")

    def writeout_pe(g, qg):
        ocb, ocres = cur[("ocb", g, qg)]
        for qt in range(4):
            for c in range(2):
                kb.tr(pst[:, c * 512 + qt * 128:c * 512 + (qt + 1) * 128], ocb[:, qt, c * 128:(c + 1) * 128], C["ident_b"], r=[ocres, "consts2"], w=["pst"])
        ot, otres = otr.next()
        kb.copy(ot, pst.rearrange("p (c q) -> p c q", c=2), r=["pst"], w=[otres])
        for c in range(2):
            kb.dma("sp", SC["oT"][2, g * 256 + c * 128:g * 256 + (c + 1) * 128, qg * 512:(qg + 1) * 512], ot[:, c, :], r=[otres], key=otres + "c%d" % c)

    def stage1(it):
        kind, g, qg, jh = it[0], it[1], it[2], it[3]
        if kind == "sync":
            return None
        if kind == "cmp" and g == 0 and qg == 0 and jh == 0:
            cur[0] = load_group(0)
        if kind == "cmp" and qg == 1 and jh == 0 and g + 1 < 4:
            cur[g + 1] = load_group(g + 1)
        if kind == "cmp" and jh == 0:
            cur[("oacc", g, qg)] = oaccr.next()
        G = cur[g]
        c, hb = jh // 2, (jh % 2) * 64
        qT, qres = G["qT"], G["qres"]
        ps, pres = pss.next()
        p, ptres = pT.next()
        if kind == "cmp":
            kb.mm(ps, kcT2[:, jh % 2, g, :], qT[:, c, qg * 512:(qg + 1) * 512], True, False, r=["kcT2", qres], w=[pres])
            kb.mm(ps, C["ident_b"], cmp_pen[:, qg * 512:(qg + 1) * 512], False, True, r=["consts2"], w=[pres])
            kb.act(p, ps, AF.Exp, r=[pres], w=[ptres])
        elif kind == "slc":
            j = it[4]
            if jh == 0 and j == 0:
                topk_pe(g, qg)
            r = j - 4 * qg
            c0 = 128 * r if r > 0 else 0
            kb.mm(ps[:, c0:], G["ks"][:, jh % 2, j * 128:(j + 1) * 128], qT[:, c, qg * 512 + c0:(qg + 1) * 512], True, False,
                  r=[G["ksres"], qres], w=[pres])
            kb.mm(ps[:, c0:], esel[:, j, :], penT[:, c0:], False, r < 0, r=["consts2", "penT"], w=[pres])
            if r >= 0:
                kb.mm(ps[:, c0:c0 + 128], C["ident_b"], C["tri_b"], False, True, r=["consts2"], w=[pres])
            kb.act(p[:, c0:], ps[:, c0:], AF.Exp, r=[pres], w=[ptres])
        else:
            j, wk, r = it[4]
            if wk == "lo":
                ca, cb, pen = 0, 128 * (r + 1), C["wlo_b"]
            else:
                ca, cb, pen = 128 * r, 512, C["tri_b"]
            kb.mm(ps[:, ca:cb], G["kw"][:, jh % 2, j * 128:(j + 1) * 128], qT[:, c, qg * 512 + ca:qg * 512 + cb], True, False,
                  r=[G["kwres"], qres], w=[pres])
            kb.mm(ps[:, 128 * r:128 * r + 128], C["ident_b"], pen, False, True, r=["consts2"], w=[pres])
            kb.act(p[:, ca:cb], ps[:, ca:cb], AF.Exp, r=[pres], w=[ptres])
        return p, ptres

    def stage2(it, st):
        kind, g, qg, jh = it[0], it[1], it[2], it[3]
        if kind == "sync":
            it[4](g, qg)
            return
        p, ptres = st
        G = cur[g]
        if kind == "cmp":
            bank = pso_all[:, jh, :]
            for qt in range(4):
                kb.mm(bank[:, qt * 97:(qt + 1) * 97], p[:, qt * 128:(qt + 1) * 128], vcx[:, g, :], True, True, r=[ptres, "vcx"], w=[pso_res[jh]])
            a = bank[:, 0:388].rearrange("p (q c) -> p q c", c=97)
            evac(a[:, :, 0:64], a[:, :, 64], [pso_res[jh]], g, qg, jh, 0, True)
            dnb = dn.unsqueeze(2).broadcast_to([128, 4, 32])
            if jh == 0:
                kb.tt(imp, a[:, :, 65:97], dnb, ALU.mult, r=[pso_res[jh], "dn"], w=["imp"])
            else:
                kb.tt(tmpi, a[:, :, 65:97], dnb, ALU.mult, r=[pso_res[jh], "dn"], w=["tmpi"])
                kb.tt(imp, imp, tmpi, ALU.add, r=["imp", "tmpi"], w=["imp"], eng="pool")
            return
        if kind == "slc":
            j = it[4]
            v, vres = G["vs"], G["vsres"]
            for qt in range(4):
                T = 4 * qg + qt
                if j <= T:
                    kb.mm(pso_all[:, qt, :65], p[:, qt * 128:(qt + 1) * 128], v[:, j, :], j == 0, j == T, r=[ptres, vres], w=[pso_res[qt]])
            done = (j == 4 * qg + 3)
            branch = 1
        else:
            j, wk, r = it[4]
            v, vres = G["vw"], G["vwres"]
            uses = [qt for qt in range(4) if (qt <= r if wk == "lo" else qt >= r)]
            for qt in uses:
                first = (wk == "lo" and r == qt) or (wk == "hi" and r == 0 and qg == 0)
                last = (wk == "hi" and r == qt)
                kb.mm(pso_all[:, qt, :65], p[:, qt * 128:(qt + 1) * 128], v[:, j, :], first, last, r=[ptres, vres], w=[pso_res[qt]])
            done = (wk == "hi" and r == 3)
            branch = 2
        if done:
            evac(pso_all[:, :, 0:64], pso_all[:, :, 64], pso_res, g, qg, jh, branch, False)

    items = []
    prev_gq = None
    for g in range(4):
        for qg in range(4):
            for jh in range(4):
                items.append(("cmp", g, qg, jh))
            items.append(("sync", g, qg, 0, topk_dve))
            if prev_gq is not None:
                items.append(("sync", prev_gq[0], prev_gq[1], 0, writeout_pe))
            prev_gq = (g, qg)
            for jh in range(4):
                tiles = [(4 * qg - 4 + rp, "lo", rp) for rp in range(4) if qg > 0] + [(4 * qg + r, "hi", r) for r in range(4)]
                for t in tiles:
                    items.append(("win", g, qg, jh, t))
            for jh in range(4):
                for j in range(4 * qg + 4):
                    items.append(("slc", g, qg, jh, j))
            items.append(("sync", g, qg, 0, writeout))
    items.append(("sync", prev_gq[0], prev_gq[1], 0, writeout_pe))
    run_pipe(items, stage1, stage2, depth=2)
    kb.pop()


def ssd_phase(kb, C, l, SC):
    voff, vec, boff, bc = C["voff"], C["vec"], C["boff"], C["bc"]
    kb.push()
    BT = kb.sb("aBT", [128, 2, S], BF16)
    CT = kb.sb("aCT", [128, 2, S], BF16)
    kb.push()
    xrot = Rot(kb, "ax", 2, [128, S + 3], F32)
    for ap, res in xrot.t:
        kb.memset(ap[:, 0:3], 0.0, w=[res])
    xcr = Rot(kb, "axc", 2, [128, S], F32)
    ptr = Rot(kb, "aptr", 4, [128, 4, 128], F32, psum=True)
    st32 = Rot(kb, "ast32", 4, [128, 4, 128], F32)
    st16 = Rot(kb, "ast16", 4, [128, 4, 128], BF16)
    ev = [0]
    class _XP(Prefetch):
        def _issue(self):
            k = len(self.issued)
            if k < len(self.srcs):
                t, res = self.rot.next()
                self.kb.dma(self.q, t[:, 3:], self.srcs[k], w=[res], key=res)
                self.issued.append((t, res))
    x_pf = _XP(kb, xrot, [SC["xbcT"][c * 128:(c + 1) * 128, :] for c in range(12)], ahead=1)
    for c in range(12):
        x, xres = x_pf.get()
        xc, xcres = xcr.next()
        wcol = lambda k: vec[:, voff[("ssd_conv_w", l, k)] + c:voff[("ssd_conv_w", l, k)] + c + 1]
        bcol = vec[:, voff[("ssd_conv_b", l)] + c:voff[("ssd_conv_b", l)] + c + 1]
        kb.act(xc, x[:, 3:3 + S], AF.Identity, r=[xres, "consts"], w=[xcres], scale=wcol(3), bias=bcol)
        for k in range(3):
            kb.stt(xc, x[:, k:k + S], wcol(k), xc, ALU.mult, ALU.add, r=[xres, xcres, "consts"], w=[xcres])
        if c >= 8:
            dstT = BT if c < 10 else CT
            kb.act(dstT[:, c % 2, :], xc, AF.Silu, r=[xcres], w=["BCT%d" % c])
        kb.act(xc, xc, AF.Silu, r=[xcres], w=[xcres])
        if c < 10:
            for t4 in range(4):
                pt, ptres = ptr.next()
                for i in range(4):
                    tt_ = t4 * 4 + i
                    kb.tr(pt[:, i, :], xc[:, tt_ * 128:(tt_ + 1) * 128], C["ident_f"], r=[xcres, "consts"], w=[ptres])
                eng = "act"
                if c < 8:
                    st, sres = st32.next()
                    kb.copy(st, pt, r=[ptres], w=[sres], eng=eng)
                    kb.dma("sp", SC["xs"][t4 * 512:(t4 + 1) * 512, c * 128:(c + 1) * 128].rearrange("(i p) ch -> p i ch", p=128), st, r=[sres], key=sres)
                else:
                    st, sres = st16.next()
                    kb.copy(st, pt, r=[ptres], w=[sres], eng=eng)
                    kb.dma("sp", SC["Btok"][t4 * 512:(t4 + 1) * 512, (c - 8) * 128:(c - 7) * 128].rearrange("(i p) ch -> p i ch", p=128), st, r=[sres], key=sres)
    kb.pop()
    dts = kb.sb("adts", [128, 256], F32)
    adt = kb.sb("aadt", [128, 256], F32)
    acs = kb.sb("aacs", [128, 256], F32)
    eacs = kb.sb("aeacs", [128, 256], F32)
    dst_ = kb.sb("adst", [128, 256], F32)
    etot = kb.sb("aetot", [128, 256], F32)
    aexp = kb.sb("aaexp", [128, 256], F32)
    kb.push()
    ps1 = kb.psum("aps1", [128, 512], F32)
    ps2 = kb.psum("aps2", [128, 512], F32)
    kb.dma("sp", dts.rearrange("p (t h) -> p t h", h=16), SC["dt"].rearrange("(t p) h -> p t h", p=128), w=["dts"], key="dts")
    ob = boff[("ssd_dt_bias_rep", l)]
    kb.tt(dts, dts, bc[:, ob:ob + 256], ALU.add, r=["dts", "consts"], w=["dts"])
    kb.act(dts, dts, AF.Exp, r=["dts"], w=["dts"])
    kb.act(dts, dts, AF.Ln, r=["dts"], w=["dts"], bias=1.0)
    oa = boff[("ssd_a_log_rep", l)]
    kb.act(aexp, bc[:, oa:oa + 256], AF.Exp, r=["consts"], w=["aexp"])
    kb.stt(adt, dts, -1.0, aexp, ALU.mult, ALU.mult, r=["dts", "aexp"], w=["adt"])
    for t in range(16):
        kb.mm(ps1[:, t * 16:(t + 1) * 16], C["triu_f"], adt[:, t * 16:(t + 1) * 16], True, True, r=["adt", "consts"], w=["ps1"])
        kb.mm(ps2[:, t * 16:(t + 1) * 16], C["ones_f"], adt[:, t * 16:(t + 1) * 16], True, True, r=["adt", "consts"], w=["ps2"])
    kb.copy(acs, ps1[:, 0:256], r=["ps1"], w=["acs"])
    kb.act(eacs, acs, AF.Exp, r=["acs"], w=["eacs"])
    kb.tt(dst_, ps2[:, 0:256], acs, ALU.subtract, r=["ps2", "acs"], w=["dst"])
    kb.act(dst_, dst_, AF.Exp, r=["dst"], w=["dst"])
    kb.copy(etot, ps2[:, 0:256], r=["ps2"], w=["etot"])
    kb.act(etot, etot, AF.Exp, r=["etot"], w=["etot"])
    kb.pop()
    MTall = kb.sb("aMTall", [128, 16, 16, 128], BF16)
    kb.push()
    psg = Rot(kb, "apsg", 2, [128, 1024], F32, psum=True)
    pcb = Rot(kb, "apcb", 2, [128, 512], F32, psum=True)
    Xr = Rot(kb, "aX", 2, [128, 8, 128], F32)
    dr = Rot(kb, "ad", 2, [128, 8, 128], F32)
    cbr = Rot(kb, "acbm", 2, [128, 128], F32)
    triu_b8 = C["triu_f"].unsqueeze(1).broadcast_to([128, 8, 128])
    for t in range(16):
        tsl = slice(t * 128, (t + 1) * 128)
        for g in range(2):
            cols = slice(t * 16 + g * 8, t * 16 + g * 8 + 8)
            pc, pcres = pcb.next()
            kb.mm(pc[:, 0:128], BT[:, g, tsl], CT[:, g, tsl], True, True, r=["BCT%d" % (8 + g), "BCT%d" % (10 + g)], w=[pcres])
            cbm, cbres = cbr.next()
            kb.tt(cbm, pc[:, 0:128], C["triu_f"], ALU.mult, r=[pcres, "consts"], w=[cbres])
            xx, xxres = Xr.next()
            kb.tt(xx, triu_b8, adt[:, cols].unsqueeze(2).broadcast_to([128, 8, 128]), ALU.mult, r=["adt", "consts"], w=[xxres])
            pg, pgres = psg.next()
            for hf in range(2):
                kb.mm(pg[:, hf * 512:(hf + 1) * 512], C["ones_f"], xx[:, hf * 4:(hf + 1) * 4, :], True, True, r=[xxres, "consts"], w=[pgres])
            dd, ddres = dr.next()
            kb.tt(dd, pg.rearrange("p (h l) -> p h l", l=128), acs[:, cols].unsqueeze(2).broadcast_to([128, 8, 128]), ALU.subtract,
                  r=[pgres, "acs"], w=[ddres])
            kb.ts(dd, dd, 0.0, ALU.min, r=[ddres], w=[ddres])
            kb.act(dd, dd, AF.Exp, r=[ddres], w=[ddres])
            kb.tt(MTall[:, t, g * 8:(g + 1) * 8, :], dd, cbm.unsqueeze(1).broadcast_to([128, 8, 128]), ALU.mult,
                  r=[ddres, cbres], w=["MT%d" % t], eng="pool")
    kb.pop()
    yd = kb.psum("ayd", [128, 1024], F32)
    yo = kb.psum("ayo", [128, 1024], F32)
    psS = kb.psum("apsS", [128, 512], F32)
    pst = kb.psum("apst", [128, 1024], BF16)
    HT = kb.sb("aHT", [128, 1024], F32)
    HTb = kb.sb("aHTb", [128, 1024], BF16)
    xsr = Rot(kb, "axs", 3, [128, 1024], F32)
    zsr = Rot(kb, "azs", 3, [128, 1024], F32)
    btr = Rot(kb, "abt", 3, [128, 256], BF16)
    xdtr = Rot(kb, "axdt", 2, [128, 16, 64], BF16)
    xdtpr = Rot(kb, "axdtp", 2, [128, 16, 64], BF16)
    yoff = kb.sb("ayoff", [128, 1024], F32)
    yr = Rot(kb, "ay", 2, [128, 1024], F32)
    tmpr = Rot(kb, "atmp", 2, [128, 1024], F32)
    junk = kb.sb("ajunk", [128, 1024], F32)
    obr = Rot(kb, "aob", 2, [128, 1024], BF16)
    smr = Rot(kb, "asm", 2, [128, 4], F32)
    stg = Rot(kb, "astg", 2, [128, 8, 512], BF16)
    od = boff[("ssd_d_rep", l)]
    on = boff[("ssd_norm", l)]
    stg_cur = None
    ssd_deferred = []
    xs_pf = Prefetch(kb, xsr, [SC["xs"][t * 128:(t + 1) * 128, :] for t in range(16)], ahead=1)
    zs_pf = Prefetch(kb, zsr, [SC["zs"][t * 128:(t + 1) * 128, :] for t in range(16)], ahead=1)
    bt_pf = Prefetch(kb, btr, [SC["Btok"][t * 128:(t + 1) * 128, :] for t in range(16)], ahead=1)
    for t in range(16):
        tsl = slice(t * 128, (t + 1) * 128)
        xs, xsres = xs_pf.get()
        zs, zsres = zs_pf.get()
        bt, btres = bt_pf.get()
        xs3 = xs.rearrange("p (h d) -> p h d", d=64)
        bc16 = lambda tab: tab[:, t * 16:(t + 1) * 16].unsqueeze(2).broadcast_to([128, 16, 64])
        xdt, xdres = xdtr.next()
        xdtp, xpres = xdtpr.next()
        kb.tt(xdt, xs3, bc16(dts), ALU.mult, r=[xsres, "dts"], w=[xdres])
        kb.tt(xdtp, xdt, bc16(dst_), ALU.mult, r=[xdres, "dst"], w=[xpres], eng="pool")
        tmp, tmpres = tmpr.next()
        kb.tt(tmp, xs, bc[:, od:od + 1024], ALU.mult, r=[xsres, "consts"], w=[tmpres], eng="pool")
        for h in range(16):
            kb.mm(yd[:, h * 64:(h + 1) * 64], MTall[:, t, h, :], xdt[:, h, :], True, True, r=["MT%d" % t, xdres], w=["yd%d" % (h // 8)])
        y, yres = yr.next()
        if t > 0:
            for g in range(2):
                kb.mm(yo[:, g * 512:(g + 1) * 512], CT[:, g, tsl], HTb[:, g * 512:(g + 1) * 512], True, True,
                      r=["BCT%d" % (10 + g), "HTb%d" % g], w=["yo%d" % g])
            kb.tt(yoff.rearrange("p (h d) -> p h d", d=64), yo.rearrange("p (h d) -> p h d", d=64), bc16(eacs), ALU.mult,
                  r=["yo0", "yo1", "eacs"], w=["yoff"])
            kb.tt(y, yd, yoff, ALU.add, r=["yd0", "yd1", "yoff"], w=[yres])
        else:
            kb.copy(y, yd, r=["yd0", "yd1"], w=[yres])
        if t < 15:
            for g in range(2):
                kb.mm(psS, bt[:, g * 128:(g + 1) * 128], xdtp[:, g * 8:(g + 1) * 8, :], True, True, r=[btres, xpres], w=["psS"])
                hsl = slice(g * 512, (g + 1) * 512)
                if t == 0:
                    kb.copy(HT[:, hsl], psS, r=["psS"], w=["HT%d" % g])
                else:
                    et = etot[:, t * 16 + g * 8:t * 16 + g * 8 + 8].unsqueeze(2).broadcast_to([128, 8, 64])
                    kb.tt(HT[:, hsl].rearrange("p (h d) -> p h d", d=64), HT[:, hsl].rearrange("p (h d) -> p h d", d=64), et, ALU.mult,
                          r=["HT%d" % g, "etot"], w=["HT%d" % g])
                    kb.tt(HT[:, hsl], HT[:, hsl], psS, ALU.add, r=["HT%d" % g, "psS"], w=["HT%d" % g])
                kb.copy(HTb[:, hsl], HT[:, hsl], r=["HT%d" % g], w=["HTb%d" % g], eng="act")
        while ssd_deferred:
            ssd_deferred.pop(0)()
        sm, smres = smr.next()
        kb.tt(y, y, tmp, ALU.add, r=[yres, tmpres], w=[yres])
        kb.tt(y, y, zs, ALU.mult, r=[yres, zsres], w=[yres])
        kb.act(junk, y, AF.Square, r=[yres], w=["junk", smres + "ss"], accum_out=sm[:, 0:1])
        rms_rstd(kb, sm[:, 1:2], sm[:, 0:1], 1024, [smres + "ss"], [smres + "rstd"])
        ob16, obres = obr.next()
        kb.stt(ob16, y, sm[:, 1:2], bc[:, on:on + 1024], ALU.mult, ALU.mult, r=[yres, smres + "rstd", "consts"], w=[obres])
        if t % 4 == 0:
            stg_cur = stg.next()

        def pe_tail(t=t, ob16=ob16, obres=obres, stg_cur=stg_cur):
            for c in range(8):
                kb.tr(pst[:, c * 128:(c + 1) * 128], ob16[:, c * 128:(c + 1) * 128], C["ident_b"], r=[obres, "consts2"], w=["pst"])
            sg_, sgres_ = stg_cur
            kb.copy(sg_[:, :, (t % 4) * 128:(t % 4 + 1) * 128], pst.rearrange("p (c q) -> p c q", q=128), r=["pst"], w=[sgres_], eng="act")
            if t % 4 == 3:
                t4 = t // 4
                kb.dma("sp", SC["oT"][0].rearrange("(c p) t -> p c t", p=128)[:, :, t4 * 512:(t4 + 1) * 512], sg_, r=[sgres_], key=sgres_)
        ssd_deferred.append(pe_tail)
    while ssd_deferred:
        ssd_deferred.pop(0)()
    kb.pop()


def load_actT(kb, dst, dram, KC, res_fn, q="sp", T=S):
    v = dram.rearrange("(c p) t -> p c t", p=128)
    for c in range(KC):
        kb.dma(q, dst[:, c, :T], v[:, c, :], w=[res_fn(c, tg) for tg in range(4)], key="ld_%s_%d" % (res_fn(c, 0), c))


def gates_phase(kb, C, l, Wd, SC, hT, h_res):
    kb.push()
    wrot = Rot(kb, "gw", 2, [128, 16, 512], BF16)
    psrot = Rot(kb, "gps", 4, [128, 512], F32, psum=True)
    st16 = Rot(kb, "gst", 3, [128, 512], BF16)
    for n in range(4):
        def epi(f0, tg, ps, pres, n=n):
            st, sres = st16.next()
            kb.act(st, ps, AF.Sigmoid, r=[pres], w=[sres])
            kb.dma("sp", SC["gT"][n, f0:f0 + 128, tg * 512:(tg + 1) * 512], st, r=[sres], key=sres)
        gemm(kb, Wd["w_merge_gate"][l, n], D, D, hT, h_res, "feat", epi, wrot, psrot)
    kb.pop()


def merge_phase(kb, C, l, Wd, SC):
    kb.push()
    GC = 256
    wbrot = Rot(kb, "mwb", 3, [128, 8, GC], BF16)
    psb = Rot(kb, "mpb", 6, [128, 512], F32, psum=True)
    oT = kb.sb("moT", [128, 4, 8, S], BF16)
    for n in range(4):
        kb.dma("sp", oT[:, n], SC["oT"][n].rearrange("(c p) t -> p c t", p=128), w=["moT%d" % n], key="moT%d" % n)
    gtr = Rot(kb, "mgt", 4, [128, S], BF16)
    gpf = Prefetch(kb, gtr, [SC["gT"][n, g * GC + fc * 128:g * GC + fc * 128 + 128, :]
                             for g in range(D // GC) for n in range(4) for fc in range(GC // 128)], ahead=2)
    tmr = Rot(kb, "mtm", 6, [128, 512], F32)
    acc = kb.sb("macc", [128, GC // 128, S], F32)
    str_ = Rot(kb, "mst", 3, [128, 512], BF16)
    wpf = Prefetch(kb, wbrot, [Wd["w_branch"][l, n][:, g * GC:(g + 1) * GC].rearrange("(c p) n -> p c n", p=128)
                               for g in range(D // GC) for n in range(4)], q="pool", ahead=1)
    for g in range(D // GC):
        for n in range(4):
            wb, wbres = wpf.get()
            for fc in range(GC // 128):
                f0 = g * GC + fc * 128
                gtf, gtres = gpf.get()
                for tg in range(4):
                    gt = gtf[:, tg * 512:(tg + 1) * 512]
                    pb, pbres = psb.next()
                    for kc in range(8):
                        kb.mm(pb, wb[:, kc, fc * 128:(fc + 1) * 128], oT[:, n, kc, tg * 512:(tg + 1) * 512], kc == 0, kc == 7,
                              r=[wbres, "moT%d" % n], w=[pbres])
                    ares = "macc#%d_%d" % (fc, tg)
                    asl = acc[:, fc, tg * 512:(tg + 1) * 512]
                    if n == 0:
                        kb.tt(asl, gt, pb, ALU.mult, r=[gtres, pbres], w=[ares])
                    else:
                        tm, tmres = tmr.next()
                        kb.tt(tm, gt, pb, ALU.mult, r=[gtres, pbres], w=[tmres])
                        aeng = "dve"
                        if n < 3:
                            kb.tt(asl, asl, tm, ALU.add, r=[ares, tmres], w=[ares], eng=aeng)
                        else:
                            st, sres = str_.next()
                            kb.tt(st, asl, tm, ALU.add, r=[ares, tmres], w=[sres], eng=aeng)
                            kb.dma("sp", SC["mT"][f0:f0 + 128, tg * 512:(tg + 1) * 512], st, r=[sres], key=sres)
    kb.pop()


def resid_gemm_phase(kb, C, W, K, actT, act_res, x_src, x_dst, tag):
    kb.push()
    wrot = Rot(kb, tag + "w", 2, [128, K // 128, 512], BF16)
    psrot = Rot(kb, tag + "ps", 4, [128, 512], F32, psum=True)
    xr = Rot(kb, tag + "x", 5, [128, 512], F32)
    pf = Prefetch(kb, xr, [x_src[f0:f0 + 128, tg * 512:(tg + 1) * 512] for f0 in range(0, D, 128) for tg in range(4)])

    def epi(f0, tg, ps, pres):
        xt, xres = pf.get()
        kb.tt(xt, xt, ps, ALU.add, r=[xres, pres], w=[xres])
        kb.dma("sp", x_dst[f0:f0 + 128, tg * 512:(tg + 1) * 512], xt, r=[xres], key=xres)
    gemm(kb, W, K, D, actT, act_res, "feat", epi, wrot, psrot)
    kb.pop()


def ffn_up_phase(kb, C, l, Wd, SC, hT, h_res):
    voff, vec = C["voff"], C["vec"]
    kb.push()
    wrot = Rot(kb, "fw", FFN_WSLOTS, [128, 16, 512], BF16)
    psrot = Rot(kb, "fps", 6, [128, 512], F32, psum=True)
    urot = Rot(kb, "fu", 2, [128, S + 2], F32)
    for ap, res in urot.t:
        kb.memset(ap[:, 0:2], 0.0, w=[res])
    crot = Rot(kb, "fc", 2, [128, S], F32)
    vrot = Rot(kb, "fv", 2, [128, S], F32)
    arot = Rot(kb, "fa", 2, [128, S], BF16)
    Wup = Wd["ffn_w_up"][l]
    ev = [0]
    def issue_group(i0):
        wts = []
        for half in range(2):
            wt, wres = wrot.next()
            c0 = half * D_FF + i0 * 128
            wr = load_w(kb, wt, wres, Wup[:, c0:c0 + 512].rearrange("(c p) n -> p c n", p=128), 16, 512)
            wts.append((wt, wr))
        return wts
    nxt = issue_group(0)
    for i0 in range(0, 48, 4):
        wts = nxt
        if i0 + 4 < 48:
            nxt = issue_group(i0 + 4)
        for fc in range(4):
            i = i0 + fc
            conv = []
            for half in range(2):
                wt, wres = wts[half]
                u, ures = urot.next()
                for tg in range(4):
                    ps, pres = psrot.next()
                    for kc in range(16):
                        kb.mm(ps, wt[:, kc, fc * 128:(fc + 1) * 128], hT[:, kc, tg * 512:(tg + 1) * 512], kc == 0, kc == 15,
                              r=[wres(kc), h_res(kc, tg)], w=[pres])
                    kb.copy(u[:, 2 + tg * 512:2 + (tg + 1) * 512], ps, r=[pres], w=[ures], eng="act")
                ch = half * 48 + i
                wc = lambda k: vec[:, voff[("ffn_conv_w", l, k)] + ch:voff[("ffn_conv_w", l, k)] + ch + 1]
                bc = vec[:, voff[("ffn_conv_b", l)] + ch:voff[("ffn_conv_b", l)] + ch + 1]
                cv, cres = (crot if half == 0 else vrot).next()
                kb.ts(cv, u[:, 2:2 + S], wc(2), ALU.mult, bc, ALU.add, r=[ures, "consts"], w=[cres])
                kb.stt(cv, u[:, 1:1 + S], wc(1), cv, ALU.mult, ALU.add, r=[ures, cres, "consts"], w=[cres])
                kb.stt(cv, u[:, 0:S], wc(0), cv, ALU.mult, ALU.add, r=[ures, cres, "consts"], w=[cres])
                conv.append((cv, cres))
            (cg, cgres), (cvv, cvres) = conv
            kb.act(cg, cg, AF.Gelu_apprx_tanh, r=[cgres], w=[cgres])
            a, ares = arot.next()
            kb.tt(a, cg, cvv, ALU.mult, r=[cgres, cvres], w=[ares], eng="pool")
            kb.dma("sp", SC["aT"][i * 128:(i + 1) * 128, :], a, r=[ares], key=ares)
    kb.pop()


def ffn_down_phase(kb, C, l, Wd, SC):
    kb.push()
    GC = 256
    TH = 1024
    NQ = TH // 512
    wrot = Rot(kb, "gw", 2, [128, 48, GC], BF16)
    psrot = Rot(kb, "gps", 4, [128, 512], F32, psum=True)
    at = kb.sb("ga", [128, 48, TH], BF16)
    xr = Rot(kb, "gx", 5, [128, 512], F32)
    Wdn = Wd["ffn_w_down"][l]
    xT = SC["xT"]
    av = SC["aT"].rearrange("(c p) t -> p c t", p=128)
    pf = Prefetch(kb, xr, [xT[g * GC + fc * 128:g * GC + fc * 128 + 128, th * TH + tq * 512:th * TH + tq * 512 + 512]
                           for th in range(S // TH) for g in range(D // GC) for tq in range(NQ) for fc in range(GC // 128)])
    for th in range(S // TH):
        for tq in range(NQ):
            for c6 in range(6):
                t0 = th * TH + tq * 512
                kb.dma("sp", at[:, c6 * 8:(c6 + 1) * 8, tq * 512:(tq + 1) * 512], av[:, c6 * 8:(c6 + 1) * 8, t0:t0 + 512],
                       w=["ga_%d_%d" % (c6, tq)], key="ga_%d_%d" % (c6, tq))
        for g in range(D // GC):
            wt, wres = wrot.next()
            wr = load_w(kb, wt, wres, Wdn[:, g * GC:(g + 1) * GC].rearrange("(c p) n -> p c n", p=128), 48, GC, nsplit=6)
            for tq in range(NQ):
                for fc in range(GC // 128):
                    ps, pres = psrot.next()
                    for kc in range(48):
                        kb.mm(ps, wt[:, kc, fc * 128:(fc + 1) * 128], at[:, kc, tq * 512:(tq + 1) * 512], kc == 0, kc == 47,
                              r=[wr(kc), "ga_%d_%d" % (kc // 8, tq)], w=[pres])
                    f0 = g * GC + fc * 128
                    t0 = th * TH + tq * 512
                    xt, xres = pf.get()
                    kb.tt(xt, xt, ps, ALU.add, r=[xres, pres], w=[xres])
                    kb.dma("sp", xT[f0:f0 + 128, t0:t0 + 512], xt, r=[xres], key=xres)
    kb.pop()


def ple_phase(kb, C, l, Wd, SC, hT, h_res, pT_in):
    kb.push()
    wgrot = Rot(kb, "ewg", 2, [128, 16, 512], BF16)
    wprot = Rot(kb, "ewp", 2, [128, 2, 512], BF16)
    psg = Rot(kb, "epg", 3, [128, 512], F32, psum=True)
    psp = Rot(kb, "epp", 3, [128, 512], F32, psum=True)
    pT = kb.sb("epT", [128, 2, S], BF16)
    kb.dma("pool", pT, pT_in[l].rearrange("(c p) t -> p c t", p=128), w=["pT"], key="pT")
    sgr = Rot(kb, "esg", 2, [128, 512], F32)
    xr = Rot(kb, "ex", 5, [128, 512], F32)
    xT = SC["xT"]
    pf = Prefetch(kb, xr, [xT[g * 512 + fc * 128:g * 512 + fc * 128 + 128, tg * 512:(tg + 1) * 512]
                           for g in range(4) for fc in range(4) for tg in range(4)])
    def issue_g(g):
        wg, wgres = wgrot.next()
        wgr = load_w(kb, wg, wgres, Wd["ple_w_gate"][l][:, g * 512:(g + 1) * 512].rearrange("(c p) n -> p c n", p=128), 16, 512)
        wp, wpres = wprot.next()
        kb.dma("pool", wp, Wd["ple_w_proj"][l][:, g * 512:(g + 1) * 512].rearrange("(c p) n -> p c n", p=128), w=[wpres], key=wpres)
        return wg, wgr, wp, wpres
    nxt = issue_g(0)
    for g in range(4):
        wg, wgr, wp, wpres = nxt
        if g + 1 < 4:
            nxt = issue_g(g + 1)
        for fc in range(4):
            for tg in range(4):
                pg, pgres = psg.next()
                for kc in range(16):
                    kb.mm(pg, wg[:, kc, fc * 128:(fc + 1) * 128], hT[:, kc, tg * 512:(tg + 1) * 512], kc == 0, kc == 15,
                          r=[wgr(kc), h_res(kc, tg)], w=[pgres])
                pp, ppres = psp.next()
                for kc in range(2):
                    kb.mm(pp, wp[:, kc, fc * 128:(fc + 1) * 128], pT[:, kc, tg * 512:(tg + 1) * 512], kc == 0, kc == 1,
                          r=[wpres, "pT"], w=[ppres])
                sg, sgres = sgr.next()
                kb.act(sg, pg, AF.Sigmoid, r=[pgres], w=[sgres])
                kb.tt(sg, sg, pp, ALU.mult, r=[sgres, ppres], w=[sgres])
                f0 = g * 512 + fc * 128
                xt, xres = pf.get()
                kb.tt(xt, xt, sg, ALU.add, r=[xres, sgres], w=[xres], eng="dve")
                kb.dma("sp", xT[f0:f0 + 128, tg * 512:(tg + 1) * 512], xt, r=[xres], key=xres)
    kb.pop()


WEIGHT_NAMES = ("w_in", "nsa_ck_w1", "nsa_ck_w2", "nsa_cv_w1", "nsa_cv_w2", "rnn_w_r", "rnn_w_i",
                "w_merge_gate", "w_branch", "w_out", "ffn_w_up", "ffn_w_down", "ple_w_proj", "ple_w_gate")
WEIGHT_SHAPES = {
    "w_in": [DEPTH, D, D_IN], "nsa_ck_w1": [DEPTH, 2048, 256], "nsa_ck_w2": [DEPTH, 256, 64],
    "nsa_cv_w1": [DEPTH, 2048, 256], "nsa_cv_w2": [DEPTH, 256, 64], "rnn_w_r": [DEPTH, 16, 64, 64],
    "rnn_w_i": [DEPTH, 16, 64, 64], "nsa_pos_cmp": [DEPTH, 32, 64],
    "w_merge_gate": [DEPTH, 4, D, D], "w_branch": [DEPTH, 4, 1024, D], "w_out": [DEPTH, D, D],
    "ffn_w_up": [DEPTH, D, 2 * D_FF], "ffn_w_down": [DEPTH, D_FF, D], "ple_w_proj": [DEPTH, PLE, D],
    "ple_w_gate": [DEPTH, D, D],
}


def build(stage="full", debug=(), nlayers=DEPTH, inject=(), skip=()):
    kb = KB(debug)
    kb.inject = set(inject)
    kb.fin = []
    kb.push()
    xT_in = kb.din("xT_in", [D, S], F32)
    pT_in = kb.din("pT_in", [DEPTH, PLE, S], F32)
    Wd = {n: kb.din(n, WEIGHT_SHAPES[n], F32) for n in WEIGHT_NAMES}
    outT = kb.dout("outT", [D, S], F32)
    C = load_consts(kb)
    SC = alloc_scratch(kb)
    kb.P.barrier()
    voff = C["voff"]
    h_res = lambda c, tg: "hT#%d_%d" % (c, tg)

    def normed(x_src, gcol):
        kb.push()
        hT = kb.sb("hT", [128, 16, S], BF16)
        kb.push()
        nps = Rot(kb, "nps", 2, [128, 512], F32, psum=True)
        norm_phase(kb, C, x_src, gcol, hT, h_res, nps)
        kb.pop()
        return hT

    for l in range(nlayers):
        x_src = xT_in if l == 0 else SC["xT"]
        kb.push()
        bct = kb.sb("bc", [128, C["nb"]], F32)
        kb.dma("sp", bct, C["bc_d"][l], w=["consts"], key="bc")
        C["bc"] = bct
        hT = normed(x_src, voff[("norm_mix", l)])
        if "P" not in skip:
            proj_phase(kb, C, l, Wd["w_in"], hT, h_res, SC)
        if stage != "P" and "G" not in skip:
            gates_phase(kb, C, l, Wd, SC, hT, h_res)
        kb.pop()
        if stage == "P":
            break
        if "D" not in skip:
            rglru_phase(kb, C, l, Wd, SC)
        if stage == "D":
            break
        if "B" not in skip:
            diff_phase(kb, C, l, SC)
        if stage == "B":
            break
        if "C" not in skip:
            nsa_phase(kb, C, l, Wd, SC, C["posT_d"])
        if stage == "C":
            break
        if "A" not in skip:
            ssd_phase(kb, C, l, SC)
        if stage == "A":
            break
        kb.pop()
        merge_phase(kb, C, l, Wd, SC)
        kb.push()
        hT = kb.sb("hT", [128, 16, S], BF16)
        load_actT(kb, hT, SC["mT"], 16, h_res)
        resid_gemm_phase(kb, C, Wd["w_out"][l], D, hT, h_res, x_src, SC["xT"], "o")
        kb.pop()
        hT = normed(SC["xT"], voff[("norm_ffn", l)])
        ffn_up_phase(kb, C, l, Wd, SC, hT, h_res)
        kb.pop()
        ffn_down_phase(kb, C, l, Wd, SC)
        hT = normed(SC["xT"], voff[("norm_ple", l)])
        ple_phase(kb, C, l, Wd, SC, hT, h_res, pT_in)
        kb.pop()
    if stage in ("full", "rest"):
        kb.push()
        nps = Rot(kb, "nps", 2, [128, 512], F32, psum=True)
        norm_phase(kb, C, SC["xT"], voff[("norm_final", 0)], None, h_res, nps, out_f32=outT)
        kb.pop()
    else:
        kb.push()
        t = kb.sb("dummy", [128, 512], F32)
        kb.memset(t, 0.0, w=["dummy"])
        kb.fin.append(kb.dma("sp", outT[0:128, 0:512], t, r=["dummy"], key="dummy"))
        kb.pop()
    kb.pop()
    kb.P.emit(final_waits=kb.fin)
    return kb


def make_in_maps(inputs, kb):
    hc = host_consts()
    vec = pack_vec(inputs)
    bc = pack_bc(inputs)
    shared = {"c_ones_f": hc["ones_f"], "c_ident_f": hc["ident_f"], "c_pswap": hc["pswap"], "c_triu_f": hc["triu_f"],
              "c_vec": vec, "c_bc": bc, "c_rope": hc["rope"], "c_tri": hc["tri"], "c_wlo": hc["wlo"],
              "c_forced": hc["forced"], "c_future": hc["future"], "c_ovl": hc["ovl"], "c_cmp_pen": hc["cmp_pen"], "c_esel": hc["esel"],
              "c_posT": np.ascontiguousarray(np.asarray(inputs["nsa_pos_cmp"], np.float32).transpose(0, 2, 1))}
    for n in WEIGHT_NAMES:
        shared[n] = np.ascontiguousarray(np.asarray(inputs[n], np.float32))
    for k in list(shared):
        if k not in kb.ins:
            del shared[k]
    maps = []
    x = np.asarray(inputs["x"], np.float32)
    p = np.asarray(inputs["p"], np.float32)
    for b in range(x.shape[0]):
        m = dict(shared)
        m["xT_in"] = np.ascontiguousarray(x[b].T)
        m["pT_in"] = np.ascontiguousarray(p[:, b].transpose(0, 2, 1))
        maps.append(m)
    return maps


def kernel(**inputs):
    kb = build("full")
    maps = make_in_maps(inputs, kb)
    res = run_bass_kernel_spmd(kb.nc, maps, core_ids=list(range(8)))
    out = np.stack([np.ascontiguousarray(r["outT"].T) for r in res.results], 0)
    return out.astype(np.float32)
```
